# Optimizing a Trainium2 kernel written in Bass

```python
import jax, jax.numpy as jnp
from jax import lax
import numpy as np

D_MODEL = 1024
BATCH = 16
SEQ = 256
DEPTH = 2
DEC_BATCH = 8
DEC_SEQ = 4096
PAST_LEN = 512

GRID_W = 64
N_HEADS = 8
KV_HEADS = 2
HEAD_DIM = 64
Q_PER_KV = N_HEADS // KV_HEADS
ATT_W = N_HEADS * HEAD_DIM
KV_W = KV_HEADS * HEAD_DIM
WINDOW = 128
BLOCK = 128
CONV_W = D_MODEL // 2
CONV_K = 3
FOU_GROUPS = 4
FOU_GW = 128
FOU_W = FOU_GROUPS * FOU_GW
BRANCH_W = 512
N_BRANCH = 3
ROT_AXIS = HEAD_DIM // 2
ROPE_BASE = 10000.0
LN_EPS = 1e-6
NEG = -1e30
DEEPNORM_ALPHA = (2 * DEPTH) ** 0.25
DEEPNORM_BETA = (8 * DEPTH) ** -0.25
IN_SPLITS = (ATT_W, KV_W, KV_W, ATT_W, CONV_W, CONV_W, CONV_W, CONV_W, FOU_W, FOU_W, N_BRANCH * D_MODEL)
IN_W = 2 * ATT_W + 2 * KV_W + 4 * CONV_W + 2 * FOU_W + N_BRANCH * D_MODEL

kernel_name = "hybrid_diffusion_gated_branches_step"


def _layernorm(x):
    xf = x.astype(jnp.float32)
    mu = jnp.mean(xf, axis=-1, keepdims=True)
    var = jnp.mean(jnp.square(xf - mu), axis=-1, keepdims=True)
    return (xf - mu) * lax.rsqrt(var + LN_EPS)


def _modulation(cond, w_mod_l, b_mod_l):
    m = jax.nn.silu(cond) @ w_mod_l + b_mod_l
    shift, scale, gate = jnp.split(m, 3, axis=-1)
    return shift[:, None, :], scale[:, None, :], gate[:, None, :]


def _grid_angles(n_tokens):
    rows = n_tokens // GRID_W
    row = jnp.repeat(jnp.arange(rows, dtype=jnp.int32), GRID_W).astype(jnp.float32)
    col = jnp.tile(jnp.arange(GRID_W, dtype=jnp.int32), rows).astype(jnp.float32)
    n_freq = ROT_AXIS // 2
    inv_freq = ROPE_BASE ** (-jnp.arange(n_freq, dtype=jnp.float32) / n_freq)
    return row[:, None] * inv_freq, col[:, None] * inv_freq


def _rotate(seg, ang):
    x1, x2 = jnp.split(seg, 2, axis=-1)
    shape = (1, seg.shape[1]) + (1,) * (seg.ndim - 3) + (ang.shape[-1],)
    cos = jnp.cos(ang).reshape(shape).astype(seg.dtype)
    sin = jnp.sin(ang).reshape(shape).astype(seg.dtype)
    return jnp.concatenate([x1 * cos - x2 * sin, x2 * cos + x1 * sin], axis=-1)


def _rope2d(x, ang_r, ang_c):
    return jnp.concatenate([_rotate(x[..., :ROT_AXIS], ang_r), _rotate(x[..., ROT_AXIS:], ang_c)], axis=-1)


def _softmax_with_sink(logits, sink_l):
    sink = jnp.broadcast_to(sink_l.astype(jnp.float32).reshape(1, KV_HEADS, Q_PER_KV, 1, 1),
                            logits.shape[:-1] + (1,))
    p = jax.nn.softmax(jnp.concatenate([logits, sink], axis=-1), axis=-1)
    return p[..., :-1]


def _attend_context(q, k, v, sink_l):
    b, s = q.shape[:2]
    logits = jnp.einsum('bqkgd,bckd->bkgqc', q, k).astype(jnp.float32) * (HEAD_DIM ** -0.5)
    p = _softmax_with_sink(logits, sink_l).astype(v.dtype)
    out = jnp.einsum('bkgqc,bckd->bqkgd', p, v)
    return out.reshape(b, s, ATT_W)


def _attend_latent(q, k, v, k_ctx, v_ctx, sink_l):
    b, s = q.shape[:2]
    nb = s // BLOCK
    pad = ((0, 0), (BLOCK, BLOCK), (0, 0), (0, 0))
    kp = jnp.pad(k, pad)
    vp = jnp.pad(v, pad)
    scale = HEAD_DIM ** -0.5

    def block(i):
        start = i * BLOCK
        qb = lax.dynamic_slice_in_dim(q, start, BLOCK, axis=1)
        kb = lax.dynamic_slice_in_dim(kp, start, 3 * BLOCK, axis=1)
        vb = lax.dynamic_slice_in_dim(vp, start, 3 * BLOCK, axis=1)
        s_loc = jnp.einsum('bqkgd,bjkd->bkgqj', qb, kb).astype(jnp.float32) * scale
        tq = start + jnp.arange(BLOCK)
        tk = start - BLOCK + jnp.arange(3 * BLOCK)
        valid = (tk[None, :] >= 0) & (tk[None, :] < s) & (jnp.abs(tq[:, None] - tk[None, :]) <= WINDOW)
        s_loc = jnp.where(valid, s_loc, NEG)
        s_ctx = jnp.einsum('bqkgd,bckd->bkgqc', qb, k_ctx).astype(jnp.float32) * scale
        p = _softmax_with_sink(jnp.concatenate([s_loc, s_ctx], axis=-1), sink_l).astype(v.dtype)
        return (jnp.einsum('bkgqj,bjkd->bqkgd', p[..., :3 * BLOCK], vb)
                + jnp.einsum('bkgqc,bckd->bqkgd', p[..., 3 * BLOCK:], v_ctx))

    outs = lax.map(block, jnp.arange(nb))
    return jnp.moveaxis(outs, 0, 1).reshape(b, s, ATT_W)


def _short_conv(b_gate, c_gate, xin, conv_w_l):
    u = c_gate * xin
    s = u.shape[1]
    up = jnp.pad(u, ((0, 0), (1, 1), (0, 0)))
    y = up[:, 0:s] * conv_w_l[0] + up[:, 1:s + 1] * conv_w_l[1] + up[:, 2:s + 2] * conv_w_l[2]
    return b_gate * y


def _fourier(f):
    b, s, _ = f.shape
    g = f.reshape(b, s, FOU_GROUPS, FOU_GW).astype(jnp.float32)
    y = jnp.fft.fft2(g, axes=(1, 3), norm='ortho').real
    return y.astype(f.dtype).reshape(b, s, FOU_W)


def _layer(x, shift, scale, gate, w_in_l, conv_w_l, w_branch_l, w_o_l, ln_g_l, ln_b_l, attn_fn):
    b, s, _ = x.shape
    h = (_layernorm(x) * (1.0 + scale) + shift).astype(x.dtype)
    offsets = np.cumsum(IN_SPLITS)[:-1].tolist()
    q, k, v, z_a, cb, cc, cx, z_b, fx, z_c, g = jnp.split(h @ w_in_l, offsets, axis=-1)
    q = q.reshape(b, s, KV_HEADS, Q_PER_KV, HEAD_DIM)
    k = k.reshape(b, s, KV_HEADS, HEAD_DIM)
    v = v.reshape(b, s, KV_HEADS, HEAD_DIM)
    y_a = attn_fn(q, k, v) * jax.nn.silu(z_a)
    y_b = _short_conv(cb, cc, cx, conv_w_l) * jax.nn.silu(z_b)
    y_c = _fourier(fx) * jax.nn.silu(z_c)
    gates = jax.nn.sigmoid(g.reshape(b, s, N_BRANCH, D_MODEL))
    merged = (gates[:, :, 0] * (y_a @ w_branch_l[0])
              + gates[:, :, 1] * (y_b @ w_branch_l[1])
              + gates[:, :, 2] * (y_c @ w_branch_l[2]))
    out = merged @ w_o_l
    r = _layernorm(DEEPNORM_ALPHA * x + gate * out) * ln_g_l + ln_b_l
    return r.astype(x.dtype), k, v


def setup_inputs(seed: int = 0) -> dict:
    key = jax.random.key(seed)
    ks = jax.random.split(key, 16)
    nrm = jax.random.normal
    f32 = jnp.float32
    return {
        'x_prompt': nrm(ks[0], (BATCH, SEQ, D_MODEL), f32),
        'x_sample': nrm(ks[1], (DEC_BATCH, DEC_SEQ, D_MODEL), f32),
        'cache_k': nrm(ks[2], (DEC_BATCH, DEPTH, PAST_LEN, KV_HEADS, HEAD_DIM), f32),
        'cache_v': nrm(ks[3], (DEC_BATCH, DEPTH, PAST_LEN, KV_HEADS, HEAD_DIM), f32),
        'c': nrm(ks[4], (DEC_BATCH, D_MODEL), f32),
        'c_ctx': nrm(ks[5], (D_MODEL,), f32),
        'w_mod': nrm(ks[6], (DEPTH, D_MODEL, 3 * D_MODEL), f32) * D_MODEL ** -0.5,
        'b_mod': 0.01 * nrm(ks[7], (DEPTH, 3 * D_MODEL), f32),
        'w_in': nrm(ks[8], (DEPTH, D_MODEL, IN_W), f32) * D_MODEL ** -0.5,
        'sink': 0.5 * nrm(ks[9], (DEPTH, N_HEADS), f32),
        'conv_w': nrm(ks[10], (DEPTH, CONV_K, CONV_W), f32) * CONV_K ** -0.5,
        'w_branch': nrm(ks[11], (DEPTH, N_BRANCH, BRANCH_W, D_MODEL), f32) * (BRANCH_W ** -0.5 * DEEPNORM_BETA),
        'w_o': nrm(ks[12], (DEPTH, D_MODEL, D_MODEL), f32) * (D_MODEL ** -0.5 * DEEPNORM_BETA),
        'ln_g': 1.0 + 0.01 * nrm(ks[13], (DEPTH, D_MODEL), f32),
        'ln_b': 0.01 * nrm(ks[14], (DEPTH, D_MODEL), f32),
    }


def reference(x_prompt, x_sample, cache_k, cache_v, c, c_ctx, w_mod, b_mod, w_in, sink, conv_w,
              w_branch, w_o, ln_g, ln_b):
    y_prompt = x_prompt
    k_list, v_list = [], []
    for l in range(DEPTH):
        shift, scale, gate = _modulation(c_ctx[None, :], w_mod[l], b_mod[l])
        attn = lambda q, k, v, l=l: _attend_context(q, k, v, sink[l])
        y_prompt, k_l, v_l = _layer(y_prompt, shift, scale, gate, w_in[l], conv_w[l], w_branch[l],
                                    w_o[l], ln_g[l], ln_b[l], attn)
        k_list.append(k_l)
        v_list.append(v_l)
    new_k = jnp.stack(k_list, axis=1)
    new_v = jnp.stack(v_list, axis=1)

    y_sample = x_sample
    ang_r, ang_c = _grid_angles(x_sample.shape[1])
    for l in range(DEPTH):
        shift, scale, gate = _modulation(c, w_mod[l], b_mod[l])
        attn = lambda q, k, v, l=l: _attend_latent(_rope2d(q, ang_r, ang_c), _rope2d(k, ang_r, ang_c), v,
                                                   cache_k[:, l], cache_v[:, l], sink[l])
        y_sample, _, _ = _layer(y_sample, shift, scale, gate, w_in[l], conv_w[l], w_branch[l],
                                w_o[l], ln_g[l], ln_b[l], attn)

    return (y_prompt, y_sample, new_k, new_v)
```

```python
import os
import numpy as np
import ml_dtypes
from contextlib import ExitStack
import concourse.bass as bass
import concourse.mybir as mybir
from concourse.bass_utils import run_bass_kernel_spmd

F32 = mybir.dt.float32
BF16 = mybir.dt.bfloat16
AF = mybir.ActivationFunctionType
ALU = mybir.AluOpType
NPBF = ml_dtypes.bfloat16

D = 1024
KC = 8
T = 512
L = 2
PAST = 512
SP_LEN = 256
NPS = 2
ALPHA = float((2 * L) ** 0.25)
EPS = 1e-6
UW = 4608
NUNIT = 17
NS_W = 3
NS_T = 3


class Buf:
    def __init__(self, name, multi=False, excl=False):
        self.name = name
        self.multi = multi
        self.excl = excl
        self.w = {}
        self.r = {}


class _Rec:
    def __getattr__(self, name):
        def f(*a, **k):
            return (name, a, k)
        return f


REC = _Rec()


class Prog:
    CE = ("pe", "act", "dve", "pool")

    def __init__(self, nc, es):
        self.nc = nc
        self.es = es
        self.eh = {"pe": nc.tensor, "act": nc.scalar, "dve": nc.vector, "pool": nc.gpsimd, "sp": nc.sync}
        self.ops = {e: [] for e in self.eh}
        self.csem = {e: es.enter_context(nc.semaphore("c_" + e)) for e in self.CE}
        self.dsem = {}
        self.nbank = 0

    def _key(self, tok):
        return tok[1] if tok[0] == "c" else ("d", tok[1])

    def _collect(self, eng, R, W):
        waits = []
        for b in R:
            waits.extend(b.w.values())
            if b.excl:
                waits.extend(t for k, t in b.r.items() if k != eng)
        for b in W:
            if not b.multi:
                waits.extend(b.w.values())
            waits.extend(b.r.values())
        out = []
        for t in waits:
            if t[0] == "c":
                if t[1] == "pe" and eng == "pe":
                    continue
                self.ops[t[1]][t[2]]["signal"] = True
                out.append(t)
            else:
                out.append(("d", t[1], self.dsem[t[1]][1]))
        return out

    def _commit(self, tok, R, W):
        k = self._key(tok)
        for b in R:
            b.r[k] = tok
        for b in W:
            if b.multi:
                b.w[k] = tok
            else:
                b.w = {k: tok}
                b.r = {}

    def op(self, eng, fn, R=(), W=()):
        waits = self._collect(eng, R, W)
        idx = len(self.ops[eng])
        self.ops[eng].append({"fn": fn(REC), "waits": waits, "signal": False, "dma": None})
        self._commit(("c", eng, idx), R, W)

    def dma(self, eng, out, in_, R, W, owner):
        if owner.name not in self.dsem:
            self.dsem[owner.name] = [self.es.enter_context(self.nc.semaphore("d_" + owner.name)), 0]
        waits = self._collect(eng, R, W)
        self.dsem[owner.name][1] += 16
        self.ops[eng].append({"fn": ("dma_start", (), {"out": out, "in_": in_}), "waits": waits,
                              "signal": False, "dma": owner.name})
        self._commit(("d", owner.name), R, W)

    def barrier(self):
        last = {}
        for e in self.CE:
            if self.ops[e]:
                for i in range(len(self.ops[e]) - 1, -1, -1):
                    if self.ops[e][i]["fn"] is not None and self.ops[e][i]["dma"] is None:
                        self.ops[e][i]["signal"] = True
                        last[e] = ("c", e, i)
                        break
        dw = [("d", n, v[1]) for n, v in self.dsem.items() if v[1] > 0]
        for e in self.eh:
            waits = [t for k, t in last.items() if not (k == "pe" and e == "pe")] + dw
            self.ops[e].append({"fn": None, "waits": waits, "signal": False, "dma": None})

    def finish(self):
        dw = [("d", n, v[1]) for n, v in self.dsem.items() if v[1] > 0]
        self.ops["sp"].append({"fn": None, "waits": dw, "signal": False, "dma": None})
        ordn = {}
        for e in self.CE:
            c = 0
            for i, o in enumerate(self.ops[e]):
                if o["signal"]:
                    c += 1
                    ordn[(e, i)] = c
        for e, h in self.eh.items():
            seen = {}
            for o in self.ops[e]:
                for t in o["waits"]:
                    if t[0] == "c":
                        sem, val, key = self.csem[t[1]], ordn[(t[1], t[2])], t[1]
                    else:
                        sem, val, key = self.dsem[t[1]][0], t[2], ("d", t[1])
                    if seen.get(key, 0) >= val:
                        continue
                    seen[key] = val
                    h.wait_ge(sem, val)
                if o["fn"] is None:
                    continue
                ins = getattr(h, o["fn"][0])(*o["fn"][1], **o["fn"][2])
                if o["dma"] is not None:
                    ins.then_inc(self.dsem[o["dma"]][0], 16)
                elif o["signal"]:
                    ins.then_inc(self.csem[e], 1)


def _attn_perm():
    idx = np.zeros(512, np.int64)
    for c in range(4):
        for p in range(128):
            head = c if p < 64 else 4 + c
            idx[c * 128 + p] = head * 64 + (p % 64)
    return idx


def _chunks(w):
    n = w.shape[1] // 128
    return np.ascontiguousarray(w.reshape(8, 128, n, 128).transpose(2, 1, 0, 3))


def _prep_weights(w_mod, b_mod, w_in, sink, conv_w, w_branch, w_o, ln_g, ln_b):
    perm = _attn_perm()
    wp1 = np.zeros((L, 128, 8, 768), np.float32)
    wmain = np.zeros((L, NUNIT, 128, UW), np.float32)
    wmod = np.zeros((L, 6, 128, 4096), np.float32)
    bmod_fm = np.zeros((L, 128, 16), np.float32)
    bmod_g = np.zeros((L, 128, 1024), np.float32)
    convw = np.zeros((L, 128, 12), np.float32)
    sinkrep = np.zeros((L, 128, 2, 512), np.float32)
    lngb = np.zeros((L, 128, 2, 1024), np.float32)
    for l in range(L):
        wi = w_in[l]
        q, k, v, za = wi[:, 0:512], wi[:, 512:640], wi[:, 640:768], wi[:, 768:1280]
        cb, cc, cx, zb = wi[:, 1280:1792], wi[:, 1792:2304], wi[:, 2304:2816], wi[:, 2816:3328]
        fx, zc, g = wi[:, 3328:3840], wi[:, 3840:4352], wi[:, 4352:7424]
        p1 = np.concatenate([fx, k, v], axis=1)
        wp1[l] = p1.reshape(8, 128, 768).transpose(1, 0, 2)
        qc, zac, zcc = _chunks(q[:, perm]), _chunks(za[:, perm]), _chunks(zc)
        ccc, cxc, cbc, zbc = _chunks(cc), _chunks(cx), _chunks(cb), _chunks(zb)
        ulist = [qc, zac] + [np.stack([cxc[c], ccc[c], cbc[c], zbc[c]]) for c in range(4)] + [zcc]
        for u in range(7):
            wmain[l, u, :, :4096] = ulist[u].transpose(1, 0, 2, 3).reshape(128, 4096)
        gch = _chunks(g)
        for j in range(8):
            parts = [gch[b * 8 + j].reshape(128, 1024) for b in range(3)]
            for b in range(3):
                wb = w_branch[l, b]
                if b == 0:
                    wb = wb[perm, :]
                parts.append(wb[:, j * 128:(j + 1) * 128].reshape(4, 128, 128).transpose(1, 0, 2).reshape(128, 512))
            wmain[l, 7 + j] = np.concatenate(parts, axis=1)
        for hf in range(2):
            wmain[l, 15 + hf, :, :4096] = (w_o[l][:, hf * 512:(hf + 1) * 512]
                                          .reshape(8, 128, 512).transpose(1, 0, 2).reshape(128, 4096))
        for m in range(6):
            wmod[l, m] = (w_mod[l][:, m * 512:(m + 1) * 512].reshape(8, 128, 512).transpose(1, 0, 2).reshape(128, 4096))
        bmod_fm[l] = b_mod[l][:2048].reshape(16, 128).T
        bmod_g[l] = np.broadcast_to(b_mod[l][2048:3072][None, :], (128, 1024))
        convw[l] = conv_w[l].reshape(3, 4, 128).transpose(2, 0, 1).reshape(128, 12)
        for g_ in range(2):
            for c in range(4):
                sinkrep[l, :, g_, c * 128:(c + 1) * 128] = sink[l, 4 * g_ + c]
        lngb[l, :, 0, :] = ln_g[l][None, :]
        lngb[l, :, 1, :] = ln_b[l][None, :]
    return dict(wp1=wp1.reshape(L, 128, 6144), wmain=wmain, wmod=wmod, bmod_fm=bmod_fm, bmod_g=bmod_g,
                convw=convw, sinkrep=sinkrep.reshape(L, 128, 1024), lngb=lngb.reshape(L, 128, 2048))


def _consts(S):
    nt = S // T
    nu = S // 512
    ident = np.eye(128, dtype=np.float32).astype(NPBF)
    rm = np.zeros((128, 128), np.float32)
    for d in range(128):
        w = d % 32
        if w < 16:
            rm[d + 16, d] = -1.0
        else:
            rm[d - 16, d] = 1.0
    rmat = rm.astype(NPBF)
    ch = np.arange(128)
    ang = 2.0 * np.pi * np.outer(ch, ch) / 128.0
    cdft = np.stack([np.cos(ang), -np.sin(ang)], axis=1) / np.sqrt(128.0)
    cdft = cdft.astype(np.float32).astype(NPBF)
    kp = np.arange(128)[:, None]
    qf = np.arange(128)[None, :]
    mL = (qf <= kp).astype(np.float32)
    mR = (kp <= qf).astype(np.float32)
    masks = np.stack([np.tile(mL, (1, 4)), np.tile(mR, (1, 4))], axis=1).astype(NPBF)
    t = np.arange(S)
    row = (t // 64).astype(np.float32)
    col = (t % 64).astype(np.float32)
    inv = (10000.0 ** (-np.arange(16, dtype=np.float32) / 16.0)).astype(np.float32)
    rope = np.zeros((128, 2, S), np.float32)
    for p in range(128):
        d = p % 64
        seg, i = d // 32, (d % 32) % 16
        a = (row if seg == 0 else col) * inv[i]
        rope[p, 0] = np.cos(a.astype(np.float32))
        rope[p, 1] = np.sin(a.astype(np.float32))
    rope = np.ascontiguousarray(rope.reshape(128, 2, nt, 512).transpose(2, 0, 1, 3))
    s = np.arange(S, dtype=np.int64)
    prod = (np.outer(s, s) % S).astype(np.float64) * (2.0 * np.pi / S)
    tabs = []
    for fn in (np.cos, np.sin):
        m = (fn(prod) / np.sqrt(float(S))).astype(np.float32).astype(NPBF)
        m = m.reshape(nu, 4, 128, nt, 512).transpose(3, 0, 2, 1, 4)
        tabs.append(m)
    dfts = np.ascontiguousarray(np.stack(tabs, axis=2)[:, :nu // 2] if nu >= 2 else np.stack(tabs, axis=2)).reshape(nt, -1, 2, 128, 2048)
    sp = np.arange(SP_LEN, dtype=np.int64)
    prodp = (np.outer(sp, sp) % SP_LEN).astype(np.float64) * (2.0 * np.pi / SP_LEN)
    tp = []
    for fn in (np.cos, np.sin):
        m = (fn(prodp) / np.sqrt(float(SP_LEN))).astype(np.float32).astype(NPBF)
        tp.append(m.reshape(2, 128, SP_LEN).transpose(1, 0, 2))
    dftp = np.ascontiguousarray(np.stack(tp, axis=1)).reshape(128, 2 * 2 * SP_LEN)
    jm = np.zeros((128, 2, 128), np.float32)
    for p in range(1, 128):
        jm[128 - p, 0, p] = 1.0
    jm[0, 1, 0] = 1.0
    altrow = (((-1.0) ** np.arange(512)) / np.sqrt(float(S))).astype(np.float32).astype(NPBF).reshape(1, 512)
    return dict(jmat=jm.astype(NPBF).reshape(128, 256), altrow=altrow, ident=ident, rmat=rmat, cdft=cdft.reshape(128, 256), masks=masks.reshape(128, 1024),
                rope=rope.reshape(nt, 128, 1024), dfts=dfts, dftp=dftp)


def build_program(S, debug=False, stop=None):
    NT = S // T
    NBS = S // 128
    NU = S // 512
    TP = NPS * SP_LEN
    nc = bass.Bass("TRN2", target_bir_lowering=False)

    def din(name, shape, dt=F32):
        return nc.dram_tensor(name, list(shape), dt, kind="ExternalInput").ap()

    def dout(name, shape, dt=F32):
        return nc.dram_tensor(name, list(shape), dt, kind="ExternalOutput").ap()

    def dscr(name, shape, dt):
        return nc.dram_tensor(name, list(shape), dt, kind="Internal").ap()

    xs = din("xs", [S, D])
    xp = din("xp", [TP, D])
    ck = din("ck", [L, PAST, 128])
    cv = din("cv", [L, PAST, 128])
    cfm = din("cfm", [128, 16])
    wp1_d = din("wp1", [L, 128, 6144])
    wmain_d = din("wmain", [L, NUNIT, 128, UW])
    wmod_d = din("wmod", [L, 6, 128, 4096])
    bmodfm_d = din("bmod_fm", [L, 128, 16])
    bmodg_d = din("bmod_g", [L, 128, 1024])
    convw_d = din("convw", [L, 128, 12])
    sinkrep_d = din("sinkrep", [L, 128, 1024])
    lngb_d = din("lngb", [L, 128, 2048])
    ident_d = din("ident", [128, 128], BF16)
    rmat_d = din("rmat", [128, 128], BF16)
    cdft_d = din("cdft", [128, 256], BF16)
    masks_d = din("masks", [128, 1024], BF16)
    rope_d = din("rope", [NT, 128, 1024])
    NU2 = max(NU // 2, 1)
    dfts_d = din("dfts", [NT, NU2, 2, 128, 2048], BF16)
    dftp_d = din("dftp", [128, 1024], BF16)
    jmat_d = din("jmat", [128, 256], BF16)
    altrow_d = din("altrow", [1, 512], BF16)

    ys = dout("ys", [S, D])
    yp = dout("yp", [TP, D])
    nk = dout("nk", [NPS, L, SP_LEN, 128])
    nv = dout("nv", [NPS, L, SP_LEN, 128])

    s_wp1 = dscr("s_wp1", [L, 128, 6144], BF16)
    s_wmain = dscr("s_wmain", [L, NUNIT, 128, UW], BF16)
    s_wmod = dscr("s_wmod", [L, 6, 128, 4096], BF16)
    s_ht = dscr("s_ht", [L, 128, KC, S + TP], BF16)
    s_x1 = dscr("s_x1", [S + TP, D], F32)

    dbg = {}
    es = ExitStack()
    with es:
        P = Prog(nc, es)

        nmc = [0]

        def sb(stack, name, shape, dt):
            nmc[0] += 1
            return stack.enter_context(nc.sbuf_tensor(f"sb{nmc[0]}_{name}", list(shape), dt))

        banks = [es.enter_context(nc.psum_tensor(f"bank{i}", [128, 512], F32)) for i in range(8)]
        bbuf = [Buf(f"bank{i}", excl=True) for i in range(8)]
        ring = [0]
        pinned = set()

        def next_bank():
            while True:
                i = ring[0]
                ring[0] = (i + 1) % 8
                if i not in pinned:
                    return banks[i], bbuf[i]

        def pin(bk):
            pinned.add([i for i, b in enumerate(banks) if b is bk][0])

        def unpin(bk):
            pinned.discard([i for i, b in enumerate(banks) if b is bk][0])

        NBLK = NBS + TP // 128
        fxr = sb(es, "fxr", [128, NBLK, 512], BF16)
        B_fxr = Buf("fxr", multi=True)
        kt_s = sb(es, "kt_s", [128, S], BF16)
        kt_p = sb(es, "kt_p", [128, TP], BF16)
        kt_c = sb(es, "kt_c", [128, PAST], BF16)
        va_s = sb(es, "va_s", [128, NBS, 256], BF16)
        va_p = sb(es, "va_p", [128, TP // 128, 256], BF16)
        va_c = sb(es, "va_c", [128, PAST // 128, 256], BF16)
        B_kts, B_ktp, B_ktc = Buf("kt_s", True), Buf("kt_p", True), Buf("kt_c", True)
        B_vas, B_vap, B_vac = Buf("va_s", True), Buf("va_p", True), Buf("va_c", True)
        ident = sb(es, "ident", [128, 128], BF16)
        rmat = sb(es, "rmat", [128, 128], BF16)
        cdft = sb(es, "cdft", [128, 2, 128], BF16)
        masks = sb(es, "masks", [128, 2, 512], BF16)
        jmat = sb(es, "jmat", [128, 2, 128], BF16)
        altrow = sb(es, "altrow", [1, 512], BF16)
        sprow = sb(es, "sprow", [1, 512], BF16)
        B_sprow = Buf("sprow")
        B_const = Buf("const", True)
        shsc = sb(es, "shsc", [128, 16, 2], F32)
        gate = sb(es, "gate", [128, 2, 1024], F32)
        lngb = sb(es, "lngb", [128, 2, 1024], F32)
        esink = sb(es, "esink", [128, 2, 512], F32)
        convw = sb(es, "convw", [128, 3, 4], F32)
        nconvw = sb(es, "nconvw", [128, 3, 4], F32)
        B_lc = Buf("layerconst", True)
        xb = [sb(es, f"xb{i}", [128, D], F32) for i in range(2)]
        B_xb = [Buf(f"xb{i}") for i in range(2)]
        ropet = sb(es, "ropet", [128, 2, 512], F32)
        B_rope = Buf("ropet")
        stat = [sb(es, f"stat{i}", [128, 16], F32) for i in range(2)]
        B_stat = [Buf(f"stat{i}") for i in range(2)]

        B_out = Buf("outs", True)
        B_sx1 = Buf("s_x1", True)
        B_sht = [Buf(f"s_ht{l}", True) for l in range(L)]
        B_wbf = [Buf(f"wbf{l}", True) for l in range(L)]

        def dump(name, ap, shape, dt, B):
            if not debug:
                return
            o = nc.dram_tensor("dbg_" + name, list(shape), dt, kind="ExternalOutput").ap()
            dbg[name] = o
            Bl = B if isinstance(B, list) else [B]
            P.dma("sp", o, ap, Bl, [B_out], Bl[0])

        def act_copy(out, in_):
            return lambda e: e.activation(out=out, in_=in_, func=AF.Copy)

        P.dma("sp", ident[:], ident_d, [], [B_const], B_const)
        P.dma("sp", rmat[:], rmat_d, [], [B_const], B_const)
        P.dma("sp", cdft[:].rearrange("p a b -> p (a b)"), cdft_d, [], [B_const], B_const)
        P.dma("sp", masks[:].rearrange("p a b -> p (a b)"), masks_d, [], [B_const], B_const)
        P.dma("sp", jmat[:].rearrange("p a b -> p (a b)"), jmat_d, [], [B_const], B_const)
        P.dma("sp", altrow[:], altrow_d, [], [B_const], B_const)
        def cbuf_of(l_, kind_, i_=0):
            if l_ == 0:
                key = (kind_, i_)
                if key not in castb:
                    castb[key] = Buf(f"cast_{kind_}{i_}", True)
                return castb[key]
            return B_wbf[l_]

        castb = {}

        def emit_casts(l_):
            for m in range(6):
                P.dma("pool", s_wmod[l_, m], wmod_d[l_, m], [], [cbuf_of(l_, "wmod", m)], cbuf_of(l_, "wmod", m))
            P.dma("pool", s_wp1[l_], wp1_d[l_], [], [cbuf_of(l_, "wp1")], cbuf_of(l_, "wp1"))
            for u in range(NUNIT):
                P.dma("pool", s_wmain[l_, u], wmain_d[l_, u], [], [cbuf_of(l_, "wmain", u)], cbuf_of(l_, "wmain", u))

        emit_casts(0)
        P.op("dve", lambda e: e.memset(va_s[:].rearrange("p a b -> p (a b)"), 1.0), [], [B_vas])
        P.op("dve", lambda e: e.memset(va_p[:].rearrange("p a b -> p (a b)"), 1.0), [], [B_vap])
        P.op("dve", lambda e: e.memset(va_c[:].rearrange("p a b -> p (a b)"), 1.0), [], [B_vac])

        def vaug_dst(va, blk):
            return va[:, blk, :].rearrange("p (a b) -> p a b", b=64)[:, 0:4:3, :]

        if stop == 'prologue':
            P.barrier()
            P.finish()
            return nc, dbg
        tiles = [("s", i) for i in range(NT)] + [("p", 0)]

        def x_src(l, kind, ti, blk):
            r0 = (ti * T if kind == "s" else S) + blk * 128
            if l == 0:
                return (xs[r0:r0 + 128, :] if kind == "s" else xp[blk * 128:(blk + 1) * 128, :]), []
            return s_x1[r0:r0 + 128, :], [B_sx1]

        def tok0(kind, ti):
            return ti * T if kind == "s" else S

        for l in range(L):
            with ExitStack() as ps:
                wm = [sb(ps, f"wm{i}", [128, KC, 512], BF16) for i in range(2)]
                B_wm = [Buf(f"wm{i}") for i in range(2)]
                cf = sb(ps, "cf", [128, KC, 2], F32)
                sc = sb(ps, "sc", [128, KC, 2], BF16)
                scr = sb(ps, "scr", [128, KC, 2, 128], BF16)
                bmfm = sb(ps, "bmfm", [128, 16], F32)
                bmg = sb(ps, "bmg", [128, 1024], F32)
                ckt = sb(ps, "ckt", [128, 4, 128], BF16)
                cvt = sb(ps, "cvt", [128, 4, 128], BF16)
                B_pp = Buf("prep", True)
                B_ckt, B_cvt = Buf("ckt"), Buf("cvt")
                P.dma("sp", cf[:].rearrange("p a b -> p (a b)"), cfm, [], [B_pp], B_pp)
                P.dma("sp", bmfm[:], bmodfm_d[l], [], [B_pp], B_pp)
                P.dma("sp", bmg[:], bmodg_d[l], [], [B_pp], B_pp)
                P.dma("sp", convw[:].rearrange("p a b -> p (a b)"), convw_d[l], [], [B_lc], B_lc)
                P.dma("sp", esink[:].rearrange("p a b -> p (a b)"), sinkrep_d[l], [], [B_lc], B_lc)
                P.dma("sp", lngb[:].rearrange("p a b -> p (a b)"), lngb_d[l], [], [B_lc], B_lc)
                P.dma("pool", ckt[:], ck[l].rearrange("(b p) f -> p b f", p=128), [], [B_ckt], B_ckt)
                P.dma("pool", cvt[:], cv[l].rearrange("(b p) f -> p b f", p=128), [], [B_cvt], B_cvt)
                P.op("act", lambda e: e.activation(out=esink[:], in_=esink[:], func=AF.Exp), [B_lc], [B_lc])
                P.op("act", lambda e: e.activation(out=sc[:], in_=cf[:], func=AF.Silu), [B_pp], [B_pp])
                P.op("dve", lambda e: e.tensor_scalar(out=nconvw[:], in0=convw[:], scalar1=-1.0, scalar2=None,
                                                      op0=ALU.mult), [B_lc], [B_lc])
                for cnd in range(2):
                    P.op("dve", lambda e, cnd=cnd: e.tensor_copy(
                        out=scr[:, :, cnd, :], in_=sc[:, :, cnd:cnd + 1].to_broadcast([128, KC, 128])),
                        [B_pp], [B_pp])
                bk, bb = next_bank()
                bkv = bk[:].bitcast(BF16)
                for b4 in range(4):
                    P.op("pe", lambda e, b4=b4: e.transpose(bkv[:, b4 * 128:(b4 + 1) * 128], ckt[:, b4, :], ident[:]),
                         [B_ckt, B_const], [bb])
                P.op("dve", lambda e: e.tensor_copy(out=kt_c[:], in_=bkv[:, 0:512]), [bb], [B_ktc])
                for b4 in range(4):
                    P.op("dve", lambda e, b4=b4: e.tensor_copy(
                        out=vaug_dst(va_c, b4), in_=cvt[:, b4, :].rearrange("p (a b) -> p a b", b=64)),
                        [B_cvt], [B_vac])
                sbk, sbb = next_bank()
                for m in range(6):
                    sl = m % 2
                    P.dma("sp", wm[sl][:].rearrange("p a b -> p (a b)"), s_wmod[l, m], [cbuf_of(l, "wmod", m)], [B_wm[sl]], B_wm[sl])
                    if m < 4:
                        for c4 in range(4):
                            mm = m * 4 + c4
                            for kc in range(KC):
                                P.op("pe", lambda e, sl=sl, c4=c4, kc=kc, mm=mm: e.matmul(
                                    sbk[:, mm * 2:mm * 2 + 2], lhsT=wm[sl][:, kc, c4 * 128:(c4 + 1) * 128],
                                    rhs=sc[:, kc, :], start=(kc == 0), stop=(kc == KC - 1)),
                                    [B_wm[sl], B_pp], [sbb])
                    else:
                        hf = m - 4
                        for cnd in range(2):
                            gk, gb = next_bank()
                            for kc in range(KC):
                                P.op("pe", lambda e, sl=sl, kc=kc, cnd=cnd, gk=gk: e.matmul(
                                    gk[:], lhsT=scr[:, kc, cnd, :], rhs=wm[sl][:, kc, :],
                                    start=(kc == 0), stop=(kc == KC - 1)), [B_wm[sl], B_pp], [gb])
                            P.op("dve", lambda e, gk=gk, cnd=cnd, hf=hf: e.tensor_tensor(
                                out=gate[:, cnd, hf * 512:(hf + 1) * 512], in0=gk[:], in1=bmg[:, hf * 512:(hf + 1) * 512],
                                op=ALU.add), [gb, B_pp], [B_lc])
                for cnd in range(2):
                    P.op("dve", lambda e, cnd=cnd: e.tensor_tensor(
                        out=shsc[:, :, cnd], in0=sbk[:, 0:32].rearrange("p (m c) -> p m c", c=2)[:, :, cnd],
                        in1=bmfm[:], op=ALU.add), [sbb, B_pp], [B_lc])
                P.op("dve", lambda e: e.tensor_scalar(out=shsc[:, 8:16, :], in0=shsc[:, 8:16, :], scalar1=1.0,
                                                      scalar2=None, op0=ALU.add), [B_lc], [B_lc])
                P.barrier()
                if stop == 'prep':
                    P.finish()
                    return nc, dbg

            with ExitStack() as ps:
                wp1 = sb(ps, "wp1", [128, KC, 768], BF16)
                B_wp1 = Buf("wp1")
                xn = [sb(ps, f"xn{i}", [128, D], BF16) for i in range(4)]
                B_xn = [Buf(f"xn{i}") for i in range(4)]
                xb1 = [sb(ps, f"xb1_{i}", [128, D], F32) for i in range(4)]
                B_xb1 = [Buf(f"xb1_{i}") for i in range(4)]
                st1 = [sb(ps, f"st1_{i}", [128, 16], F32) for i in range(4)]
                B_st1 = [Buf(f"st1_{i}") for i in range(4)]
                htt = [sb(ps, f"htt{i}", [128, KC, T], BF16) for i in range(2)]
                B_htt = [Buf(f"htt{i}") for i in range(2)]
                ktr = sb(ps, "ktr", [128, T], BF16)
                B_ktr = Buf("ktr")
                kvo = [sb(ps, f"kvo{i}", [128, 256], F32) for i in range(2)]
                B_kvo = [Buf(f"kvo{i}") for i in range(2)]
                kvv = [sb(ps, f"kvv{i}", [128, 128], F32) for i in range(2)]
                B_kvv = [Buf(f"kvv{i}") for i in range(2)]
                rt1 = sb(ps, "rt1", [128, T], F32)
                rt2 = sb(ps, "rt2", [128, T], F32)
                B_rt1, B_rt2 = Buf("rt1"), Buf("rt2")
                P.dma("sp", wp1[:].rearrange("p a b -> p (a b)"), s_wp1[l], [cbuf_of(l, "wp1")], [B_wp1], B_wp1)
                def p1_ln(tix):
                        kind, ti = tiles[tix]
                        cnd = 0 if kind == "s" else 1
                        t0 = tok0(kind, ti)
                        hs = tix % 2
                        for blk in range(4):
                            src, srcb = x_src(l, kind, ti, blk)
                            P.dma("sp", xb1[blk][:], src, srcb, [B_xb1[blk]], B_xb1[blk])
                            st = st1[blk]
                            for h2 in range(2):
                                P.op("dve", lambda e, st=st, blk=blk, h2=h2: e.bn_stats(
                                    st[:, h2 * 6:h2 * 6 + 6], xb1[blk][:, h2 * 512:(h2 + 1) * 512]),
                                    [B_xb1[blk]], [B_st1[blk]])
                            P.op("dve", lambda e, st=st: e.bn_aggr(st[:, 12:14], st[:, 0:12]), [B_st1[blk]], [B_st1[blk]])
                        for blk in range(4):
                            st = st1[blk]
                            P.op("act", lambda e, st=st: e.activation(out=st[:, 14:15], in_=st[:, 13:14], func=AF.Sqrt, bias=EPS, scale=1.0),
                                 [B_st1[blk]], [B_st1[blk]])
                        for blk in range(4):
                            st = st1[blk]
                            P.op("dve", lambda e, st=st: e.reciprocal(out=st[:, 14:15], in_=st[:, 14:15]), [B_st1[blk]], [B_st1[blk]])
                            P.op("dve", lambda e, st=st: e.scalar_tensor_tensor(
                                out=st[:, 15:16], in0=st[:, 12:13], scalar=-1.0, in1=st[:, 14:15],
                                op0=ALU.mult, op1=ALU.mult), [B_st1[blk]], [B_st1[blk]])
                        for blk in range(4):
                            st = st1[blk]
                            P.op("act", lambda e, st=st, blk=blk: e.activation(
                                out=xn[blk][:], in_=xb1[blk][:], func=AF.Identity, bias=st[:, 15:16], scale=st[:, 14:15]),
                                [B_xb1[blk], B_st1[blk]], [B_xn[blk]])

                def p1_tr(tix):
                        kind, ti = tiles[tix]
                        cnd = 0 if kind == "s" else 1
                        t0 = tok0(kind, ti)
                        hs = tix % 2
                        tb_banks = [next_bank() for _ in range(4)]
                        tviews = [bk[:].bitcast(BF16) for bk, _ in tb_banks]
                        for blk in range(4):
                            for kc in range(KC):
                                tv = tviews[kc // 2]
                                c0 = (kc % 2) * 512 + blk * 128
                                P.op("pe", lambda e, tv=tv, c0=c0, blk=blk, kc=kc: e.transpose(
                                    tv[:, c0:c0 + 128], xn[blk][:, kc * 128:(kc + 1) * 128], ident[:]),
                                    [B_xn[blk], B_const], [tb_banks[kc // 2][1]])
                        for kc in range(KC):
                            tv = tviews[kc // 2]
                            c0 = (kc % 2) * 512
                            eng = "act" if (kc // 2) % 2 == 0 else "dve"
                            if eng == "act":
                                P.op("act", lambda e, tv=tv, c0=c0, kc=kc, hs=hs, cnd=cnd: e.activation(
                                    out=htt[hs][:, kc, :], in_=tv[:, c0:c0 + 512], func=AF.Identity,
                                    bias=shsc[:, kc, cnd:cnd + 1], scale=shsc[:, 8 + kc, cnd:cnd + 1]),
                                    [tb_banks[kc // 2][1], B_lc], [B_htt[hs]])
                            else:
                                P.op("dve", lambda e, tv=tv, c0=c0, kc=kc, hs=hs, cnd=cnd: e.tensor_scalar(
                                    out=htt[hs][:, kc, :], in0=tv[:, c0:c0 + 512], scalar1=shsc[:, 8 + kc, cnd:cnd + 1],
                                    scalar2=shsc[:, kc, cnd:cnd + 1], op0=ALU.mult, op1=ALU.add),
                                    [tb_banks[kc // 2][1], B_lc], [B_htt[hs]])

                def p1_mm(tix):
                        kind, ti = tiles[tix]
                        cnd = 0 if kind == "s" else 1
                        t0 = tok0(kind, ti)
                        hs = tix % 2
                        if kind == "s":
                            P.dma("sp", ropet[:].rearrange("p a b -> p (a b)"), rope_d[ti], [], [B_rope], B_rope)
                        P.dma("pool", s_ht[l, :, :, t0:t0 + T], htt[hs][:], [B_htt[hs]], [B_sht[l]], B_htt[hs])
                        if l == 0 and tix == 0:
                            dump("ht0", htt[hs][:], [128, KC, T], BF16, B_htt[hs])
                        for blk in range(4):
                            bglob = (t0 // 128) + blk
                            fk, fb = next_bank()
                            kk, kb = next_bank()
                            for kc in range(KC):
                                P.op("pe", lambda e, fk=fk, kc=kc, hs=hs, blk=blk: e.matmul(
                                    fk[:], lhsT=htt[hs][:, kc, blk * 128:(blk + 1) * 128], rhs=wp1[:, kc, 0:512],
                                    start=(kc == 0), stop=(kc == KC - 1)), [B_htt[hs], B_wp1], [fb])
                            for kc in range(KC):
                                P.op("pe", lambda e, kk=kk, kc=kc, hs=hs, blk=blk: e.matmul(
                                    kk[:, 0:256], lhsT=htt[hs][:, kc, blk * 128:(blk + 1) * 128], rhs=wp1[:, kc, 512:768],
                                    start=(kc == 0), stop=(kc == KC - 1)), [B_htt[hs], B_wp1], [kb])
                            P.op("act", act_copy(fxr[:, bglob, :], fk[:]), [fb], [B_fxr])
                            va, B_va, vblk = (va_s, B_vas, ti * 4 + blk) if kind == "s" else (va_p, B_vap, blk)
                            P.op("dve", lambda e, va=va, vblk=vblk, kk=kk: e.tensor_copy(
                                out=vaug_dst(va, vblk), in_=kk[:, 128:256].rearrange("p (a b) -> p a b", b=64)),
                                [kb], [B_va])
                            if kind == "p" and not os.environ.get("NO_NKV"):
                                ks = blk % 2
                                sq, r0 = blk // 2, (blk % 2) * 128
                                P.op("act", act_copy(kvo[ks][:, 0:128], kk[:, 0:128]), [kb], [B_kvo[ks]])
                                P.dma("sp", nk[sq, l, r0:r0 + 128, :], kvo[ks][:, 0:128], [B_kvo[ks]], [B_out], B_kvo[ks])
                                P.op("act", act_copy(kvv[ks][:], kk[:, 128:256]), [kb], [B_kvv[ks]])
                                P.dma("sp", nv[sq, l, r0:r0 + 128, :], kvv[ks][:], [B_kvv[ks]], [B_out], B_kvv[ks])
                        qk, qb_ = next_bank()
                        for kc in range(KC):
                            P.op("pe", lambda e, qk=qk, kc=kc, hs=hs: e.matmul(
                                qk[:], lhsT=wp1[:, kc, 512:640], rhs=htt[hs][:, kc, :],
                                start=(kc == 0), stop=(kc == KC - 1)), [B_htt[hs], B_wp1], [qb_])
                        if kind == "p":
                            P.op("act", act_copy(kt_p[:], qk[:]), [qb_], [B_ktp])
                        else:
                            P.op("act", act_copy(ktr[:], qk[:]), [qb_], [B_ktr])
                            rk, rb = next_bank()
                            P.op("pe", lambda e, rk=rk: e.matmul(rk[:], lhsT=rmat[:], rhs=ktr[:], start=True, stop=True),
                                 [B_ktr, B_const], [rb])
                            P.op("dve", lambda e, rk=rk: e.tensor_tensor(out=rt1[:], in0=rk[:], in1=ropet[:, 1, :], op=ALU.mult),
                                 [rb, B_rope], [B_rt1])
                            P.op("pool", lambda e: e.tensor_tensor(out=rt2[:], in0=ktr[:], in1=ropet[:, 0, :], op=ALU.mult),
                                 [B_ktr, B_rope], [B_rt2])
                            P.op("dve", lambda e, t0=t0: e.tensor_tensor(out=kt_s[:, t0:t0 + T], in0=rt1[:], in1=rt2[:], op=ALU.add),
                                 [B_rt1, B_rt2], [B_kts])

                p1_ln(0)
                p1_tr(0)
                for tix in range(len(tiles)):
                    kind, ti = tiles[tix]
                    if tix + 1 < len(tiles):
                        p1_ln(tix + 1)
                    p1_mm(tix)
                    if tix + 1 < len(tiles):
                        p1_tr(tix + 1)
                if l == 0:
                    dump("fxr", fxr[:], [128, NBLK, 512], BF16, B_fxr)
                    dump("kts", kt_s[:], [128, S], BF16, B_kts)
                    dump("vas", va_s[:], [128, NBS, 256], BF16, B_vas)
                    dump("ktc", kt_c[:], [128, PAST], BF16, B_ktc)
                    dump("shsc", shsc[:], [128, 16, 2], F32, B_lc)
                    dump("gate", gate[:], [128, 2, 1024], F32, B_lc)
                HB = NBS // 2
                P.op("act", act_copy(sprow[:], fxr[0:1, HB, :]), [B_fxr], [B_sprow])
                for b in range(HB - 1, -1, -1):
                    rk, rb = next_bank()
                    P.op("pe", lambda e, rk=rk, b=b: e.matmul(rk[:], lhsT=jmat[:, 0, :], rhs=fxr[:, NBS - 1 - b, :],
                                                             start=True, stop=(b == 0)), [B_fxr, B_const, B_sprow], [rb])
                    if b >= 1:
                        P.op("pe", lambda e, rk=rk, b=b: e.matmul(rk[:], lhsT=jmat[:, 1, :], rhs=fxr[:, NBS - b, :],
                                                                 start=False, stop=True), [B_fxr, B_const], [rb])
                    P.op("dve", lambda e, rk=rk, b=b: e.tensor_tensor(out=fxr[:, NBS - 1 - b, :], in0=fxr[:, b, :], in1=rk[:], op=ALU.subtract),
                         [rb, B_fxr], [B_fxr])
                    P.op("dve", lambda e, rk=rk, b=b: e.tensor_tensor(out=fxr[:, b, :], in0=fxr[:, b, :], in1=rk[:], op=ALU.add),
                         [rb, B_fxr], [B_fxr])
                P.barrier()
                if stop == 'pass1':
                    P.finish()
                    return nc, dbg

            with ExitStack() as ps:
                wsl = [sb(ps, f"wsl{i}", [128, UW], BF16) for i in range(NS_W)]
                B_wsl = [Buf(f"wsl{i}") for i in range(NS_W)]
                tsl = [sb(ps, f"tsl{i}", [128, 4, 512], BF16) for i in range(NS_T)]
                B_tsl = [Buf(f"tsl{i}") for i in range(NS_T)]
                ht = sb(ps, "ht", [128, KC, 516], BF16)
                B_ht = Buf("ht")
                qrqt = sb(ps, "qrqt", [128, 8, T], BF16)
                B_qr, B_qt = Buf("qr"), Buf("qt")
                za = sb(ps, "za", [128, 4, T], BF16)
                zc = sb(ps, "zc", [128, 4, T], BF16)
                B_za, B_zc = Buf("za"), Buf("zc")
                ya = sb(ps, "ya", [128, 4, T], BF16)
                yb = sb(ps, "yb", [128, 4, T], BF16)
                yc = sb(ps, "yc", [128, 4, T], BF16)
                B_ya, B_yb, B_yc = Buf("ya"), Buf("yb"), Buf("yc")
                cbuf = sb(ps, "cbuf", [128, 2064], F32)
                cy = [cbuf[:, 0:512], cbuf[:, 1032:1544]]
                cu = [cbuf[:, 512:1028], cbuf[:, 1544:2060]]
                B_cu = [Buf(f"cu{i}") for i in range(2)]
                B_cy = [Buf(f"cy{i}") for i in range(2)]
                NPT = 6
                pt = [sb(ps, f"pt{i}", [128, T], BF16) for i in range(NPT)]
                B_pt = [Buf(f"pt{i}") for i in range(NPT)]
                sg = [sb(ps, f"sg{i}", [128, T], BF16) for i in range(2)]
                B_sg = [Buf(f"sg{i}") for i in range(2)]
                tmp = [sb(ps, f"tmp{i}", [128, T], F32) for i in range(2)]
                B_tmp = [Buf(f"tmp{i}") for i in range(2)]
                acc = [sb(ps, f"acc{i}", [128, T], F32) for i in range(2)]
                B_acc = [Buf(f"acc{i}") for i in range(2)]
                accv = [a_[:].bitcast(BF16) for a_ in acc]
                cnt = {"tmp": 0, "pt": 0, "sg": 0, "acc": 0, "sb": 0}
                if os.environ.get('KDBG'):
                    print('pass2 sbuf remaining', nc.sbuf_bytes_remaining)

                def nxt(name, arr, barr):
                    i = cnt[name]
                    cnt[name] = (i + 1) % len(arr)
                    return arr[i], barr[i]

                rbuf = [cbuf[:, 0:1024], cbuf[:, 1032:2056],
                        ya[:].rearrange("p a b -> p (a b)").bitcast(F32), yc[:].rearrange("p a b -> p (a b)").bitcast(F32)]
                B_r = [[B_cy[0], B_cu[0]], [B_cy[1], B_cu[1]], [B_ya], [B_yc]]
                st2 = [sb(ps, f"st2_{i}", [128, 16], F32) for i in range(4)]
                B_st2 = [Buf(f"st2_{i}") for i in range(4)]
                pq = qrqt
                mg = qrqt
                B_mgl = [B_qr, B_qt]

                nun = len(tiles) * NUNIT
                wst = {"issued": 0, "cur": 0}

                def issue_w(n):
                    while wst["issued"] < min(n, nun):
                        i = wst["issued"]
                        u = i % NUNIT
                        sl = i % NS_W
                        P.dma("sp", wsl[sl][:], s_wmain[l, u], [cbuf_of(l, "wmain", u)], [B_wsl[sl]], B_wsl[sl])
                        wst["issued"] += 1

                def take_unit(hold=0):
                    i = wst["cur"]
                    wst["cur"] += 1
                    issue_w(i + NS_W - hold)
                    sl = i % NS_W
                    return wsl[sl], B_wsl[sl]

                tunits = [(ti_, u_, tb_) for ti_ in range(NT) for tb_ in range(2) for rep_ in range(2) for u_ in range(NU2)] + [("p", 0, 0)]
                tst = {"issued": 0, "cur": 0}

                def issue_t(n):
                    while tst["issued"] < min(n, len(tunits)):
                        i = tst["issued"]
                        ti_, u_, tb_ = tunits[i]
                        sl = i % NS_T
                        if ti_ == "p":
                            P.dma("sp", tsl[sl][:].rearrange("p a b -> p (a b)")[:, 0:1024], dftp_d, [], [B_tsl[sl]], B_tsl[sl])
                        else:
                            P.dma("sp", tsl[sl][:].rearrange("p a b -> p (a b)"), dfts_d[ti_, u_, tb_], [], [B_tsl[sl]], B_tsl[sl])
                        tst["issued"] += 1

                def take_tunit():
                    i = tst["cur"]
                    tst["cur"] += 1
                    issue_t(i + NS_T)
                    sl = i % NS_T
                    return tsl[sl], B_tsl[sl]

                htp = ht[:, :, :].rearrange("p k (s t) -> p k s t", s=2)

                def load_tile_inputs(tix_):
                    kind_, ti_ = tiles[tix_]
                    t0_ = tok0(kind_, ti_)
                    if kind_ == "s":
                        lo = 1 if ti_ == 0 else 0
                        hi = 513 if ti_ == NT - 1 else 514
                        if ti_ == 0:
                            P.op("pool", lambda e: e.memset(ht[:, :, 0:1], 0.0), [], [B_ht])
                        if ti_ == NT - 1:
                            P.op("pool", lambda e: e.memset(ht[:, :, 513:514], 0.0), [], [B_ht])
                        P.dma("sp", ht[:, :, lo:hi], s_ht[l, :, :, t0_ - 1 + lo:t0_ - 1 + hi], [B_sht[l]], [B_ht], B_ht)
                        P.dma("sp", ropet[:].rearrange("p a b -> p (a b)"), rope_d[ti_], [], [B_rope], B_rope)
                    else:
                        P.op("pool", lambda e: e.memset(htp[:, :, :, 0:258:257], 0.0), [], [B_ht])
                        for sq_ in range(2):
                            P.dma("sp", htp[:, :, sq_, 1:257], s_ht[l, :, :, t0_ + sq_ * 256:t0_ + (sq_ + 1) * 256],
                                  [B_sht[l]], [B_ht], B_ht)

                deferred = []
                issue_w(NS_W - 1)
                issue_t(NS_T - 1)
                load_tile_inputs(0)
                if l + 1 < L:
                    emit_casts(l + 1)
                gblk = 0
                for tix, (kind, ti) in enumerate(tiles):
                    cnd = 0 if kind == "s" else 1
                    t0 = tok0(kind, ti)
                    isS = kind == "s"

                    def v2(ap):
                        return ap if isS else ap.rearrange("p (s t) -> p s t", s=2)

                    if isS:
                        def hmain(kc):
                            return ht[:, kc, 1:513]

                        def hhalo(kc):
                            return ht[:, kc, 0:514:513]
                        nh = 2
                    else:
                        def hmain(kc):
                            return htp[:, kc, :, 1:257]

                        def hhalo(kc):
                            return htp[:, kc, :, 0:258:257]
                        nh = 4

                    def proj(wt, ch, B_w, into, B_into):
                        for kc in range(KC):
                            P.op("pe", lambda e, kc=kc: e.matmul(
                                into, lhsT=wt[:, ch * 1024 + kc * 128: ch * 1024 + (kc + 1) * 128], rhs=hmain(kc),
                                start=(kc == 0), stop=(kc == KC - 1)), [B_w, B_ht], [B_into])

                    wt, B_w = take_unit()
                    for c in range(4):
                        bk, bb = next_bank()
                        proj(wt, c, B_w, bk[:], bb)
                        if isS:
                            P.op("act", act_copy(qrqt[:, c, :], bk[:]), [bb], [B_qr])
                            rk, rb = next_bank()
                            P.op("pe", lambda e, rk=rk, c=c: e.matmul(rk[:], lhsT=rmat[:], rhs=qrqt[:, c, :], start=True, stop=True),
                                 [B_qr, B_const], [rb])
                            t1, B_t1 = nxt("tmp", tmp, B_tmp)
                            t2, B_t2 = nxt("tmp", tmp, B_tmp)
                            P.op("dve", lambda e, rk=rk, t1=t1: e.tensor_tensor(out=t1[:], in0=rk[:], in1=ropet[:, 1, :], op=ALU.mult),
                                 [rb, B_rope], [B_t1])
                            P.op("pool", lambda e, t2=t2, c=c: e.tensor_tensor(out=t2[:], in0=qrqt[:, c, :], in1=ropet[:, 0, :], op=ALU.mult),
                                 [B_qr, B_rope], [B_t2])
                            P.op("dve", lambda e, t1=t1, t2=t2, c=c: e.tensor_tensor(out=qrqt[:, 4 + c, :], in0=t1[:], in1=t2[:], op=ALU.add),
                                 [B_t1, B_t2], [B_qt])
                        else:
                            P.op("act", act_copy(qrqt[:, 4 + c, :], bk[:]), [bb], [B_qt])
                    wt, B_w = take_unit()
                    for c in range(4):
                        bk, bb = next_bank()
                        proj(wt, c, B_w, bk[:], bb)
                        P.op("act", lambda e, bk=bk, c=c: e.activation(out=za[:, c, :], in_=bk[:], func=AF.Silu), [bb], [B_za])
                    while deferred:
                        deferred.pop(0)()
                    for c in range(4):
                        wt, B_w = take_unit()
                        bx, bbx = next_bank()
                        bc, bbc = next_bank()
                        proj(wt, 0, B_w, bx[:], bbx)
                        proj(wt, 1, B_w, bc[:], bbc)
                        hk, hb = next_bank()
                        for wi_ in range(2):
                            for kc in range(KC):
                                o_ = hk[:, wi_ * 8: wi_ * 8 + nh]
                                P.op("pe", lambda e, kc=kc, o_=o_, wi_=wi_, wt=wt: e.matmul(
                                    o_, lhsT=wt[:, wi_ * 1024 + kc * 128: wi_ * 1024 + (kc + 1) * 128], rhs=hhalo(kc),
                                    start=(kc == 0), stop=(kc == KC - 1)), [B_w, B_ht], [hb])
                        bcb, bbcb = next_bank()
                        bzb, bbzb = next_bank()
                        proj(wt, 2, B_w, bcb[:], bbcb)
                        proj(wt, 3, B_w, bzb[:], bbzb)
                        cs = c % 2
                        cuu, B_cuu = cu[cs], B_cu[cs]
                        cyy, B_cyy = cy[cs], B_cy[cs]
                        if isS:
                            um = cuu[:, 1:513]
                            uh = cuu[:, 0:514:513]
                            taps = [cuu[:, 0:512], cuu[:, 1:513], cuu[:, 2:514]]
                        else:
                            cup = cuu.rearrange("p (s t) -> p s t", s=2)
                            um = cup[:, :, 1:257]
                            uh = cup[:, :, 0:258:257]
                            taps = [cup[:, :, 0:256], cup[:, :, 1:257], cup[:, :, 2:258]]
                        cyv = v2(cyy)
                        bxm, bcm = v2(bx[:]), v2(bc[:])
                        hx, hc = v2(hk[:, 0:nh]), v2(hk[:, 8:8 + nh])
                        P.op("act", act_copy(um, bxm), [bbx], [B_cuu])
                        P.op("act", act_copy(uh, hx), [hb], [B_cuu])
                        P.op("dve", lambda e, um=um, bcm=bcm: e.tensor_tensor(out=um, in0=bcm, in1=um, op=ALU.mult), [bbc, B_cuu], [B_cuu])
                        P.op("dve", lambda e, uh=uh, hc=hc: e.tensor_tensor(out=uh, in0=hc, in1=uh, op=ALU.mult), [hb, B_cuu], [B_cuu])
                        P.op("act", lambda e, cyv=cyv, taps=taps, c=c: e.activation(
                            out=cyv, in_=taps[1], func=AF.Copy, scale=convw[:, 1, c:c + 1]), [B_cuu, B_lc], [B_cyy])
                        P.op("dve", lambda e, cyv=cyv, taps=taps, c=c: e.scalar_tensor_tensor(
                            out=cyv, in0=taps[0], scalar=convw[:, 0, c:c + 1], in1=cyv, op0=ALU.mult, op1=ALU.add), [B_cuu, B_lc, B_cyy], [B_cyy])
                        P.op("dve", lambda e, cyv=cyv, taps=taps, c=c: e.scalar_tensor_tensor(
                            out=cyv, in0=taps[2], scalar=convw[:, 2, c:c + 1], in1=cyv, op0=ALU.mult, op1=ALU.add), [B_cuu, B_lc, B_cyy], [B_cyy])
                        zs, B_zs = nxt("sg", sg, B_sg)
                        P.op("act", lambda e, zs=zs, bzb=bzb: e.activation(out=zs[:], in_=bzb[:], func=AF.Silu), [bbzb], [B_zs])
                        gt, B_gt = nxt("tmp", tmp, B_tmp)
                        P.op("dve", lambda e, gt=gt, bcb=bcb, zs=zs: e.tensor_tensor(out=gt[:], in0=bcb[:], in1=zs[:], op=ALU.mult),
                             [bbcb, B_zs], [B_gt])
                        P.op("dve", lambda e, gt=gt, cyy=cyy, c=c: e.tensor_tensor(out=yb[:, c, :], in0=cyy, in1=gt[:], op=ALU.mult),
                             [B_cyy, B_gt], [B_yb])
                    wt, B_w = take_unit()
                    for c in range(4):
                        bk, bb = next_bank()
                        proj(wt, c, B_w, bk[:], bb)
                        P.op("act", lambda e, bk=bk, c=c: e.activation(out=zc[:, c, :], in_=bk[:], func=AF.Silu), [bb], [B_zc])

                    pend_norm = []
                    for qb in range(4):
                        if isS:
                            i = ti * 4 + qb
                            chunks = []
                            if i > 0:
                                chunks.append((kt_s, B_kts, (i - 1) * 128, va_s, B_vas, i - 1, 0))
                            chunks.append((kt_s, B_kts, i * 128, va_s, B_vas, i, None))
                            if i < NBS - 1:
                                chunks.append((kt_s, B_kts, (i + 1) * 128, va_s, B_vas, i + 1, 1))
                            for b4 in range(4):
                                chunks.append((kt_c, B_ktc, b4 * 128, va_c, B_vac, b4, None))
                        else:
                            sq = qb // 2
                            chunks = [(kt_p, B_ktp, (2 * sq + j_) * 128, va_p, B_vap, 2 * sq + j_, None) for j_ in range(2)]
                        qs = slice(qb * 128, (qb + 1) * 128)
                        obs = [(banks[(qb % 2) * 2 + g], bbuf[(qb % 2) * 2 + g]) for g in range(2)]

                        def emit_s(g, n):
                            kt, B_kt, k0, va, B_va, vblk, mk = chunks[n]
                            gs = slice(g * 64, (g + 1) * 64)
                            si = 4 + cnt["sb"]
                            cnt["sb"] = (cnt["sb"] + 1) % (2 if isS else 4)
                            sk, sbb = banks[si], bbuf[si]
                            P.op("pe", lambda e, sk=sk, kt=kt, k0=k0, gs=gs: e.matmul(
                                sk[:], lhsT=kt[gs, k0:k0 + 128], rhs=qrqt[gs, 4:8, qs],
                                start=True, stop=True), [B_kt, B_qt], [sbb])
                            pp, B_pp_ = nxt("pt", pt, B_pt)
                            P.op("act", lambda e, pp=pp, sk=sk: e.activation(out=pp[:], in_=sk[:], func=AF.Exp, scale=0.125),
                                 [sbb], [B_pp_])
                            if mk is not None:
                                P.op("pool", lambda e, pp=pp, mk=mk: e.tensor_tensor(out=pp[:], in0=pp[:], in1=masks[:, mk, :], op=ALU.mult),
                                     [B_pp_, B_const], [B_pp_])
                            return pp, B_pp_

                        def emit_pv(g, n, pp, B_pp_):
                            kt, B_kt, k0, va, B_va, vblk, mk = chunks[n]
                            ob, obb = obs[g]
                            P.op("pe", lambda e, va=va, vblk=vblk, pp=pp, n=n, ob=ob, g=g: e.matmul(
                                ob[:], lhsT=va[:, vblk, g * 128:(g + 1) * 128], rhs=pp[:],
                                start=(n == 0), stop=(n == len(chunks) - 1)), [B_va, B_pp_], [obb])

                        tasks = [(g, n) for n in range(len(chunks)) for g in range(2)]
                        LA = 1 if isS else 3
                        fl = []
                        if isS:
                            ftb = qb // 2
                            fgp = [(qb % 2) * 2, (qb % 2) * 2 + 1]
                            fstate = {}
                            for u in range(NU2):
                                for gi, gch in enumerate(fgp):
                                    for sc_ in range(4):
                                        def ffn(u=u, gi=gi, gch=gch, sc_=sc_, ftb=ftb, fstate=fstate):
                                            if u not in fstate:
                                                fstate[u] = take_tunit()
                                            tt, B_tt = fstate[u]
                                            blk = u * 4 + sc_
                                            if ftb == 1:
                                                blk = NBS - 1 - blk
                                            last = (u == NU2 - 1 and sc_ == 3)
                                            P.op("pe", lambda e: e.matmul(
                                                banks[6 + gi][:], lhsT=fxr[:, blk, gch * 128:(gch + 1) * 128], rhs=tt[:, sc_, :],
                                                start=(u == 0 and sc_ == 0), stop=(last and ftb == 1)), [B_fxr, B_tt], [bbuf[6 + gi]])
                                        fl.append(ffn)
                            if ftb == 0:
                                for gi, gch in enumerate(fgp):
                                    def ffs(gi=gi, gch=gch):
                                        P.op("pe", lambda e: e.matmul(
                                            banks[6 + gi][:], lhsT=sprow[0:1, gch * 128:(gch + 1) * 128], rhs=altrow[0:1, :],
                                            start=False, stop=True), [B_sprow, B_const], [bbuf[6 + gi]])
                                    fl.append(ffs)
                        pend = []
                        if isS:
                            prev = None
                            for n in range(len(chunks)):
                                cur = [(g, n) + emit_s(g, n) for g in range(2)]
                                for _ in range(4):
                                    if fl:
                                        fl.pop(0)()
                                if prev is not None:
                                    for t_ in prev:
                                        emit_pv(*t_)
                                    for _ in range(2):
                                        if fl:
                                            fl.pop(0)()
                                prev = cur
                            for t_ in prev:
                                emit_pv(*t_)
                        else:
                            for (g, n) in tasks:
                                pend.append((g, n) + emit_s(g, n))
                                if len(pend) > LA:
                                    emit_pv(*pend.pop(0))
                            while pend:
                                emit_pv(*pend.pop(0))
                        while fl:
                            fl.pop(0)()
                        if isS:
                            for gi, gch in enumerate(fgp):
                                if ftb == 0:
                                    dst, B_dst = pq[:, gch, :], B_qr
                                else:
                                    dst, B_dst = accv[gch // 2][:, (gch % 2) * 512:(gch % 2 + 1) * 512], B_acc[gch // 2]
                                if gi == 0:
                                    P.op("act", act_copy(dst, banks[6 + gi][:]), [bbuf[6 + gi]], [B_dst])
                                else:
                                    P.op("dve", lambda e, dst=dst, gi=gi: e.tensor_copy(out=dst, in_=banks[6 + gi][:]), [bbuf[6 + gi]], [B_dst])
                        def emit_norm(obs=obs, qs=qs):
                            t1, B_t1 = nxt("tmp", tmp, B_tmp)
                            for g in range(2):
                                ob, obb = obs[g]
                                orow = slice(g * 64, (g + 1) * 64)
                                drow = slice(64, 128) if g == 0 else slice(0, 64)
                                P.op("dve", lambda e, t1=t1, ob=ob, orow=orow, drow=drow, g=g: e.tensor_tensor(
                                    out=t1[orow, :], in0=ob[drow, :], in1=esink[orow, g, :], op=ALU.add), [obb, B_lc], [B_t1])
                            P.op("dve", lambda e, t1=t1: e.reciprocal(out=t1[:], in_=t1[:]), [B_t1], [B_t1])
                            for g in range(2):
                                ob, obb = obs[g]
                                orow = slice(g * 64, (g + 1) * 64)
                                P.op("dve", lambda e, t1=t1, ob=ob, orow=orow: e.tensor_tensor(
                                    out=t1[orow, :], in0=ob[orow, :], in1=t1[orow, :], op=ALU.mult), [obb, B_t1], [B_t1])
                            P.op("dve", lambda e, t1=t1, qs=qs: e.tensor_tensor(
                                out=ya[:, :, qs], in0=t1[:].rearrange("p (c q) -> p c q", c=4), in1=za[:, :, qs], op=ALU.mult),
                                [B_t1, B_za], [B_ya])

                        emit_norm()
                    while pend_norm:
                        pend_norm.pop(0)()

                    if not isS:
                        pbs = [next_bank() for _ in range(8)]
                        tt, B_tt = take_tunit()
                        dftp = tt[:].rearrange("p a b -> p (a b)")[:, 0:1024].rearrange("p (a b c) -> p a b c", a=2, b=2)
                        for tb in range(2):
                            for gch in range(4):
                                bk, bb = pbs[tb * 4 + gch]
                                for sq in range(2):
                                    for sc_ in range(2):
                                        blk = NBS + 2 * sq + sc_
                                        P.op("pe", lambda e, bk=bk, blk=blk, gch=gch, tb=tb, sc_=sc_, sq=sq: e.matmul(
                                            bk[:, sq * 256:(sq + 1) * 256], lhsT=fxr[:, blk, gch * 128:(gch + 1) * 128],
                                            rhs=dftp[:, tb, sc_, :], start=(sc_ == 0), stop=(sc_ == 1)), [B_fxr, B_tt], [bb])
                        for gch in range(4):
                            bkp, bbp = pbs[gch]
                            bkq, bbq = pbs[4 + gch]
                            P.op("act", act_copy(pq[:, gch, :], bkp[:]), [bbp], [B_qr])
                            P.op("dve", lambda e, gch=gch, bkq=bkq: e.tensor_copy(out=pq[:, 4 + gch, :], in_=bkq[:]), [bbq], [B_qt])
                    def emit_cdft():
                      for gch in range(4):
                        bk, bb = next_bank()
                        if isS:
                            qsrc, B_qsrc = accv[gch // 2][:, (gch % 2) * 512:(gch % 2 + 1) * 512], B_acc[gch // 2]
                        else:
                            qsrc, B_qsrc = pq[:, 4 + gch, :], B_qt
                        P.op("pe", lambda e, bk=bk, gch=gch: e.matmul(bk[:], lhsT=cdft[:, 0, :], rhs=pq[:, gch, :], start=True, stop=False),
                             [B_const, B_qr], [bb])
                        P.op("pe", lambda e, bk=bk, qsrc=qsrc: e.matmul(bk[:], lhsT=cdft[:, 1, :], rhs=qsrc, start=False, stop=True),
                             [B_const, B_qsrc], [bb])
                        P.op("dve", lambda e, bk=bk, gch=gch: e.tensor_tensor(out=yc[:, gch, :], in0=bk[:], in1=zc[:, gch, :], op=ALU.mult),
                             [bb, B_zc], [B_yc])

                    if l == 0 and tix == 0:
                        dump("ya", ya[:], [128, 4, T], BF16, B_ya)
                        dump("yb", yb[:], [128, 4, T], BF16, B_yb)
                        dump("yc", yc[:], [128, 4, T], BF16, B_yc)

                    Ys = ((ya, B_ya), (yb, B_yb), (yc, B_yc))
                    for j in range(8):
                        wt, B_w = take_unit()
                        ac, B_ac = nxt("acc", acc, B_acc)
                        for bi, b in enumerate((1, 2, 0)):
                            gk, gbb = next_bank()
                            for kc in range(KC):
                                P.op("pe", lambda e, gk=gk, kc=kc, b=b, wt=wt: e.matmul(
                                    gk[:], lhsT=wt[:, b * 1024 + kc * 128: b * 1024 + (kc + 1) * 128], rhs=hmain(kc),
                                    start=(kc == 0), stop=(kc == KC - 1)), [B_w, B_ht], [gbb])
                            s_, B_s = nxt("sg", sg, B_sg)
                            P.op("act", lambda e, s_=s_, gk=gk: e.activation(out=s_[:], in_=gk[:], func=AF.Sigmoid), [gbb], [B_s])
                            bk, bb = next_bank()
                            Y, B_Y = Ys[b]
                            for kc in range(4):
                                P.op("pe", lambda e, bk=bk, kc=kc, b=b, wt=wt, Y=Y: e.matmul(
                                    bk[:], lhsT=wt[:, 3072 + b * 512 + kc * 128: 3072 + b * 512 + (kc + 1) * 128], rhs=Y[:, kc, :],
                                    start=(kc == 0), stop=(kc == 3)), [B_w, B_Y], [bb])
                            if j == 0 and bi == 0:
                                emit_cdft()
                            if bi == 0:
                                P.op("dve", lambda e, ac=ac, bk=bk, s_=s_: e.tensor_tensor(out=ac[:], in0=bk[:], in1=s_[:], op=ALU.mult),
                                     [bb, B_s], [B_ac])
                            else:
                                tq, B_tq = nxt("tmp", tmp, B_tmp)
                                P.op("dve", lambda e, tq=tq, bk=bk, s_=s_: e.tensor_tensor(out=tq[:], in0=bk[:], in1=s_[:], op=ALU.mult),
                                     [bb, B_s], [B_tq])
                                if bi == 1:
                                    P.op("pool", lambda e, ac=ac, tq=tq: e.tensor_tensor(out=ac[:], in0=ac[:], in1=tq[:], op=ALU.add),
                                         [B_ac, B_tq], [B_ac])
                                else:
                                    P.op("pool", lambda e, ac=ac, tq=tq, j=j: e.tensor_tensor(out=mg[:, j, :], in0=ac[:], in1=tq[:], op=ALU.add),
                                         [B_ac, B_tq], [B_mgl[j // 4]])
                    if l == 0 and tix == 0:
                        dump("mg", mg[:], [128, 8, T], BF16, [B_qt, B_qr])

                    if tix + 1 < len(tiles):
                        load_tile_inputs(tix + 1)
                    wo = [take_unit(), take_unit(hold=1)]

                    def load_x(tb_):
                        xsl_ = tb_ % 2
                        src, srcb = x_src(l, kind, ti, tb_)
                        P.dma("sp", xb[xsl_][:], src, srcb, [B_xb[xsl_]], B_xb[xsl_])

                    load_x(0)
                    load_x(1)

                    def fin1(tb):
                        xsl = tb % 2
                        r, B_rr = rbuf[tb], B_r[tb]
                        st, B_st = st2[tb], B_st2[tb]
                        for hf in range(2):
                            bk, bb = next_bank()
                            wot, B_wo = wo[hf]
                            for kc in range(KC):
                                P.op("pe", lambda e, bk=bk, kc=kc, tb=tb, wot=wot: e.matmul(
                                    bk[:], lhsT=mg[:, kc, tb * 128:(tb + 1) * 128], rhs=wot[:, kc * 512:(kc + 1) * 512],
                                    start=(kc == 0), stop=(kc == KC - 1)), [B_qr, B_qt, B_wo], [bb])
                            P.op("dve", lambda e, r=r, bk=bk, hf=hf: e.tensor_tensor(
                                out=r[:, hf * 512:(hf + 1) * 512], in0=bk[:], in1=gate[:, cnd, hf * 512:(hf + 1) * 512], op=ALU.mult),
                                [bb, B_lc], B_rr)
                        P.op("dve", lambda e, r=r, xsl=xsl: e.scalar_tensor_tensor(
                            out=r, in0=xb[xsl][:], scalar=ALPHA, in1=r, op0=ALU.mult, op1=ALU.add), [B_xb[xsl]] + B_rr, B_rr)
                        if tb + 2 < 4:
                            load_x(tb + 2)
                        for h2 in range(2):
                            P.op("dve", lambda e, st=st, r=r, h2=h2: e.bn_stats(st[:, h2 * 6:h2 * 6 + 6], r[:, h2 * 512:(h2 + 1) * 512]),
                                 B_rr, [B_st])
                        P.op("dve", lambda e, st=st: e.bn_aggr(st[:, 12:14], st[:, 0:12]), [B_st], [B_st])

                    def fin2(tb, t0=t0, isS=isS, tix=tix):
                        r, B_rr = rbuf[tb], B_r[tb]
                        st, B_st = st2[tb], B_st2[tb]
                        P.op("act", lambda e, st=st: e.activation(out=st[:, 14:15], in_=st[:, 13:14], func=AF.Sqrt, bias=EPS, scale=1.0),
                             [B_st], [B_st])
                        P.op("dve", lambda e, st=st: e.reciprocal(out=st[:, 14:15], in_=st[:, 14:15]), [B_st], [B_st])
                        P.op("dve", lambda e, st=st: e.scalar_tensor_tensor(
                            out=st[:, 15:16], in0=st[:, 12:13], scalar=-1.0, in1=st[:, 14:15], op0=ALU.mult, op1=ALU.mult), [B_st], [B_st])
                        P.op("act", lambda e, st=st, r=r: e.activation(out=r, in_=r, func=AF.Identity, bias=st[:, 15:16], scale=st[:, 14:15]),
                             B_rr + [B_st], B_rr)
                        P.op("pool", lambda e, r=r: e.tensor_tensor(out=r, in0=r, in1=lngb[:, 0, :], op=ALU.mult), B_rr + [B_lc], B_rr)
                        P.op("pool", lambda e, r=r: e.tensor_tensor(out=r, in0=r, in1=lngb[:, 1, :], op=ALU.add), B_rr + [B_lc], B_rr)
                        r0 = t0 + tb * 128
                        if l == L - 1:
                            dst = ys[r0:r0 + 128, :] if isS else yp[tb * 128:(tb + 1) * 128, :]
                            P.dma("pool", dst, r, B_rr, [B_out], B_rr[0])
                        else:
                            P.dma("pool", s_x1[r0:r0 + 128, :], r, B_rr, [B_sx1], B_rr[0])
                            if debug and tix == 0:
                                if tb == 0:
                                    dbg["x1"] = nc.dram_tensor("dbg_x1", [512, 1024], F32, kind="ExternalOutput").ap()
                                P.dma("sp", dbg["x1"][tb * 128:(tb + 1) * 128, :], r, B_rr, [B_out], B_rr[0])

                    fin1(0)
                    fin1(1)
                    fin2(0)
                    fin1(2)
                    fin2(1)
                    fin1(3)
                    issue_w(wst["cur"] + NS_W)
                    if tix + 1 < len(tiles):
                        deferred.extend([lambda f=fin2: f(2), lambda f=fin2: f(3)])
                    else:
                        fin2(2)
                        fin2(3)
                P.barrier()

        P.finish()
    return nc, dbg


_CACHE = {}


def _get_program(S):
    if S not in _CACHE:
        _CACHE[S] = build_program(S)[0]
    return _CACHE[S]


def make_in_maps(x_prompt, x_sample, cache_k, cache_v, c, c_ctx, w_mod, b_mod, w_in, sink, conv_w,
                 w_branch, w_o, ln_g, ln_b, n_cores):
    f = lambda a: np.ascontiguousarray(np.asarray(a, dtype=np.float32))
    x_prompt, x_sample, cache_k, cache_v, c, c_ctx = map(f, (x_prompt, x_sample, cache_k, cache_v, c, c_ctx))
    S = x_sample.shape[1]
    shared = _prep_weights(*map(f, (w_mod, b_mod, w_in, sink, conv_w, w_branch, w_o, ln_g, ln_b)))
    shared.update(_consts(S))
    in_maps = []
    for i in range(n_cores):
        cf = np.zeros((128, 8, 2), np.float32)
        cf[:, :, 0] = c[i].reshape(8, 128).T
        cf[:, :, 1] = c_ctx.reshape(8, 128).T
        m = dict(shared)
        m["xs"] = x_sample[i]
        m["xp"] = np.ascontiguousarray(x_prompt[NPS * i:NPS * (i + 1)].reshape(NPS * SP_LEN, D))
        m["ck"] = np.ascontiguousarray(cache_k[i].reshape(L, PAST, 128))
        m["cv"] = np.ascontiguousarray(cache_v[i].reshape(L, PAST, 128))
        m["cfm"] = cf.reshape(128, 16)
        in_maps.append(m)
    return in_maps, S


def kernel(x_prompt, x_sample, cache_k, cache_v, c, c_ctx, w_mod, b_mod, w_in, sink, conv_w,
           w_branch, w_o, ln_g, ln_b):
    n = 8
    in_maps, S = make_in_maps(x_prompt, x_sample, cache_k, cache_v, c, c_ctx, w_mod, b_mod, w_in, sink, conv_w,
                              w_branch, w_o, ln_g, ln_b, n)
    nc = _get_program(S)
    res = run_bass_kernel_spmd(nc, in_maps, core_ids=list(range(n)))
    rs = res.results
    y_sample = np.stack([np.asarray(r["ys"], dtype=np.float32) for r in rs], axis=0)
    y_prompt = np.concatenate([np.asarray(r["yp"], dtype=np.float32).reshape(NPS, SP_LEN, D) for r in rs], axis=0)
    new_k = np.concatenate([np.asarray(r["nk"], dtype=np.float32).reshape(NPS, L, SP_LEN, 2, 64) for r in rs], axis=0)
    new_v = np.concatenate([np.asarray(r["nv"], dtype=np.float32).reshape(NPS, L, SP_LEN, 2, 64) for r in rs], axis=0)
    return (y_prompt, y_sample, new_k, new_v)
```

```python
import os
import numpy as np
import ml_dtypes
from contextlib import ExitStack
import concourse.bass as bass
import concourse.mybir as mybir
from concourse.bass_utils import run_bass_kernel_spmd

F32 = mybir.dt.float32
BF16 = mybir.dt.bfloat16
AF = mybir.ActivationFunctionType
ALU = mybir.AluOpType
NPBF = ml_dtypes.bfloat16

D = 1024
KC = 8
T = 512
L = 2
PAST = 512
SP_LEN = 256
NPS = 2
ALPHA = float((2 * L) ** 0.25)
EPS = 1e-6
UW = 4608
NUNIT = 17
NS_W = 3
NS_T = 3


class Buf:
    def __init__(self, name, multi=False, excl=False):
        self.name = name
        self.multi = multi
        self.excl = excl
        self.w = {}
        self.r = {}


class _Rec:
    def __getattr__(self, name):
        def f(*a, **k):
            return (name, a, k)
        return f


REC = _Rec()


class Prog:
    CE = ("pe", "act", "dve", "pool")

    def __init__(self, nc, es):
        self.nc = nc
        self.es = es
        self.eh = {"pe": nc.tensor, "act": nc.scalar, "dve": nc.vector, "pool": nc.gpsimd, "sp": nc.sync}
        self.ops = {e: [] for e in self.eh}
        self.csem = {e: es.enter_context(nc.semaphore("c_" + e)) for e in self.CE}
        self.dsem = {}
        self.nbank = 0

    def _key(self, tok):
        return tok[1] if tok[0] == "c" else ("d", tok[1])

    def _collect(self, eng, R, W):
        waits = []
        for b in R:
            waits.extend(b.w.values())
            if b.excl:
                waits.extend(t for k, t in b.r.items() if k != eng)
        for b in W:
            if not b.multi:
                waits.extend(b.w.values())
            waits.extend(b.r.values())
        out = []
        for t in waits:
            if t[0] == "c":
                if t[1] == "pe" and eng == "pe":
                    continue
                self.ops[t[1]][t[2]]["signal"] = True
                out.append(t)
            else:
                out.append(("d", t[1], self.dsem[t[1]][1]))
        return out

    def _commit(self, tok, R, W):
        k = self._key(tok)
        for b in R:
            b.r[k] = tok
        for b in W:
            if b.multi:
                b.w[k] = tok
            else:
                b.w = {k: tok}
                b.r = {}

    def op(self, eng, fn, R=(), W=()):
        waits = self._collect(eng, R, W)
        idx = len(self.ops[eng])
        self.ops[eng].append({"fn": fn(REC), "waits": waits, "signal": False, "dma": None})
        self._commit(("c", eng, idx), R, W)

    def dma(self, eng, out, in_, R, W, owner):
        if owner.name not in self.dsem:
            self.dsem[owner.name] = [self.es.enter_context(self.nc.semaphore("d_" + owner.name)), 0]
        waits = self._collect(eng, R, W)
        self.dsem[owner.name][1] += 16
        self.ops[eng].append({"fn": ("dma_start", (), {"out": out, "in_": in_}), "waits": waits,
                              "signal": False, "dma": owner.name})
        self._commit(("d", owner.name), R, W)

    def barrier(self):
        last = {}
        for e in self.CE:
            if self.ops[e]:
                for i in range(len(self.ops[e]) - 1, -1, -1):
                    if self.ops[e][i]["fn"] is not None and self.ops[e][i]["dma"] is None:
                        self.ops[e][i]["signal"] = True
                        last[e] = ("c", e, i)
                        break
        dw = [("d", n, v[1]) for n, v in self.dsem.items() if v[1] > 0 and not n.startswith(("cast_", "wbf"))]
        for e in self.eh:
            waits = [t for k, t in last.items() if not (k == "pe" and e == "pe")] + dw
            self.ops[e].append({"fn": None, "waits": waits, "signal": False, "dma": None})

    def finish(self):
        dw = [("d", n, v[1]) for n, v in self.dsem.items() if v[1] > 0]
        self.ops["sp"].append({"fn": None, "waits": dw, "signal": False, "dma": None})
        ordn = {}
        for e in self.CE:
            c = 0
            for i, o in enumerate(self.ops[e]):
                if o["signal"]:
                    c += 1
                    ordn[(e, i)] = c
        for e, h in self.eh.items():
            seen = {}
            for o in self.ops[e]:
                for t in o["waits"]:
                    if t[0] == "c":
                        sem, val, key = self.csem[t[1]], ordn[(t[1], t[2])], t[1]
                    else:
                        sem, val, key = self.dsem[t[1]][0], t[2], ("d", t[1])
                    if seen.get(key, 0) >= val:
                        continue
                    seen[key] = val
                    h.wait_ge(sem, val)
                if o["fn"] is None:
                    continue
                ins = getattr(h, o["fn"][0])(*o["fn"][1], **o["fn"][2])
                if o["dma"] is not None:
                    ins.then_inc(self.dsem[o["dma"]][0], 16)
                elif o["signal"]:
                    ins.then_inc(self.csem[e], 1)


def _attn_perm():
    idx = np.zeros(512, np.int64)
    for c in range(4):
        for p in range(128):
            head = c if p < 64 else 4 + c
            idx[c * 128 + p] = head * 64 + (p % 64)
    return idx


def _chunks(w):
    n = w.shape[1] // 128
    return np.ascontiguousarray(w.reshape(8, 128, n, 128).transpose(2, 1, 0, 3))


def _prep_weights(w_mod, b_mod, w_in, sink, conv_w, w_branch, w_o, ln_g, ln_b):
    perm = _attn_perm()
    wp1 = np.zeros((L, 128, 8, 768), np.float32)
    wmain = np.zeros((L, NUNIT, 128, UW), np.float32)
    wmod = np.zeros((L, 6, 128, 4096), np.float32)
    bmod_fm = np.zeros((L, 128, 16), np.float32)
    bmod_g = np.zeros((L, 128, 1024), np.float32)
    convw = np.zeros((L, 128, 12), np.float32)
    sinkrep = np.zeros((L, 128, 2, 512), np.float32)
    lngb = np.zeros((L, 128, 2, 1024), np.float32)
    for l in range(L):
        wi = w_in[l]
        q, k, v, za = wi[:, 0:512], wi[:, 512:640], wi[:, 640:768], wi[:, 768:1280]
        cb, cc, cx, zb = wi[:, 1280:1792], wi[:, 1792:2304], wi[:, 2304:2816], wi[:, 2816:3328]
        fx, zc, g = wi[:, 3328:3840], wi[:, 3840:4352], wi[:, 4352:7424]
        p1 = np.concatenate([fx, k, v], axis=1)
        wp1[l] = p1.reshape(8, 128, 768).transpose(1, 0, 2)
        qc, zac, zcc = _chunks(q[:, perm]), _chunks(za[:, perm]), _chunks(zc)
        ccc, cxc, cbc, zbc = _chunks(cc), _chunks(cx), _chunks(cb), _chunks(zb)
        ulist = [qc, zac] + [np.stack([cxc[c], ccc[c], cbc[c], zbc[c]]) for c in range(4)] + [zcc]
        for u in range(7):
            wmain[l, u, :, :4096] = ulist[u].transpose(1, 0, 2, 3).reshape(128, 4096)
        gch = _chunks(g)
        for j in range(8):
            parts = [gch[b * 8 + j].reshape(128, 1024) for b in range(3)]
            for b in range(3):
                wb = w_branch[l, b]
                if b == 0:
                    wb = wb[perm, :]
                parts.append(wb[:, j * 128:(j + 1) * 128].reshape(4, 128, 128).transpose(1, 0, 2).reshape(128, 512))
            wmain[l, 7 + j] = np.concatenate(parts, axis=1)
        for hf in range(2):
            wmain[l, 15 + hf, :, :4096] = (w_o[l][:, hf * 512:(hf + 1) * 512]
                                          .reshape(8, 128, 512).transpose(1, 0, 2).reshape(128, 4096))
        for m in range(6):
            wmod[l, m] = (w_mod[l][:, m * 512:(m + 1) * 512].reshape(8, 128, 512).transpose(1, 0, 2).reshape(128, 4096))
        bmod_fm[l] = b_mod[l][:2048].reshape(16, 128).T
        bmod_g[l] = np.broadcast_to(b_mod[l][2048:3072][None, :], (128, 1024))
        convw[l] = conv_w[l].reshape(3, 4, 128).transpose(2, 0, 1).reshape(128, 12)
        for g_ in range(2):
            for c in range(4):
                sinkrep[l, :, g_, c * 128:(c + 1) * 128] = sink[l, 4 * g_ + c]
        lngb[l, :, 0, :] = ln_g[l][None, :]
        lngb[l, :, 1, :] = ln_b[l][None, :]
    return dict(wp1=wp1.reshape(L, 128, 6144), wmain=wmain, wmod=wmod, bmod_fm=bmod_fm, bmod_g=bmod_g,
                convw=convw, sinkrep=sinkrep.reshape(L, 128, 1024), lngb=lngb.reshape(L, 128, 2048))


def _consts(S):
    nt = S // T
    nu = S // 512
    ident = np.eye(128, dtype=np.float32).astype(NPBF)
    rm = np.zeros((128, 128), np.float32)
    for d in range(128):
        w = d % 32
        if w < 16:
            rm[d + 16, d] = -1.0
        else:
            rm[d - 16, d] = 1.0
    rmat = rm.astype(NPBF)
    ch = np.arange(128)
    ang = 2.0 * np.pi * np.outer(ch, ch) / 128.0
    cdft = np.stack([np.cos(ang), -np.sin(ang)], axis=1) / np.sqrt(128.0)
    cdft = cdft.astype(np.float32).astype(NPBF)
    kp = np.arange(128)[:, None]
    qf = np.arange(128)[None, :]
    mL = (qf <= kp).astype(np.float32)
    mR = (kp <= qf).astype(np.float32)
    masks = np.stack([np.tile(mL, (1, 4)), np.tile(mR, (1, 4))], axis=1).astype(NPBF)
    t = np.arange(S)
    row = (t // 64).astype(np.float32)
    col = (t % 64).astype(np.float32)
    inv = (10000.0 ** (-np.arange(16, dtype=np.float32) / 16.0)).astype(np.float32)
    rope = np.zeros((128, 2, S), np.float32)
    for p in range(128):
        d = p % 64
        seg, i = d // 32, (d % 32) % 16
        a = (row if seg == 0 else col) * inv[i]
        rope[p, 0] = np.cos(a.astype(np.float32))
        rope[p, 1] = np.sin(a.astype(np.float32))
    rope = np.ascontiguousarray(rope.reshape(128, 2, nt, 512).transpose(2, 0, 1, 3))
    s = np.arange(S, dtype=np.int64)
    prod = (np.outer(s, s) % S).astype(np.float64) * (2.0 * np.pi / S)
    tabs = []
    for fn in (np.cos, np.sin):
        m = (fn(prod) / np.sqrt(float(S))).astype(np.float32).astype(NPBF)
        m = m.reshape(nu, 4, 128, nt, 512).transpose(3, 0, 2, 1, 4)
        tabs.append(m)
    dfts = np.ascontiguousarray(np.stack(tabs, axis=2)[:, :nu // 2] if nu >= 2 else np.stack(tabs, axis=2)).reshape(nt, -1, 2, 128, 2048)
    sp = np.arange(SP_LEN, dtype=np.int64)
    prodp = (np.outer(sp, sp) % SP_LEN).astype(np.float64) * (2.0 * np.pi / SP_LEN)
    tp = []
    for fn in (np.cos, np.sin):
        m = (fn(prodp) / np.sqrt(float(SP_LEN))).astype(np.float32).astype(NPBF)
        tp.append(m.reshape(2, 128, SP_LEN).transpose(1, 0, 2))
    dftp = np.ascontiguousarray(np.stack(tp, axis=1)).reshape(128, 2 * 2 * SP_LEN)
    jm = np.zeros((128, 2, 128), np.float32)
    for p in range(1, 128):
        jm[128 - p, 0, p] = 1.0
    jm[0, 1, 0] = 1.0
    altrow = (((-1.0) ** np.arange(512)) / np.sqrt(float(S))).astype(np.float32).astype(NPBF).reshape(1, 512)
    return dict(jmat=jm.astype(NPBF).reshape(128, 256), altrow=altrow, ident=ident, rmat=rmat, cdft=cdft.reshape(128, 256), masks=masks.reshape(128, 1024),
                rope=rope.reshape(nt, 128, 1024), dfts=dfts, dftp=dftp)


def build_program(S, debug=False, stop=None):
    NT = S // T
    NBS = S // 128
    NU = S // 512
    TP = NPS * SP_LEN
    nc = bass.Bass("TRN2", target_bir_lowering=False)

    def din(name, shape, dt=F32):
        return nc.dram_tensor(name, list(shape), dt, kind="ExternalInput").ap()

    def dout(name, shape, dt=F32):
        return nc.dram_tensor(name, list(shape), dt, kind="ExternalOutput").ap()

    def dscr(name, shape, dt):
        return nc.dram_tensor(name, list(shape), dt, kind="Internal").ap()

    xs = din("xs", [S, D])
    xp = din("xp", [TP, D])
    ck = din("ck", [L, PAST, 128])
    cv = din("cv", [L, PAST, 128])
    cfm = din("cfm", [128, 16])
    wp1_d = din("wp1", [L, 128, 6144])
    wmain_d = din("wmain", [L, NUNIT, 128, UW])
    wmod_d = din("wmod", [L, 6, 128, 4096])
    bmodfm_d = din("bmod_fm", [L, 128, 16])
    bmodg_d = din("bmod_g", [L, 128, 1024])
    convw_d = din("convw", [L, 128, 12])
    sinkrep_d = din("sinkrep", [L, 128, 1024])
    lngb_d = din("lngb", [L, 128, 2048])
    ident_d = din("ident", [128, 128], BF16)
    rmat_d = din("rmat", [128, 128], BF16)
    cdft_d = din("cdft", [128, 256], BF16)
    masks_d = din("masks", [128, 1024], BF16)
    rope_d = din("rope", [NT, 128, 1024])
    NU2 = max(NU // 2, 1)
    dfts_d = din("dfts", [NT, NU2, 2, 128, 2048], BF16)
    dftp_d = din("dftp", [128, 1024], BF16)
    jmat_d = din("jmat", [128, 256], BF16)
    altrow_d = din("altrow", [1, 512], BF16)

    ys = dout("ys", [S, D])
    yp = dout("yp", [TP, D])
    nk = dout("nk", [NPS, L, SP_LEN, 128])
    nv = dout("nv", [NPS, L, SP_LEN, 128])

    s_wp1 = dscr("s_wp1", [L, 128, 6144], BF16)
    s_wmain = dscr("s_wmain", [L, NUNIT, 128, UW], BF16)
    s_wmod = dscr("s_wmod", [L, 6, 128, 4096], BF16)
    s_ht = dscr("s_ht", [L, 128, KC, S + TP], BF16)
    s_x1 = dscr("s_x1", [S + TP, D], F32)

    dbg = {}
    es = ExitStack()
    with es:
        P = Prog(nc, es)

        nmc = [0]

        def sb(stack, name, shape, dt):
            nmc[0] += 1
            return stack.enter_context(nc.sbuf_tensor(f"sb{nmc[0]}_{name}", list(shape), dt))

        banks = [es.enter_context(nc.psum_tensor(f"bank{i}", [128, 512], F32)) for i in range(8)]
        bbuf = [Buf(f"bank{i}", excl=True) for i in range(8)]
        ring = [0]
        pinned = set()

        def next_bank():
            while True:
                i = ring[0]
                ring[0] = (i + 1) % 8
                if i not in pinned:
                    return banks[i], bbuf[i]

        def pin(bk):
            pinned.add([i for i, b in enumerate(banks) if b is bk][0])

        def unpin(bk):
            pinned.discard([i for i, b in enumerate(banks) if b is bk][0])

        NBLK = NBS + TP // 128
        fxr = sb(es, "fxr", [128, NBLK, 512], BF16)
        B_fxr = Buf("fxr", multi=True)
        kt_s = sb(es, "kt_s", [128, S], BF16)
        kt_p = sb(es, "kt_p", [128, TP], BF16)
        kt_c = sb(es, "kt_c", [128, PAST], BF16)
        va_s = sb(es, "va_s", [128, NBS, 256], BF16)
        va_p = sb(es, "va_p", [128, TP // 128, 256], BF16)
        va_c = sb(es, "va_c", [128, PAST // 128, 256], BF16)
        B_kts, B_ktp, B_ktc = Buf("kt_s", True), Buf("kt_p", True), Buf("kt_c", True)
        B_vas, B_vap, B_vac = Buf("va_s", True), Buf("va_p", True), Buf("va_c", True)
        ident = sb(es, "ident", [128, 128], BF16)
        rmat = sb(es, "rmat", [128, 128], BF16)
        cdft = sb(es, "cdft", [128, 2, 128], BF16)
        masks = sb(es, "masks", [128, 2, 512], BF16)
        jmat = sb(es, "jmat", [128, 2, 128], BF16)
        altrow = sb(es, "altrow", [1, 512], BF16)
        sprow = sb(es, "sprow", [1, 512], BF16)
        B_sprow = Buf("sprow")
        B_const = Buf("const", True)
        shsc = sb(es, "shsc", [128, 16, 2], F32)
        gate = sb(es, "gate", [128, 2, 1024], F32)
        lngb = sb(es, "lngb", [128, 2, 1024], F32)
        esink = sb(es, "esink", [128, 2, 512], F32)
        convw = sb(es, "convw", [128, 3, 4], F32)
        nconvw = sb(es, "nconvw", [128, 3, 4], F32)
        B_lc = Buf("layerconst", True)
        xb = [sb(es, f"xb{i}", [128, D], F32) for i in range(2)]
        B_xb = [Buf(f"xb{i}") for i in range(2)]
        ropet = sb(es, "ropet", [128, 2, 512], F32)
        B_rope = Buf("ropet")
        stat = [sb(es, f"stat{i}", [128, 16], F32) for i in range(2)]
        B_stat = [Buf(f"stat{i}") for i in range(2)]

        B_out = Buf("outs", True)
        B_sx1 = Buf("s_x1", True)
        B_sht = [Buf(f"s_ht{l}", True) for l in range(L)]
        B_wbf = [Buf(f"wbf{l}", True) for l in range(L)]

        def dump(name, ap, shape, dt, B):
            if not debug:
                return
            o = nc.dram_tensor("dbg_" + name, list(shape), dt, kind="ExternalOutput").ap()
            dbg[name] = o
            Bl = B if isinstance(B, list) else [B]
            P.dma("sp", o, ap, Bl, [B_out], Bl[0])

        def act_copy(out, in_):
            return lambda e: e.activation(out=out, in_=in_, func=AF.Copy)

        P.dma("sp", ident[:], ident_d, [], [B_const], B_const)
        P.dma("sp", rmat[:], rmat_d, [], [B_const], B_const)
        P.dma("sp", cdft[:].rearrange("p a b -> p (a b)"), cdft_d, [], [B_const], B_const)
        P.dma("sp", masks[:].rearrange("p a b -> p (a b)"), masks_d, [], [B_const], B_const)
        P.dma("sp", jmat[:].rearrange("p a b -> p (a b)"), jmat_d, [], [B_const], B_const)
        P.dma("sp", altrow[:], altrow_d, [], [B_const], B_const)
        def cbuf_of(l_, kind_, i_=0):
            if l_ == 0:
                key = (kind_, i_)
                if key not in castb:
                    castb[key] = Buf(f"cast_{kind_}{i_}", True)
                return castb[key]
            return B_wbf[l_]

        castb = {}

        def cast_list(l_):
            lst = [(s_wmod[l_, m], wmod_d[l_, m], cbuf_of(l_, "wmod", m)) for m in range(6)]
            lst.append((s_wp1[l_], wp1_d[l_], cbuf_of(l_, "wp1")))
            lst += [(s_wmain[l_, u], wmain_d[l_, u], cbuf_of(l_, "wmain", u)) for u in range(NUNIT)]
            return lst

        def emit_casts(l_):
            for (o_, i_, b_) in cast_list(l_):
                P.dma("pool", o_, i_, [], [b_], b_)

        P.op("dve", lambda e: e.memset(va_s[:].rearrange("p a b -> p (a b)"), 1.0), [], [B_vas])
        P.op("dve", lambda e: e.memset(va_p[:].rearrange("p a b -> p (a b)"), 1.0), [], [B_vap])
        P.op("dve", lambda e: e.memset(va_c[:].rearrange("p a b -> p (a b)"), 1.0), [], [B_vac])

        def vaug_dst(va, blk):
            return va[:, blk, :].rearrange("p (a b) -> p a b", b=64)[:, 0:4:3, :]

        if stop == 'prologue':
            P.barrier()
            P.finish()
            return nc, dbg
        tiles = [("s", i) for i in range(NT)] + [("p", 0)]

        def x_src(l, kind, ti, blk):
            r0 = (ti * T if kind == "s" else S) + blk * 128
            if l == 0:
                return (xs[r0:r0 + 128, :] if kind == "s" else xp[blk * 128:(blk + 1) * 128, :]), []
            return s_x1[r0:r0 + 128, :], [B_sx1]

        def tok0(kind, ti):
            return ti * T if kind == "s" else S

        for l in range(L):
            with ExitStack() as ps:
                wm = [sb(ps, f"wm{i}", [128, KC, 512], BF16) for i in range(2)]
                B_wm = [Buf(f"wm{i}") for i in range(2)]
                cf = sb(ps, "cf", [128, KC, 2], F32)
                sc = sb(ps, "sc", [128, KC, 2], BF16)
                scr = sb(ps, "scr", [128, KC, 2, 128], BF16)
                bmfm = sb(ps, "bmfm", [128, 16], F32)
                bmg = sb(ps, "bmg", [128, 1024], F32)
                ckt = sb(ps, "ckt", [128, 4, 128], BF16)
                cvt = sb(ps, "cvt", [128, 4, 128], BF16)
                B_pp = Buf("prep", True)
                B_ckt, B_cvt = Buf("ckt"), Buf("cvt")
                P.dma("sp", cf[:].rearrange("p a b -> p (a b)"), cfm, [], [B_pp], B_pp)
                P.dma("sp", bmfm[:], bmodfm_d[l], [], [B_pp], B_pp)
                P.dma("sp", bmg[:], bmodg_d[l], [], [B_pp], B_pp)
                P.dma("sp", convw[:].rearrange("p a b -> p (a b)"), convw_d[l], [], [B_lc], B_lc)
                P.dma("sp", esink[:].rearrange("p a b -> p (a b)"), sinkrep_d[l], [], [B_lc], B_lc)
                P.dma("sp", lngb[:].rearrange("p a b -> p (a b)"), lngb_d[l], [], [B_lc], B_lc)
                P.dma("pool", ckt[:], ck[l].rearrange("(b p) f -> p b f", p=128), [], [B_ckt], B_ckt)
                P.dma("pool", cvt[:], cv[l].rearrange("(b p) f -> p b f", p=128), [], [B_cvt], B_cvt)
                if l == 0:
                    emit_casts(0)
                P.op("act", lambda e: e.activation(out=esink[:], in_=esink[:], func=AF.Exp), [B_lc], [B_lc])
                P.op("act", lambda e: e.activation(out=sc[:], in_=cf[:], func=AF.Silu), [B_pp], [B_pp])
                P.op("dve", lambda e: e.tensor_scalar(out=nconvw[:], in0=convw[:], scalar1=-1.0, scalar2=None,
                                                      op0=ALU.mult), [B_lc], [B_lc])
                for cnd in range(2):
                    P.op("dve", lambda e, cnd=cnd: e.tensor_copy(
                        out=scr[:, :, cnd, :], in_=sc[:, :, cnd:cnd + 1].to_broadcast([128, KC, 128])),
                        [B_pp], [B_pp])
                bk, bb = next_bank()
                bkv = bk[:].bitcast(BF16)
                for b4 in range(4):
                    P.op("pe", lambda e, b4=b4: e.transpose(bkv[:, b4 * 128:(b4 + 1) * 128], ckt[:, b4, :], ident[:]),
                         [B_ckt, B_const], [bb])
                P.op("dve", lambda e: e.tensor_copy(out=kt_c[:], in_=bkv[:, 0:512]), [bb], [B_ktc])
                for b4 in range(4):
                    P.op("dve", lambda e, b4=b4: e.tensor_copy(
                        out=vaug_dst(va_c, b4), in_=cvt[:, b4, :].rearrange("p (a b) -> p a b", b=64)),
                        [B_cvt], [B_vac])
                sbk, sbb = next_bank()
                for m in range(6):
                    sl = m % 2
                    P.dma("sp", wm[sl][:].rearrange("p a b -> p (a b)"), s_wmod[l, m], [cbuf_of(l, "wmod", m)], [B_wm[sl]], B_wm[sl])
                    if m < 4:
                        for c4 in range(4):
                            mm = m * 4 + c4
                            for kc in range(KC):
                                P.op("pe", lambda e, sl=sl, c4=c4, kc=kc, mm=mm: e.matmul(
                                    sbk[:, mm * 2:mm * 2 + 2], lhsT=wm[sl][:, kc, c4 * 128:(c4 + 1) * 128],
                                    rhs=sc[:, kc, :], start=(kc == 0), stop=(kc == KC - 1)),
                                    [B_wm[sl], B_pp], [sbb])
                    else:
                        hf = m - 4
                        for cnd in range(2):
                            gk, gb = next_bank()
                            for kc in range(KC):
                                P.op("pe", lambda e, sl=sl, kc=kc, cnd=cnd, gk=gk: e.matmul(
                                    gk[:], lhsT=scr[:, kc, cnd, :], rhs=wm[sl][:, kc, :],
                                    start=(kc == 0), stop=(kc == KC - 1)), [B_wm[sl], B_pp], [gb])
                            P.op("dve", lambda e, gk=gk, cnd=cnd, hf=hf: e.tensor_tensor(
                                out=gate[:, cnd, hf * 512:(hf + 1) * 512], in0=gk[:], in1=bmg[:, hf * 512:(hf + 1) * 512],
                                op=ALU.add), [gb, B_pp], [B_lc])
                for cnd in range(2):
                    P.op("dve", lambda e, cnd=cnd: e.tensor_tensor(
                        out=shsc[:, :, cnd], in0=sbk[:, 0:32].rearrange("p (m c) -> p m c", c=2)[:, :, cnd],
                        in1=bmfm[:], op=ALU.add), [sbb, B_pp], [B_lc])
                P.op("dve", lambda e: e.tensor_scalar(out=shsc[:, 8:16, :], in0=shsc[:, 8:16, :], scalar1=1.0,
                                                      scalar2=None, op0=ALU.add), [B_lc], [B_lc])
                P.barrier()
                if stop == 'prep':
                    P.finish()
                    return nc, dbg

            with ExitStack() as ps:
                wp1 = sb(ps, "wp1", [128, KC, 768], BF16)
                B_wp1 = Buf("wp1")
                xn = [sb(ps, f"xn{i}", [128, D], BF16) for i in range(4)]
                B_xn = [Buf(f"xn{i}") for i in range(4)]
                xb1 = [sb(ps, f"xb1_{i}", [128, D], F32) for i in range(4)]
                B_xb1 = [Buf(f"xb1_{i}") for i in range(4)]
                st1 = [sb(ps, f"st1_{i}", [128, 16], F32) for i in range(4)]
                B_st1 = [Buf(f"st1_{i}") for i in range(4)]
                htt = [sb(ps, f"htt{i}", [128, KC, T], BF16) for i in range(2)]
                B_htt = [Buf(f"htt{i}") for i in range(2)]
                ktr = sb(ps, "ktr", [128, T], BF16)
                B_ktr = Buf("ktr")
                kvo = [sb(ps, f"kvo{i}", [128, 256], F32) for i in range(2)]
                B_kvo = [Buf(f"kvo{i}") for i in range(2)]
                kvv = [sb(ps, f"kvv{i}", [128, 128], F32) for i in range(2)]
                B_kvv = [Buf(f"kvv{i}") for i in range(2)]
                rt1 = sb(ps, "rt1", [128, T], F32)
                rt2 = sb(ps, "rt2", [128, T], F32)
                B_rt1, B_rt2 = Buf("rt1"), Buf("rt2")
                P.dma("sp", wp1[:].rearrange("p a b -> p (a b)"), s_wp1[l], [cbuf_of(l, "wp1")], [B_wp1], B_wp1)
                def p1_ln(tix):
                        kind, ti = tiles[tix]
                        cnd = 0 if kind == "s" else 1
                        t0 = tok0(kind, ti)
                        hs = tix % 2
                        for blk in range(4):
                            src, srcb = x_src(l, kind, ti, blk)
                            P.dma("sp", xb1[blk][:], src, srcb, [B_xb1[blk]], B_xb1[blk])
                            st = st1[blk]
                            for h2 in range(2):
                                P.op("dve", lambda e, st=st, blk=blk, h2=h2: e.bn_stats(
                                    st[:, h2 * 6:h2 * 6 + 6], xb1[blk][:, h2 * 512:(h2 + 1) * 512]),
                                    [B_xb1[blk]], [B_st1[blk]])
                            P.op("dve", lambda e, st=st: e.bn_aggr(st[:, 12:14], st[:, 0:12]), [B_st1[blk]], [B_st1[blk]])
                        for blk in range(4):
                            st = st1[blk]
                            P.op("act", lambda e, st=st: e.activation(out=st[:, 14:15], in_=st[:, 13:14], func=AF.Sqrt, bias=EPS, scale=1.0),
                                 [B_st1[blk]], [B_st1[blk]])
                        for blk in range(4):
                            st = st1[blk]
                            P.op("dve", lambda e, st=st: e.reciprocal(out=st[:, 14:15], in_=st[:, 14:15]), [B_st1[blk]], [B_st1[blk]])
                            P.op("dve", lambda e, st=st: e.scalar_tensor_tensor(
                                out=st[:, 15:16], in0=st[:, 12:13], scalar=-1.0, in1=st[:, 14:15],
                                op0=ALU.mult, op1=ALU.mult), [B_st1[blk]], [B_st1[blk]])
                        for blk in range(4):
                            st = st1[blk]
                            P.op("act", lambda e, st=st, blk=blk: e.activation(
                                out=xn[blk][:], in_=xb1[blk][:], func=AF.Identity, bias=st[:, 15:16], scale=st[:, 14:15]),
                                [B_xb1[blk], B_st1[blk]], [B_xn[blk]])

                def p1_tr(tix):
                        kind, ti = tiles[tix]
                        cnd = 0 if kind == "s" else 1
                        t0 = tok0(kind, ti)
                        hs = tix % 2
                        tb_banks = [next_bank() for _ in range(4)]
                        tviews = [bk[:].bitcast(BF16) for bk, _ in tb_banks]
                        for blk in range(4):
                            for kc in range(KC):
                                tv = tviews[kc // 2]
                                c0 = (kc % 2) * 512 + blk * 128
                                P.op("pe", lambda e, tv=tv, c0=c0, blk=blk, kc=kc: e.transpose(
                                    tv[:, c0:c0 + 128], xn[blk][:, kc * 128:(kc + 1) * 128], ident[:]),
                                    [B_xn[blk], B_const], [tb_banks[kc // 2][1]])
                        for kc in range(KC):
                            tv = tviews[kc // 2]
                            c0 = (kc % 2) * 512
                            eng = "act" if (kc // 2) % 2 == 0 else "dve"
                            if eng == "act":
                                P.op("act", lambda e, tv=tv, c0=c0, kc=kc, hs=hs, cnd=cnd: e.activation(
                                    out=htt[hs][:, kc, :], in_=tv[:, c0:c0 + 512], func=AF.Identity,
                                    bias=shsc[:, kc, cnd:cnd + 1], scale=shsc[:, 8 + kc, cnd:cnd + 1]),
                                    [tb_banks[kc // 2][1], B_lc], [B_htt[hs]])
                            else:
                                P.op("dve", lambda e, tv=tv, c0=c0, kc=kc, hs=hs, cnd=cnd: e.tensor_scalar(
                                    out=htt[hs][:, kc, :], in0=tv[:, c0:c0 + 512], scalar1=shsc[:, 8 + kc, cnd:cnd + 1],
                                    scalar2=shsc[:, kc, cnd:cnd + 1], op0=ALU.mult, op1=ALU.add),
                                    [tb_banks[kc // 2][1], B_lc], [B_htt[hs]])

                def p1_mm(tix):
                        kind, ti = tiles[tix]
                        cnd = 0 if kind == "s" else 1
                        t0 = tok0(kind, ti)
                        hs = tix % 2
                        if kind == "s":
                            P.dma("sp", ropet[:].rearrange("p a b -> p (a b)"), rope_d[ti], [], [B_rope], B_rope)
                        P.dma("pool", s_ht[l, :, :, t0:t0 + T], htt[hs][:], [B_htt[hs]], [B_sht[l]], B_htt[hs])
                        if l == 0 and tix == 0:
                            dump("ht0", htt[hs][:], [128, KC, T], BF16, B_htt[hs])
                        for blk in range(4):
                            bglob = (t0 // 128) + blk
                            fk, fb = next_bank()
                            kk, kb = next_bank()
                            for kc in range(KC):
                                P.op("pe", lambda e, fk=fk, kc=kc, hs=hs, blk=blk: e.matmul(
                                    fk[:], lhsT=htt[hs][:, kc, blk * 128:(blk + 1) * 128], rhs=wp1[:, kc, 0:512],
                                    start=(kc == 0), stop=(kc == KC - 1)), [B_htt[hs], B_wp1], [fb])
                            for kc in range(KC):
                                P.op("pe", lambda e, kk=kk, kc=kc, hs=hs, blk=blk: e.matmul(
                                    kk[:, 0:256], lhsT=htt[hs][:, kc, blk * 128:(blk + 1) * 128], rhs=wp1[:, kc, 512:768],
                                    start=(kc == 0), stop=(kc == KC - 1)), [B_htt[hs], B_wp1], [kb])
                            P.op("act", act_copy(fxr[:, bglob, :], fk[:]), [fb], [B_fxr])
                            va, B_va, vblk = (va_s, B_vas, ti * 4 + blk) if kind == "s" else (va_p, B_vap, blk)
                            P.op("dve", lambda e, va=va, vblk=vblk, kk=kk: e.tensor_copy(
                                out=vaug_dst(va, vblk), in_=kk[:, 128:256].rearrange("p (a b) -> p a b", b=64)),
                                [kb], [B_va])
                            if kind == "p" and not os.environ.get("NO_NKV"):
                                ks = blk % 2
                                sq, r0 = blk // 2, (blk % 2) * 128
                                P.op("act", act_copy(kvo[ks][:, 0:128], kk[:, 0:128]), [kb], [B_kvo[ks]])
                                P.dma("sp", nk[sq, l, r0:r0 + 128, :], kvo[ks][:, 0:128], [B_kvo[ks]], [B_out], B_kvo[ks])
                                P.op("act", act_copy(kvv[ks][:], kk[:, 128:256]), [kb], [B_kvv[ks]])
                                P.dma("sp", nv[sq, l, r0:r0 + 128, :], kvv[ks][:], [B_kvv[ks]], [B_out], B_kvv[ks])
                        qk, qb_ = next_bank()
                        for kc in range(KC):
                            P.op("pe", lambda e, qk=qk, kc=kc, hs=hs: e.matmul(
                                qk[:], lhsT=wp1[:, kc, 512:640], rhs=htt[hs][:, kc, :],
                                start=(kc == 0), stop=(kc == KC - 1)), [B_htt[hs], B_wp1], [qb_])
                        if kind == "p":
                            P.op("act", act_copy(kt_p[:], qk[:]), [qb_], [B_ktp])
                        else:
                            P.op("act", act_copy(ktr[:], qk[:]), [qb_], [B_ktr])
                            rk, rb = next_bank()
                            P.op("pe", lambda e, rk=rk: e.matmul(rk[:], lhsT=rmat[:], rhs=ktr[:], start=True, stop=True),
                                 [B_ktr, B_const], [rb])
                            P.op("dve", lambda e, rk=rk: e.tensor_tensor(out=rt1[:], in0=rk[:], in1=ropet[:, 1, :], op=ALU.mult),
                                 [rb, B_rope], [B_rt1])
                            P.op("pool", lambda e: e.tensor_tensor(out=rt2[:], in0=ktr[:], in1=ropet[:, 0, :], op=ALU.mult),
                                 [B_ktr, B_rope], [B_rt2])
                            P.op("dve", lambda e, t0=t0: e.tensor_tensor(out=kt_s[:, t0:t0 + T], in0=rt1[:], in1=rt2[:], op=ALU.add),
                                 [B_rt1, B_rt2], [B_kts])

                p1_ln(0)
                p1_tr(0)
                for tix in range(len(tiles)):
                    kind, ti = tiles[tix]
                    if tix + 1 < len(tiles):
                        p1_ln(tix + 1)
                    p1_mm(tix)
                    if tix + 1 < len(tiles):
                        p1_tr(tix + 1)
                if l == 0:
                    dump("fxr", fxr[:], [128, NBLK, 512], BF16, B_fxr)
                    dump("kts", kt_s[:], [128, S], BF16, B_kts)
                    dump("vas", va_s[:], [128, NBS, 256], BF16, B_vas)
                    dump("ktc", kt_c[:], [128, PAST], BF16, B_ktc)
                    dump("shsc", shsc[:], [128, 16, 2], F32, B_lc)
                    dump("gate", gate[:], [128, 2, 1024], F32, B_lc)
                HB = NBS // 2
                P.op("act", act_copy(sprow[:], fxr[0:1, HB, :]), [B_fxr], [B_sprow])
                for b in range(HB - 1, -1, -1):
                    rk, rb = next_bank()
                    P.op("pe", lambda e, rk=rk, b=b: e.matmul(rk[:], lhsT=jmat[:, 0, :], rhs=fxr[:, NBS - 1 - b, :],
                                                             start=True, stop=(b == 0)), [B_fxr, B_const, B_sprow], [rb])
                    if b >= 1:
                        P.op("pe", lambda e, rk=rk, b=b: e.matmul(rk[:], lhsT=jmat[:, 1, :], rhs=fxr[:, NBS - b, :],
                                                                 start=False, stop=True), [B_fxr, B_const], [rb])
                    P.op("dve", lambda e, rk=rk, b=b: e.tensor_tensor(out=fxr[:, NBS - 1 - b, :], in0=fxr[:, b, :], in1=rk[:], op=ALU.subtract),
                         [rb, B_fxr], [B_fxr])
                    P.op("dve", lambda e, rk=rk, b=b: e.tensor_tensor(out=fxr[:, b, :], in0=fxr[:, b, :], in1=rk[:], op=ALU.add),
                         [rb, B_fxr], [B_fxr])
                P.barrier()
                if stop == 'pass1':
                    P.finish()
                    return nc, dbg

            with ExitStack() as ps:
                wsl = [sb(ps, f"wsl{i}", [128, UW], BF16) for i in range(NS_W)]
                B_wsl = [Buf(f"wsl{i}") for i in range(NS_W)]
                tsl = [sb(ps, f"tsl{i}", [128, 4, 512], BF16) for i in range(NS_T)]
                B_tsl = [Buf(f"tsl{i}") for i in range(NS_T)]
                ht = sb(ps, "ht", [128, KC, 516], BF16)
                B_ht = Buf("ht")
                qrqt = sb(ps, "qrqt", [128, 8, T], BF16)
                B_qr, B_qt = Buf("qr"), Buf("qt")
                za = sb(ps, "za", [128, 4, T], BF16)
                zc = sb(ps, "zc", [128, 4, T], BF16)
                B_za, B_zc = Buf("za"), Buf("zc")
                ya = sb(ps, "ya", [128, 4, T], BF16)
                yb = sb(ps, "yb", [128, 4, T], BF16)
                yc = sb(ps, "yc", [128, 4, T], BF16)
                B_ya, B_yb, B_yc = Buf("ya"), Buf("yb"), Buf("yc")
                cbuf = sb(ps, "cbuf", [128, 2064], F32)
                cy = [cbuf[:, 0:512], cbuf[:, 1032:1544]]
                cu = [cbuf[:, 512:1028], cbuf[:, 1544:2060]]
                B_cu = [Buf(f"cu{i}") for i in range(2)]
                B_cy = [Buf(f"cy{i}") for i in range(2)]
                NPT = 6
                pt = [sb(ps, f"pt{i}", [128, T], BF16) for i in range(NPT)]
                B_pt = [Buf(f"pt{i}") for i in range(NPT)]
                sg = [sb(ps, f"sg{i}", [128, T], BF16) for i in range(2)]
                B_sg = [Buf(f"sg{i}") for i in range(2)]
                tmp = [sb(ps, f"tmp{i}", [128, T], F32) for i in range(2)]
                B_tmp = [Buf(f"tmp{i}") for i in range(2)]
                acc = [sb(ps, f"acc{i}", [128, T], F32) for i in range(2)]
                B_acc = [Buf(f"acc{i}") for i in range(2)]
                accv = [a_[:].bitcast(BF16) for a_ in acc]
                cnt = {"tmp": 0, "pt": 0, "sg": 0, "acc": 0, "sb": 0}
                if os.environ.get('KDBG'):
                    print('pass2 sbuf remaining', nc.sbuf_bytes_remaining)

                def nxt(name, arr, barr):
                    i = cnt[name]
                    cnt[name] = (i + 1) % len(arr)
                    return arr[i], barr[i]

                rbuf = [cbuf[:, 0:1024], cbuf[:, 1032:2056],
                        ya[:].rearrange("p a b -> p (a b)").bitcast(F32), yc[:].rearrange("p a b -> p (a b)").bitcast(F32)]
                B_r = [[B_cy[0], B_cu[0]], [B_cy[1], B_cu[1]], [B_ya], [B_yc]]
                st2 = [sb(ps, f"st2_{i}", [128, 16], F32) for i in range(4)]
                B_st2 = [Buf(f"st2_{i}") for i in range(4)]
                pq = qrqt
                mg = qrqt
                B_mgl = [B_qr, B_qt]

                nun = len(tiles) * NUNIT
                wst = {"issued": 0, "cur": 0}

                def issue_w(n):
                    while wst["issued"] < min(n, nun):
                        i = wst["issued"]
                        u = i % NUNIT
                        sl = i % NS_W
                        P.dma("sp", wsl[sl][:], s_wmain[l, u], [cbuf_of(l, "wmain", u)], [B_wsl[sl]], B_wsl[sl])
                        wst["issued"] += 1

                def take_unit(hold=0):
                    i = wst["cur"]
                    wst["cur"] += 1
                    issue_w(i + NS_W - hold)
                    sl = i % NS_W
                    return wsl[sl], B_wsl[sl]

                tunits = [(ti_, u_, tb_) for ti_ in range(NT) for tb_ in range(2) for rep_ in range(2) for u_ in range(NU2)] + [("p", 0, 0)]
                tst = {"issued": 0, "cur": 0}

                def issue_t(n):
                    while tst["issued"] < min(n, len(tunits)):
                        i = tst["issued"]
                        ti_, u_, tb_ = tunits[i]
                        sl = i % NS_T
                        if ti_ == "p":
                            P.dma("sp", tsl[sl][:].rearrange("p a b -> p (a b)")[:, 0:1024], dftp_d, [], [B_tsl[sl]], B_tsl[sl])
                        else:
                            P.dma("sp", tsl[sl][:].rearrange("p a b -> p (a b)"), dfts_d[ti_, u_, tb_], [], [B_tsl[sl]], B_tsl[sl])
                        tst["issued"] += 1

                def take_tunit():
                    i = tst["cur"]
                    tst["cur"] += 1
                    issue_t(i + NS_T)
                    sl = i % NS_T
                    return tsl[sl], B_tsl[sl]

                htp = ht[:, :, :].rearrange("p k (s t) -> p k s t", s=2)

                def load_tile_inputs(tix_):
                    kind_, ti_ = tiles[tix_]
                    t0_ = tok0(kind_, ti_)
                    if kind_ == "s":
                        lo = 1 if ti_ == 0 else 0
                        hi = 513 if ti_ == NT - 1 else 514
                        if ti_ == 0:
                            P.op("pool", lambda e: e.memset(ht[:, :, 0:1], 0.0), [], [B_ht])
                        if ti_ == NT - 1:
                            P.op("pool", lambda e: e.memset(ht[:, :, 513:514], 0.0), [], [B_ht])
                        P.dma("sp", ht[:, :, lo:hi], s_ht[l, :, :, t0_ - 1 + lo:t0_ - 1 + hi], [B_sht[l]], [B_ht], B_ht)
                        P.dma("sp", ropet[:].rearrange("p a b -> p (a b)"), rope_d[ti_], [], [B_rope], B_rope)
                    else:
                        P.op("pool", lambda e: e.memset(htp[:, :, :, 0:258:257], 0.0), [], [B_ht])
                        for sq_ in range(2):
                            P.dma("sp", htp[:, :, sq_, 1:257], s_ht[l, :, :, t0_ + sq_ * 256:t0_ + (sq_ + 1) * 256],
                                  [B_sht[l]], [B_ht], B_ht)

                deferred = []
                issue_w(NS_W - 1)
                issue_t(NS_T - 1)
                load_tile_inputs(0)
                pending_casts = cast_list(l + 1) if l + 1 < L else []
                gblk = 0
                for tix, (kind, ti) in enumerate(tiles):
                    cnd = 0 if kind == "s" else 1
                    t0 = tok0(kind, ti)
                    isS = kind == "s"

                    def v2(ap):
                        return ap if isS else ap.rearrange("p (s t) -> p s t", s=2)

                    if isS:
                        def hmain(kc):
                            return ht[:, kc, 1:513]

                        def hhalo(kc):
                            return ht[:, kc, 0:514:513]
                        nh = 2
                    else:
                        def hmain(kc):
                            return htp[:, kc, :, 1:257]

                        def hhalo(kc):
                            return htp[:, kc, :, 0:258:257]
                        nh = 4

                    def proj(wt, ch, B_w, into, B_into):
                        for kc in range(KC):
                            P.op("pe", lambda e, kc=kc: e.matmul(
                                into, lhsT=wt[:, ch * 1024 + kc * 128: ch * 1024 + (kc + 1) * 128], rhs=hmain(kc),
                                start=(kc == 0), stop=(kc == KC - 1)), [B_w, B_ht], [B_into])

                    wt, B_w = take_unit()
                    for c in range(4):
                        bk, bb = next_bank()
                        proj(wt, c, B_w, bk[:], bb)
                        if isS:
                            P.op("act", act_copy(qrqt[:, c, :], bk[:]), [bb], [B_qr])
                            rk, rb = next_bank()
                            P.op("pe", lambda e, rk=rk, c=c: e.matmul(rk[:], lhsT=rmat[:], rhs=qrqt[:, c, :], start=True, stop=True),
                                 [B_qr, B_const], [rb])
                            t1, B_t1 = nxt("tmp", tmp, B_tmp)
                            t2, B_t2 = nxt("tmp", tmp, B_tmp)
                            P.op("dve", lambda e, rk=rk, t1=t1: e.tensor_tensor(out=t1[:], in0=rk[:], in1=ropet[:, 1, :], op=ALU.mult),
                                 [rb, B_rope], [B_t1])
                            P.op("pool", lambda e, t2=t2, c=c: e.tensor_tensor(out=t2[:], in0=qrqt[:, c, :], in1=ropet[:, 0, :], op=ALU.mult),
                                 [B_qr, B_rope], [B_t2])
                            P.op("dve", lambda e, t1=t1, t2=t2, c=c: e.tensor_tensor(out=qrqt[:, 4 + c, :], in0=t1[:], in1=t2[:], op=ALU.add),
                                 [B_t1, B_t2], [B_qt])
                        else:
                            P.op("act", act_copy(qrqt[:, 4 + c, :], bk[:]), [bb], [B_qt])
                    wt, B_w = take_unit()
                    for c in range(4):
                        bk, bb = next_bank()
                        proj(wt, c, B_w, bk[:], bb)
                        P.op("act", lambda e, bk=bk, c=c: e.activation(out=za[:, c, :], in_=bk[:], func=AF.Silu), [bb], [B_za])
                    while deferred:
                        deferred.pop(0)()
                    for _ in range(4):
                        if pending_casts:
                            o_, i_, b_ = pending_casts.pop(0)
                            P.dma("pool", o_, i_, [], [b_], b_)
                    for c in range(4):
                        wt, B_w = take_unit()
                        bx, bbx = next_bank()
                        bc, bbc = next_bank()
                        proj(wt, 0, B_w, bx[:], bbx)
                        proj(wt, 1, B_w, bc[:], bbc)
                        hk, hb = next_bank()
                        for wi_ in range(2):
                            for kc in range(KC):
                                o_ = hk[:, wi_ * 8: wi_ * 8 + nh]
                                P.op("pe", lambda e, kc=kc, o_=o_, wi_=wi_, wt=wt: e.matmul(
                                    o_, lhsT=wt[:, wi_ * 1024 + kc * 128: wi_ * 1024 + (kc + 1) * 128], rhs=hhalo(kc),
                                    start=(kc == 0), stop=(kc == KC - 1)), [B_w, B_ht], [hb])
                        bcb, bbcb = next_bank()
                        bzb, bbzb = next_bank()
                        proj(wt, 2, B_w, bcb[:], bbcb)
                        proj(wt, 3, B_w, bzb[:], bbzb)
                        cs = c % 2
                        cuu, B_cuu = cu[cs], B_cu[cs]
                        cyy, B_cyy = cy[cs], B_cy[cs]
                        if isS:
                            um = cuu[:, 1:513]
                            uh = cuu[:, 0:514:513]
                            taps = [cuu[:, 0:512], cuu[:, 1:513], cuu[:, 2:514]]
                        else:
                            cup = cuu.rearrange("p (s t) -> p s t", s=2)
                            um = cup[:, :, 1:257]
                            uh = cup[:, :, 0:258:257]
                            taps = [cup[:, :, 0:256], cup[:, :, 1:257], cup[:, :, 2:258]]
                        cyv = v2(cyy)
                        bxm, bcm = v2(bx[:]), v2(bc[:])
                        hx, hc = v2(hk[:, 0:nh]), v2(hk[:, 8:8 + nh])
                        P.op("act", act_copy(um, bxm), [bbx], [B_cuu])
                        P.op("act", act_copy(uh, hx), [hb], [B_cuu])
                        P.op("dve", lambda e, um=um, bcm=bcm: e.tensor_tensor(out=um, in0=bcm, in1=um, op=ALU.mult), [bbc, B_cuu], [B_cuu])
                        P.op("dve", lambda e, uh=uh, hc=hc: e.tensor_tensor(out=uh, in0=hc, in1=uh, op=ALU.mult), [hb, B_cuu], [B_cuu])
                        P.op("act", lambda e, cyv=cyv, taps=taps, c=c: e.activation(
                            out=cyv, in_=taps[1], func=AF.Copy, scale=convw[:, 1, c:c + 1]), [B_cuu, B_lc], [B_cyy])
                        P.op("dve", lambda e, cyv=cyv, taps=taps, c=c: e.scalar_tensor_tensor(
                            out=cyv, in0=taps[0], scalar=convw[:, 0, c:c + 1], in1=cyv, op0=ALU.mult, op1=ALU.add), [B_cuu, B_lc, B_cyy], [B_cyy])
                        P.op("dve", lambda e, cyv=cyv, taps=taps, c=c: e.scalar_tensor_tensor(
                            out=cyv, in0=taps[2], scalar=convw[:, 2, c:c + 1], in1=cyv, op0=ALU.mult, op1=ALU.add), [B_cuu, B_lc, B_cyy], [B_cyy])
                        zs, B_zs = nxt("sg", sg, B_sg)
                        P.op("act", lambda e, zs=zs, bzb=bzb: e.activation(out=zs[:], in_=bzb[:], func=AF.Silu), [bbzb], [B_zs])
                        gt, B_gt = nxt("tmp", tmp, B_tmp)
                        P.op("dve", lambda e, gt=gt, bcb=bcb, zs=zs: e.tensor_tensor(out=gt[:], in0=bcb[:], in1=zs[:], op=ALU.mult),
                             [bbcb, B_zs], [B_gt])
                        P.op("dve", lambda e, gt=gt, cyy=cyy, c=c: e.tensor_tensor(out=yb[:, c, :], in0=cyy, in1=gt[:], op=ALU.mult),
                             [B_cyy, B_gt], [B_yb])
                    wt, B_w = take_unit()
                    for c in range(4):
                        bk, bb = next_bank()
                        proj(wt, c, B_w, bk[:], bb)
                        P.op("act", lambda e, bk=bk, c=c: e.activation(out=zc[:, c, :], in_=bk[:], func=AF.Silu), [bb], [B_zc])

                    pend_norm = []
                    for qb in range(4):
                        if isS:
                            i = ti * 4 + qb
                            chunks = []
                            if i > 0:
                                chunks.append((kt_s, B_kts, (i - 1) * 128, va_s, B_vas, i - 1, 0))
                            chunks.append((kt_s, B_kts, i * 128, va_s, B_vas, i, None))
                            if i < NBS - 1:
                                chunks.append((kt_s, B_kts, (i + 1) * 128, va_s, B_vas, i + 1, 1))
                            for b4 in range(4):
                                chunks.append((kt_c, B_ktc, b4 * 128, va_c, B_vac, b4, None))
                        else:
                            sq = qb // 2
                            chunks = [(kt_p, B_ktp, (2 * sq + j_) * 128, va_p, B_vap, 2 * sq + j_, None) for j_ in range(2)]
                        qs = slice(qb * 128, (qb + 1) * 128)
                        obs = [(banks[(qb % 2) * 2 + g], bbuf[(qb % 2) * 2 + g]) for g in range(2)]

                        def emit_s(g, n):
                            kt, B_kt, k0, va, B_va, vblk, mk = chunks[n]
                            gs = slice(g * 64, (g + 1) * 64)
                            si = 4 + cnt["sb"]
                            cnt["sb"] = (cnt["sb"] + 1) % (2 if isS else 4)
                            sk, sbb = banks[si], bbuf[si]
                            P.op("pe", lambda e, sk=sk, kt=kt, k0=k0, gs=gs: e.matmul(
                                sk[:], lhsT=kt[gs, k0:k0 + 128], rhs=qrqt[gs, 4:8, qs],
                                start=True, stop=True), [B_kt, B_qt], [sbb])
                            pp, B_pp_ = nxt("pt", pt, B_pt)
                            P.op("act", lambda e, pp=pp, sk=sk: e.activation(out=pp[:], in_=sk[:], func=AF.Exp, scale=0.125),
                                 [sbb], [B_pp_])
                            if mk is not None:
                                P.op("pool", lambda e, pp=pp, mk=mk: e.tensor_tensor(out=pp[:], in0=pp[:], in1=masks[:, mk, :], op=ALU.mult),
                                     [B_pp_, B_const], [B_pp_])
                            return pp, B_pp_

                        def emit_pv(g, n, pp, B_pp_):
                            kt, B_kt, k0, va, B_va, vblk, mk = chunks[n]
                            ob, obb = obs[g]
                            P.op("pe", lambda e, va=va, vblk=vblk, pp=pp, n=n, ob=ob, g=g: e.matmul(
                                ob[:], lhsT=va[:, vblk, g * 128:(g + 1) * 128], rhs=pp[:],
                                start=(n == 0), stop=(n == len(chunks) - 1)), [B_va, B_pp_], [obb])

                        tasks = [(g, n) for n in range(len(chunks)) for g in range(2)]
                        LA = 1 if isS else 3
                        fl = []
                        if isS:
                            ftb = qb // 2
                            fgp = [(qb % 2) * 2, (qb % 2) * 2 + 1]
                            fstate = {}
                            for u in range(NU2):
                                for gi, gch in enumerate(fgp):
                                    for sc_ in range(4):
                                        def ffn(u=u, gi=gi, gch=gch, sc_=sc_, ftb=ftb, fstate=fstate):
                                            if u not in fstate:
                                                fstate[u] = take_tunit()
                                            tt, B_tt = fstate[u]
                                            blk = u * 4 + sc_
                                            if ftb == 1:
                                                blk = NBS - 1 - blk
                                            last = (u == NU2 - 1 and sc_ == 3)
                                            P.op("pe", lambda e: e.matmul(
                                                banks[6 + gi][:], lhsT=fxr[:, blk, gch * 128:(gch + 1) * 128], rhs=tt[:, sc_, :],
                                                start=(u == 0 and sc_ == 0), stop=(last and ftb == 1)), [B_fxr, B_tt], [bbuf[6 + gi]])
                                        fl.append(ffn)
                            if ftb == 0:
                                for gi, gch in enumerate(fgp):
                                    def ffs(gi=gi, gch=gch):
                                        P.op("pe", lambda e: e.matmul(
                                            banks[6 + gi][:], lhsT=sprow[0:1, gch * 128:(gch + 1) * 128], rhs=altrow[0:1, :],
                                            start=False, stop=True), [B_sprow, B_const], [bbuf[6 + gi]])
                                    fl.append(ffs)
                        pend = []
                        if isS:
                            prev = None
                            for n in range(len(chunks)):
                                cur = [(g, n) + emit_s(g, n) for g in range(2)]
                                for _ in range(4):
                                    if fl:
                                        fl.pop(0)()
                                if prev is not None:
                                    for t_ in prev:
                                        emit_pv(*t_)
                                    for _ in range(2):
                                        if fl:
                                            fl.pop(0)()
                                prev = cur
                            for t_ in prev:
                                emit_pv(*t_)
                        else:
                            for (g, n) in tasks:
                                pend.append((g, n) + emit_s(g, n))
                                if len(pend) > LA:
                                    emit_pv(*pend.pop(0))
                            while pend:
                                emit_pv(*pend.pop(0))
                        while fl:
                            fl.pop(0)()
                        if isS:
                            for gi, gch in enumerate(fgp):
                                if ftb == 0:
                                    dst, B_dst = pq[:, gch, :], B_qr
                                else:
                                    dst, B_dst = accv[gch // 2][:, (gch % 2) * 512:(gch % 2 + 1) * 512], B_acc[gch // 2]
                                if gi == 0:
                                    P.op("act", act_copy(dst, banks[6 + gi][:]), [bbuf[6 + gi]], [B_dst])
                                else:
                                    P.op("dve", lambda e, dst=dst, gi=gi: e.tensor_copy(out=dst, in_=banks[6 + gi][:]), [bbuf[6 + gi]], [B_dst])
                        def emit_norm(obs=obs, qs=qs):
                            t1, B_t1 = nxt("tmp", tmp, B_tmp)
                            for g in range(2):
                                ob, obb = obs[g]
                                orow = slice(g * 64, (g + 1) * 64)
                                drow = slice(64, 128) if g == 0 else slice(0, 64)
                                P.op("dve", lambda e, t1=t1, ob=ob, orow=orow, drow=drow, g=g: e.tensor_tensor(
                                    out=t1[orow, :], in0=ob[drow, :], in1=esink[orow, g, :], op=ALU.add), [obb, B_lc], [B_t1])
                            P.op("dve", lambda e, t1=t1: e.reciprocal(out=t1[:], in_=t1[:]), [B_t1], [B_t1])
                            for g in range(2):
                                ob, obb = obs[g]
                                orow = slice(g * 64, (g + 1) * 64)
                                P.op("dve", lambda e, t1=t1, ob=ob, orow=orow: e.tensor_tensor(
                                    out=t1[orow, :], in0=ob[orow, :], in1=t1[orow, :], op=ALU.mult), [obb, B_t1], [B_t1])
                            P.op("dve", lambda e, t1=t1, qs=qs: e.tensor_tensor(
                                out=ya[:, :, qs], in0=t1[:].rearrange("p (c q) -> p c q", c=4), in1=za[:, :, qs], op=ALU.mult),
                                [B_t1, B_za], [B_ya])

                        emit_norm()
                    while pend_norm:
                        pend_norm.pop(0)()

                    if not isS:
                        pbs = [next_bank() for _ in range(8)]
                        tt, B_tt = take_tunit()
                        dftp = tt[:].rearrange("p a b -> p (a b)")[:, 0:1024].rearrange("p (a b c) -> p a b c", a=2, b=2)
                        for tb in range(2):
                            for gch in range(4):
                                bk, bb = pbs[tb * 4 + gch]
                                for sq in range(2):
                                    for sc_ in range(2):
                                        blk = NBS + 2 * sq + sc_
                                        P.op("pe", lambda e, bk=bk, blk=blk, gch=gch, tb=tb, sc_=sc_, sq=sq: e.matmul(
                                            bk[:, sq * 256:(sq + 1) * 256], lhsT=fxr[:, blk, gch * 128:(gch + 1) * 128],
                                            rhs=dftp[:, tb, sc_, :], start=(sc_ == 0), stop=(sc_ == 1)), [B_fxr, B_tt], [bb])
                        for gch in range(4):
                            bkp, bbp = pbs[gch]
                            bkq, bbq = pbs[4 + gch]
                            P.op("act", act_copy(pq[:, gch, :], bkp[:]), [bbp], [B_qr])
                            P.op("dve", lambda e, gch=gch, bkq=bkq: e.tensor_copy(out=pq[:, 4 + gch, :], in_=bkq[:]), [bbq], [B_qt])
                    def emit_cdft():
                      for gch in range(4):
                        bk, bb = next_bank()
                        if isS:
                            qsrc, B_qsrc = accv[gch // 2][:, (gch % 2) * 512:(gch % 2 + 1) * 512], B_acc[gch // 2]
                        else:
                            qsrc, B_qsrc = pq[:, 4 + gch, :], B_qt
                        P.op("pe", lambda e, bk=bk, gch=gch: e.matmul(bk[:], lhsT=cdft[:, 0, :], rhs=pq[:, gch, :], start=True, stop=False),
                             [B_const, B_qr], [bb])
                        P.op("pe", lambda e, bk=bk, qsrc=qsrc: e.matmul(bk[:], lhsT=cdft[:, 1, :], rhs=qsrc, start=False, stop=True),
                             [B_const, B_qsrc], [bb])
                        P.op("dve", lambda e, bk=bk, gch=gch: e.tensor_tensor(out=yc[:, gch, :], in0=bk[:], in1=zc[:, gch, :], op=ALU.mult),
                             [bb, B_zc], [B_yc])

                    if l == 0 and tix == 0:
                        dump("ya", ya[:], [128, 4, T], BF16, B_ya)
                        dump("yb", yb[:], [128, 4, T], BF16, B_yb)
                        dump("yc", yc[:], [128, 4, T], BF16, B_yc)

                    Ys = ((ya, B_ya), (yb, B_yb), (yc, B_yc))
                    for j in range(8):
                        wt, B_w = take_unit()
                        ac, B_ac = nxt("acc", acc, B_acc)
                        for bi, b in enumerate((1, 2, 0)):
                            gk, gbb = next_bank()
                            for kc in range(KC):
                                P.op("pe", lambda e, gk=gk, kc=kc, b=b, wt=wt: e.matmul(
                                    gk[:], lhsT=wt[:, b * 1024 + kc * 128: b * 1024 + (kc + 1) * 128], rhs=hmain(kc),
                                    start=(kc == 0), stop=(kc == KC - 1)), [B_w, B_ht], [gbb])
                            s_, B_s = nxt("sg", sg, B_sg)
                            P.op("act", lambda e, s_=s_, gk=gk: e.activation(out=s_[:], in_=gk[:], func=AF.Sigmoid), [gbb], [B_s])
                            bk, bb = next_bank()
                            Y, B_Y = Ys[b]
                            for kc in range(4):
                                P.op("pe", lambda e, bk=bk, kc=kc, b=b, wt=wt, Y=Y: e.matmul(
                                    bk[:], lhsT=wt[:, 3072 + b * 512 + kc * 128: 3072 + b * 512 + (kc + 1) * 128], rhs=Y[:, kc, :],
                                    start=(kc == 0), stop=(kc == 3)), [B_w, B_Y], [bb])
                            if j == 0 and bi == 0:
                                emit_cdft()
                            if bi == 0:
                                P.op("dve", lambda e, ac=ac, bk=bk, s_=s_: e.tensor_tensor(out=ac[:], in0=bk[:], in1=s_[:], op=ALU.mult),
                                     [bb, B_s], [B_ac])
                            else:
                                tq, B_tq = nxt("tmp", tmp, B_tmp)
                                P.op("dve", lambda e, tq=tq, bk=bk, s_=s_: e.tensor_tensor(out=tq[:], in0=bk[:], in1=s_[:], op=ALU.mult),
                                     [bb, B_s], [B_tq])
                                if bi == 1:
                                    P.op("pool", lambda e, ac=ac, tq=tq: e.tensor_tensor(out=ac[:], in0=ac[:], in1=tq[:], op=ALU.add),
                                         [B_ac, B_tq], [B_ac])
                                else:
                                    P.op("pool", lambda e, ac=ac, tq=tq, j=j: e.tensor_tensor(out=mg[:, j, :], in0=ac[:], in1=tq[:], op=ALU.add),
                                         [B_ac, B_tq], [B_mgl[j // 4]])
                    if l == 0 and tix == 0:
                        dump("mg", mg[:], [128, 8, T], BF16, [B_qt, B_qr])

                    if tix + 1 < len(tiles):
                        load_tile_inputs(tix + 1)
                    wo = [take_unit(), take_unit(hold=1)]

                    def load_x(tb_):
                        xsl_ = tb_ % 2
                        src, srcb = x_src(l, kind, ti, tb_)
                        P.dma("sp", xb[xsl_][:], src, srcb, [B_xb[xsl_]], B_xb[xsl_])

                    load_x(0)
                    load_x(1)

                    def fin1(tb):
                        xsl = tb % 2
                        r, B_rr = rbuf[tb], B_r[tb]
                        st, B_st = st2[tb], B_st2[tb]
                        for hf in range(2):
                            bk, bb = next_bank()
                            wot, B_wo = wo[hf]
                            for kc in range(KC):
                                P.op("pe", lambda e, bk=bk, kc=kc, tb=tb, wot=wot: e.matmul(
                                    bk[:], lhsT=mg[:, kc, tb * 128:(tb + 1) * 128], rhs=wot[:, kc * 512:(kc + 1) * 512],
                                    start=(kc == 0), stop=(kc == KC - 1)), [B_qr, B_qt, B_wo], [bb])
                            P.op("dve", lambda e, r=r, bk=bk, hf=hf: e.tensor_tensor(
                                out=r[:, hf * 512:(hf + 1) * 512], in0=bk[:], in1=gate[:, cnd, hf * 512:(hf + 1) * 512], op=ALU.mult),
                                [bb, B_lc], B_rr)
                        P.op("dve", lambda e, r=r, xsl=xsl: e.scalar_tensor_tensor(
                            out=r, in0=xb[xsl][:], scalar=ALPHA, in1=r, op0=ALU.mult, op1=ALU.add), [B_xb[xsl]] + B_rr, B_rr)
                        if tb + 2 < 4:
                            load_x(tb + 2)
                        for h2 in range(2):
                            P.op("dve", lambda e, st=st, r=r, h2=h2: e.bn_stats(st[:, h2 * 6:h2 * 6 + 6], r[:, h2 * 512:(h2 + 1) * 512]),
                                 B_rr, [B_st])
                        P.op("dve", lambda e, st=st: e.bn_aggr(st[:, 12:14], st[:, 0:12]), [B_st], [B_st])

                    def fin2(tb, t0=t0, isS=isS, tix=tix):
                        r, B_rr = rbuf[tb], B_r[tb]
                        st, B_st = st2[tb], B_st2[tb]
                        P.op("act", lambda e, st=st: e.activation(out=st[:, 14:15], in_=st[:, 13:14], func=AF.Sqrt, bias=EPS, scale=1.0),
                             [B_st], [B_st])
                        P.op("dve", lambda e, st=st: e.reciprocal(out=st[:, 14:15], in_=st[:, 14:15]), [B_st], [B_st])
                        P.op("dve", lambda e, st=st: e.scalar_tensor_tensor(
                            out=st[:, 15:16], in0=st[:, 12:13], scalar=-1.0, in1=st[:, 14:15], op0=ALU.mult, op1=ALU.mult), [B_st], [B_st])
                        P.op("act", lambda e, st=st, r=r: e.activation(out=r, in_=r, func=AF.Identity, bias=st[:, 15:16], scale=st[:, 14:15]),
                             B_rr + [B_st], B_rr)
                        P.op("pool", lambda e, r=r: e.tensor_tensor(out=r, in0=r, in1=lngb[:, 0, :], op=ALU.mult), B_rr + [B_lc], B_rr)
                        P.op("pool", lambda e, r=r: e.tensor_tensor(out=r, in0=r, in1=lngb[:, 1, :], op=ALU.add), B_rr + [B_lc], B_rr)
                        r0 = t0 + tb * 128
                        if l == L - 1:
                            dst = ys[r0:r0 + 128, :] if isS else yp[tb * 128:(tb + 1) * 128, :]
                            P.dma("pool", dst, r, B_rr, [B_out], B_rr[0])
                        else:
                            P.dma("pool", s_x1[r0:r0 + 128, :], r, B_rr, [B_sx1], B_rr[0])
                            if debug and tix == 0:
                                if tb == 0:
                                    dbg["x1"] = nc.dram_tensor("dbg_x1", [512, 1024], F32, kind="ExternalOutput").ap()
                                P.dma("sp", dbg["x1"][tb * 128:(tb + 1) * 128, :], r, B_rr, [B_out], B_rr[0])

                    fin1(0)
                    fin1(1)
                    fin2(0)
                    fin1(2)
                    fin2(1)
                    if tix == len(tiles) - 1:
                        while pending_casts:
                            o_, i_, b_ = pending_casts.pop(0)
                            P.dma("pool", o_, i_, [], [b_], b_)
                    fin1(3)
                    issue_w(wst["cur"] + NS_W)
                    if tix + 1 < len(tiles):
                        deferred.extend([lambda f=fin2: f(2), lambda f=fin2: f(3)])
                    else:
                        fin2(2)
                        fin2(3)
                P.barrier()

        P.finish()
    return nc, dbg


_CACHE = {}


def _get_program(S):
    if S not in _CACHE:
        _CACHE[S] = build_program(S)[0]
    return _CACHE[S]


def make_in_maps(x_prompt, x_sample, cache_k, cache_v, c, c_ctx, w_mod, b_mod, w_in, sink, conv_w,
                 w_branch, w_o, ln_g, ln_b, n_cores):
    f = lambda a: np.ascontiguousarray(np.asarray(a, dtype=np.float32))
    x_prompt, x_sample, cache_k, cache_v, c, c_ctx = map(f, (x_prompt, x_sample, cache_k, cache_v, c, c_ctx))
    S = x_sample.shape[1]
    shared = _prep_weights(*map(f, (w_mod, b_mod, w_in, sink, conv_w, w_branch, w_o, ln_g, ln_b)))
    shared.update(_consts(S))
    in_maps = []
    for i in range(n_cores):
        cf = np.zeros((128, 8, 2), np.float32)
        cf[:, :, 0] = c[i].reshape(8, 128).T
        cf[:, :, 1] = c_ctx.reshape(8, 128).T
        m = dict(shared)
        m["xs"] = x_sample[i]
        m["xp"] = np.ascontiguousarray(x_prompt[NPS * i:NPS * (i + 1)].reshape(NPS * SP_LEN, D))
        m["ck"] = np.ascontiguousarray(cache_k[i].reshape(L, PAST, 128))
        m["cv"] = np.ascontiguousarray(cache_v[i].reshape(L, PAST, 128))
        m["cfm"] = cf.reshape(128, 16)
        in_maps.append(m)
    return in_maps, S


def kernel(x_prompt, x_sample, cache_k, cache_v, c, c_ctx, w_mod, b_mod, w_in, sink, conv_w,
           w_branch, w_o, ln_g, ln_b):
    n = 8
    in_maps, S = make_in_maps(x_prompt, x_sample, cache_k, cache_v, c, c_ctx, w_mod, b_mod, w_in, sink, conv_w,
                              w_branch, w_o, ln_g, ln_b, n)
    nc = _get_program(S)
    res = run_bass_kernel_spmd(nc, in_maps, core_ids=list(range(n)))
    rs = res.results
    y_sample = np.stack([np.asarray(r["ys"], dtype=np.float32) for r in rs], axis=0)
    y_prompt = np.concatenate([np.asarray(r["yp"], dtype=np.float32).reshape(NPS, SP_LEN, D) for r in rs], axis=0)
    new_k = np.concatenate([np.asarray(r["nk"], dtype=np.float32).reshape(NPS, L, SP_LEN, 2, 64) for r in rs], axis=0)
    new_v = np.concatenate([np.asarray(r["nv"], dtype=np.float32).reshape(NPS, L, SP_LEN, 2, 64) for r in rs], axis=0)
    return (y_prompt, y_sample, new_k, new_v)
```

```python
import os
import numpy as np
import ml_dtypes
from contextlib import ExitStack
import concourse.bass as bass
import concourse.mybir as mybir
from concourse.bass_utils import run_bass_kernel_spmd

F32 = mybir.dt.float32
BF16 = mybir.dt.bfloat16
AF = mybir.ActivationFunctionType
ALU = mybir.AluOpType
NPBF = ml_dtypes.bfloat16

D = 1024
KC = 8
T = 512
L = 2
PAST = 512
SP_LEN = 256
NPS = 2
ALPHA = float((2 * L) ** 0.25)
EPS = 1e-6
UW = 4608
NUNIT = 17
NS_W = 3
NS_T = 3


class Buf:
    def __init__(self, name, multi=False, excl=False):
        self.name = name
        self.multi = multi
        self.excl = excl
        self.w = {}
        self.r = {}


class _Rec:
    def __getattr__(self, name):
        def f(*a, **k):
            return (name, a, k)
        return f


REC = _Rec()


class Prog:
    CE = ("pe", "act", "dve", "pool")

    def __init__(self, nc, es):
        self.nc = nc
        self.es = es
        self.eh = {"pe": nc.tensor, "act": nc.scalar, "dve": nc.vector, "pool": nc.gpsimd, "sp": nc.sync}
        self.ops = {e: [] for e in self.eh}
        self.csem = {e: es.enter_context(nc.semaphore("c_" + e)) for e in self.CE}
        self.dsem = {}
        self.nbank = 0

    def _key(self, tok):
        return tok[1] if tok[0] == "c" else ("d", tok[1])

    def _collect(self, eng, R, W):
        waits = []
        for b in R:
            waits.extend(b.w.values())
            if b.excl:
                waits.extend(t for k, t in b.r.items() if k != eng)
        for b in W:
            if not b.multi:
                waits.extend(b.w.values())
            waits.extend(b.r.values())
        out = []
        for t in waits:
            if t[0] == "c":
                if t[1] == "pe" and eng == "pe":
                    continue
                self.ops[t[1]][t[2]]["signal"] = True
                out.append(t)
            else:
                out.append(("d", t[1], self.dsem[t[1]][1]))
        return out

    def _commit(self, tok, R, W):
        k = self._key(tok)
        for b in R:
            b.r[k] = tok
        for b in W:
            if b.multi:
                b.w[k] = tok
            else:
                b.w = {k: tok}
                b.r = {}

    def op(self, eng, fn, R=(), W=()):
        waits = self._collect(eng, R, W)
        idx = len(self.ops[eng])
        self.ops[eng].append({"fn": fn(REC), "waits": waits, "signal": False, "dma": None})
        self._commit(("c", eng, idx), R, W)

    def dma(self, eng, out, in_, R, W, owner):
        if owner.name not in self.dsem:
            self.dsem[owner.name] = [self.es.enter_context(self.nc.semaphore("d_" + owner.name)), 0]
        waits = self._collect(eng, R, W)
        self.dsem[owner.name][1] += 16
        self.ops[eng].append({"fn": ("dma_start", (), {"out": out, "in_": in_}), "waits": waits,
                              "signal": False, "dma": owner.name})
        self._commit(("d", owner.name), R, W)

    def barrier(self):
        last = {}
        for e in self.CE:
            if self.ops[e]:
                for i in range(len(self.ops[e]) - 1, -1, -1):
                    if self.ops[e][i]["fn"] is not None and self.ops[e][i]["dma"] is None:
                        self.ops[e][i]["signal"] = True
                        last[e] = ("c", e, i)
                        break
        dw = [("d", n, v[1]) for n, v in self.dsem.items() if v[1] > 0 and not n.startswith(("cast_", "wbf"))]
        for e in self.eh:
            waits = [t for k, t in last.items() if not (k == "pe" and e == "pe")] + dw
            self.ops[e].append({"fn": None, "waits": waits, "signal": False, "dma": None})

    def finish(self):
        dw = [("d", n, v[1]) for n, v in self.dsem.items() if v[1] > 0]
        self.ops["sp"].append({"fn": None, "waits": dw, "signal": False, "dma": None})
        ordn = {}
        for e in self.CE:
            c = 0
            for i, o in enumerate(self.ops[e]):
                if o["signal"]:
                    c += 1
                    ordn[(e, i)] = c
        for e, h in self.eh.items():
            seen = {}
            for o in self.ops[e]:
                for t in o["waits"]:
                    if t[0] == "c":
                        sem, val, key = self.csem[t[1]], ordn[(t[1], t[2])], t[1]
                    else:
                        sem, val, key = self.dsem[t[1]][0], t[2], ("d", t[1])
                    if seen.get(key, 0) >= val:
                        continue
                    seen[key] = val
                    h.wait_ge(sem, val)
                if o["fn"] is None:
                    continue
                ins = getattr(h, o["fn"][0])(*o["fn"][1], **o["fn"][2])
                if o["dma"] is not None:
                    ins.then_inc(self.dsem[o["dma"]][0], 16)
                elif o["signal"]:
                    ins.then_inc(self.csem[e], 1)


def _attn_perm():
    idx = np.zeros(512, np.int64)
    for c in range(4):
        for p in range(128):
            head = c if p < 64 else 4 + c
            idx[c * 128 + p] = head * 64 + (p % 64)
    return idx


def _chunks(w):
    n = w.shape[1] // 128
    return np.ascontiguousarray(w.reshape(8, 128, n, 128).transpose(2, 1, 0, 3))


def _prep_weights(w_mod, b_mod, w_in, sink, conv_w, w_branch, w_o, ln_g, ln_b):
    perm = _attn_perm()
    wp1 = np.zeros((L, 128, 8, 768), np.float32)
    wmain = np.zeros((L, NUNIT, 128, UW), np.float32)
    wmod = np.zeros((L, 6, 128, 4096), np.float32)
    bmod_fm = np.zeros((L, 128, 16), np.float32)
    bmod_g = np.zeros((L, 128, 1024), np.float32)
    convw = np.zeros((L, 128, 12), np.float32)
    sinkrep = np.zeros((L, 128, 2, 512), np.float32)
    lngb = np.zeros((L, 128, 2, 1024), np.float32)
    for l in range(L):
        wi = w_in[l]
        q, k, v, za = wi[:, 0:512], wi[:, 512:640], wi[:, 640:768], wi[:, 768:1280]
        cb, cc, cx, zb = wi[:, 1280:1792], wi[:, 1792:2304], wi[:, 2304:2816], wi[:, 2816:3328]
        fx, zc, g = wi[:, 3328:3840], wi[:, 3840:4352], wi[:, 4352:7424]
        p1 = np.concatenate([fx, k, v], axis=1)
        wp1[l] = p1.reshape(8, 128, 768).transpose(1, 0, 2)
        qc, zac, zcc = _chunks(q[:, perm]), _chunks(za[:, perm]), _chunks(zc)
        ccc, cxc, cbc, zbc = _chunks(cc), _chunks(cx), _chunks(cb), _chunks(zb)
        ulist = [qc, zac] + [np.stack([cxc[c], ccc[c], cbc[c], zbc[c]]) for c in range(4)] + [zcc]
        for u in range(7):
            wmain[l, u, :, :4096] = ulist[u].transpose(1, 0, 2, 3).reshape(128, 4096)
        gch = _chunks(g)
        for j in range(8):
            parts = [gch[b * 8 + j].reshape(128, 1024) for b in range(3)]
            for b in range(3):
                wb = w_branch[l, b]
                if b == 0:
                    wb = wb[perm, :]
                parts.append(wb[:, j * 128:(j + 1) * 128].reshape(4, 128, 128).transpose(1, 0, 2).reshape(128, 512))
            wmain[l, 7 + j] = np.concatenate(parts, axis=1)
        for hf in range(2):
            wmain[l, 15 + hf, :, :4096] = (w_o[l][:, hf * 512:(hf + 1) * 512]
                                          .reshape(8, 128, 512).transpose(1, 0, 2).reshape(128, 4096))
        for m in range(6):
            wmod[l, m] = (w_mod[l][:, m * 512:(m + 1) * 512].reshape(8, 128, 512).transpose(1, 0, 2).reshape(128, 4096))
        bmod_fm[l] = b_mod[l][:2048].reshape(16, 128).T
        bmod_g[l] = np.broadcast_to(b_mod[l][2048:3072][None, :], (128, 1024))
        convw[l] = conv_w[l].reshape(3, 4, 128).transpose(2, 0, 1).reshape(128, 12)
        for g_ in range(2):
            for c in range(4):
                sinkrep[l, :, g_, c * 128:(c + 1) * 128] = sink[l, 4 * g_ + c]
        lngb[l, :, 0, :] = ln_g[l][None, :]
        lngb[l, :, 1, :] = ln_b[l][None, :]
    return dict(wp1=wp1.reshape(L, 128, 6144), wmain=wmain, wmod=wmod, bmod_fm=bmod_fm, bmod_g=bmod_g,
                convw=convw, sinkrep=sinkrep.reshape(L, 128, 1024), lngb=lngb.reshape(L, 128, 2048))


def _consts(S):
    nt = S // T
    nu = S // 512
    ident = np.eye(128, dtype=np.float32).astype(NPBF)
    rm = np.zeros((128, 128), np.float32)
    for d in range(128):
        w = d % 32
        if w < 16:
            rm[d + 16, d] = -1.0
        else:
            rm[d - 16, d] = 1.0
    rmat = rm.astype(NPBF)
    ch = np.arange(128)
    ang = 2.0 * np.pi * np.outer(ch, ch) / 128.0
    cdft = np.stack([np.cos(ang), -np.sin(ang)], axis=1) / np.sqrt(128.0)
    cdft = cdft.astype(np.float32).astype(NPBF)
    kp = np.arange(128)[:, None]
    qf = np.arange(128)[None, :]
    mL = (qf <= kp).astype(np.float32)
    mR = (kp <= qf).astype(np.float32)
    masks = np.stack([np.tile(mL, (1, 4)), np.tile(mR, (1, 4))], axis=1).astype(NPBF)
    t = np.arange(S)
    row = (t // 64).astype(np.float32)
    col = (t % 64).astype(np.float32)
    inv = (10000.0 ** (-np.arange(16, dtype=np.float32) / 16.0)).astype(np.float32)
    rope = np.zeros((128, 2, S), np.float32)
    for p in range(128):
        d = p % 64
        seg, i = d // 32, (d % 32) % 16
        a = (row if seg == 0 else col) * inv[i]
        rope[p, 0] = np.cos(a.astype(np.float32))
        rope[p, 1] = np.sin(a.astype(np.float32))
    rope = np.ascontiguousarray(rope.reshape(128, 2, nt, 512).transpose(2, 0, 1, 3))
    s = np.arange(S, dtype=np.int64)
    prod = (np.outer(s, s) % S).astype(np.float64) * (2.0 * np.pi / S)
    tabs = []
    for fn in (np.cos, np.sin):
        m = (fn(prod) / np.sqrt(float(S))).astype(np.float32).astype(NPBF)
        m = m.reshape(nu, 4, 128, nt, 512).transpose(3, 0, 2, 1, 4)
        tabs.append(m)
    dfts = np.ascontiguousarray(np.stack(tabs, axis=2)[:, :nu // 2] if nu >= 2 else np.stack(tabs, axis=2)).reshape(nt, -1, 2, 128, 2048)
    sp = np.arange(SP_LEN, dtype=np.int64)
    prodp = (np.outer(sp, sp) % SP_LEN).astype(np.float64) * (2.0 * np.pi / SP_LEN)
    tp = []
    for fn in (np.cos, np.sin):
        m = (fn(prodp) / np.sqrt(float(SP_LEN))).astype(np.float32).astype(NPBF)
        tp.append(m.reshape(2, 128, SP_LEN).transpose(1, 0, 2))
    dftp = np.ascontiguousarray(np.stack(tp, axis=1)).reshape(128, 2 * 2 * SP_LEN)
    jm = np.zeros((128, 2, 128), np.float32)
    for p in range(1, 128):
        jm[128 - p, 0, p] = 1.0
    jm[0, 1, 0] = 1.0
    altrow = (((-1.0) ** np.arange(512)) / np.sqrt(float(S))).astype(np.float32).astype(NPBF).reshape(1, 512)
    return dict(jmat=jm.astype(NPBF).reshape(128, 256), altrow=altrow, ident=ident, rmat=rmat, cdft=cdft.reshape(128, 256), masks=masks.reshape(128, 1024),
                rope=rope.reshape(nt, 128, 1024), dfts=dfts, dftp=dftp)


def build_program(S, debug=False, stop=None):
    NT = S // T
    NBS = S // 128
    NU = S // 512
    TP = NPS * SP_LEN
    nc = bass.Bass("TRN2", target_bir_lowering=False)

    def din(name, shape, dt=F32):
        return nc.dram_tensor(name, list(shape), dt, kind="ExternalInput").ap()

    def dout(name, shape, dt=F32):
        return nc.dram_tensor(name, list(shape), dt, kind="ExternalOutput").ap()

    def dscr(name, shape, dt):
        return nc.dram_tensor(name, list(shape), dt, kind="Internal").ap()

    xs = din("xs", [S, D])
    xp = din("xp", [TP, D])
    ck = din("ck", [L, PAST, 128])
    cv = din("cv", [L, PAST, 128])
    cfm = din("cfm", [128, 16])
    wp1_d = din("wp1", [L, 128, 6144])
    wmain_d = din("wmain", [L, NUNIT, 128, UW])
    wmod_d = din("wmod", [L, 6, 128, 4096])
    bmodfm_d = din("bmod_fm", [L, 128, 16])
    bmodg_d = din("bmod_g", [L, 128, 1024])
    convw_d = din("convw", [L, 128, 12])
    sinkrep_d = din("sinkrep", [L, 128, 1024])
    lngb_d = din("lngb", [L, 128, 2048])
    ident_d = din("ident", [128, 128], BF16)
    rmat_d = din("rmat", [128, 128], BF16)
    cdft_d = din("cdft", [128, 256], BF16)
    masks_d = din("masks", [128, 1024], BF16)
    rope_d = din("rope", [NT, 128, 1024])
    NU2 = max(NU // 2, 1)
    dfts_d = din("dfts", [NT, NU2, 2, 128, 2048], BF16)
    dftp_d = din("dftp", [128, 1024], BF16)
    jmat_d = din("jmat", [128, 256], BF16)
    altrow_d = din("altrow", [1, 512], BF16)

    ys = dout("ys", [S, D])
    yp = dout("yp", [TP, D])
    nk = dout("nk", [NPS, L, SP_LEN, 128])
    nv = dout("nv", [NPS, L, SP_LEN, 128])

    s_wp1 = dscr("s_wp1", [L, 128, 6144], BF16)
    s_wmain = dscr("s_wmain", [L, NUNIT, 128, UW], BF16)
    s_wmod = dscr("s_wmod", [L, 6, 128, 4096], BF16)
    s_ht = dscr("s_ht", [L, 128, KC, S + TP], BF16)
    s_x1 = dscr("s_x1", [S + TP, D], F32)

    dbg = {}
    es = ExitStack()
    with es:
        P = Prog(nc, es)

        nmc = [0]

        def sb(stack, name, shape, dt):
            nmc[0] += 1
            return stack.enter_context(nc.sbuf_tensor(f"sb{nmc[0]}_{name}", list(shape), dt))

        banks = [es.enter_context(nc.psum_tensor(f"bank{i}", [128, 512], F32)) for i in range(8)]
        bbuf = [Buf(f"bank{i}", excl=True) for i in range(8)]
        ring = [0]
        pinned = set()

        def next_bank():
            while True:
                i = ring[0]
                ring[0] = (i + 1) % 8
                if i not in pinned:
                    return banks[i], bbuf[i]

        def pin(bk):
            pinned.add([i for i, b in enumerate(banks) if b is bk][0])

        def unpin(bk):
            pinned.discard([i for i, b in enumerate(banks) if b is bk][0])

        NBLK = NBS + TP // 128
        fxr = sb(es, "fxr", [128, NBLK, 512], BF16)
        B_fxr = Buf("fxr", multi=True)
        kt_s = sb(es, "kt_s", [128, S], BF16)
        kt_p = sb(es, "kt_p", [128, TP], BF16)
        kt_c = sb(es, "kt_c", [128, PAST], BF16)
        va_s = sb(es, "va_s", [128, NBS, 256], BF16)
        va_p = sb(es, "va_p", [128, TP // 128, 256], BF16)
        va_c = sb(es, "va_c", [128, PAST // 128, 256], BF16)
        B_kts, B_ktp, B_ktc = Buf("kt_s", True), Buf("kt_p", True), Buf("kt_c", True)
        B_vas, B_vap, B_vac = Buf("va_s", True), Buf("va_p", True), Buf("va_c", True)
        ident = sb(es, "ident", [128, 128], BF16)
        rmat = sb(es, "rmat", [128, 128], BF16)
        cdft = sb(es, "cdft", [128, 2, 128], BF16)
        masks = sb(es, "masks", [128, 2, 512], BF16)
        jmat = sb(es, "jmat", [128, 2, 128], BF16)
        altrow = sb(es, "altrow", [1, 512], BF16)
        sprow = sb(es, "sprow", [1, 512], BF16)
        B_sprow = Buf("sprow")
        B_const = Buf("const", True)
        shsc = sb(es, "shsc", [128, 16, 2], F32)
        gate = sb(es, "gate", [128, 2, 1024], F32)
        lngb = sb(es, "lngb", [128, 2, 1024], F32)
        esink = sb(es, "esink", [128, 2, 512], F32)
        convw = sb(es, "convw", [128, 3, 4], F32)
        nconvw = sb(es, "nconvw", [128, 3, 4], F32)
        B_lc = Buf("layerconst", True)
        xb = [sb(es, f"xb{i}", [128, D], F32) for i in range(2)]
        B_xb = [Buf(f"xb{i}") for i in range(2)]
        ropet = sb(es, "ropet", [128, 2, 512], F32)
        B_rope = Buf("ropet")
        stat = [sb(es, f"stat{i}", [128, 16], F32) for i in range(2)]
        B_stat = [Buf(f"stat{i}") for i in range(2)]

        B_out = Buf("outs", True)
        B_sx1 = Buf("s_x1", True)
        B_sht = [Buf(f"s_ht{l}", True) for l in range(L)]
        B_wbf = [Buf(f"wbf{l}", True) for l in range(L)]

        def dump(name, ap, shape, dt, B):
            if not debug:
                return
            o = nc.dram_tensor("dbg_" + name, list(shape), dt, kind="ExternalOutput").ap()
            dbg[name] = o
            Bl = B if isinstance(B, list) else [B]
            P.dma("sp", o, ap, Bl, [B_out], Bl[0])

        def act_copy(out, in_):
            return lambda e: e.activation(out=out, in_=in_, func=AF.Copy)

        P.dma("sp", ident[:], ident_d, [], [B_const], B_const)
        P.dma("sp", rmat[:], rmat_d, [], [B_const], B_const)
        P.dma("sp", cdft[:].rearrange("p a b -> p (a b)"), cdft_d, [], [B_const], B_const)
        P.dma("sp", masks[:].rearrange("p a b -> p (a b)"), masks_d, [], [B_const], B_const)
        P.dma("sp", jmat[:].rearrange("p a b -> p (a b)"), jmat_d, [], [B_const], B_const)
        P.dma("sp", altrow[:], altrow_d, [], [B_const], B_const)
        def cbuf_of(l_, kind_, i_=0):
            if l_ == 0:
                key = (kind_, i_)
                if key not in castb:
                    castb[key] = Buf(f"cast_{kind_}{i_}", True)
                return castb[key]
            return B_wbf[l_]

        castb = {}

        def cast_list(l_):
            lst = [(s_wmod[l_, m], wmod_d[l_, m], cbuf_of(l_, "wmod", m)) for m in range(6)]
            lst.append((s_wp1[l_], wp1_d[l_], cbuf_of(l_, "wp1")))
            lst += [(s_wmain[l_, u], wmain_d[l_, u], cbuf_of(l_, "wmain", u)) for u in range(NUNIT)]
            return lst

        def emit_casts(l_):
            for (o_, i_, b_) in cast_list(l_):
                P.dma("pool", o_, i_, [], [b_], b_)

        P.op("dve", lambda e: e.memset(va_s[:].rearrange("p a b -> p (a b)"), 1.0), [], [B_vas])
        P.op("dve", lambda e: e.memset(va_p[:].rearrange("p a b -> p (a b)"), 1.0), [], [B_vap])
        P.op("dve", lambda e: e.memset(va_c[:].rearrange("p a b -> p (a b)"), 1.0), [], [B_vac])

        def vaug_dst(va, blk):
            return va[:, blk, :].rearrange("p (a b) -> p a b", b=64)[:, 0:4:3, :]

        if stop == 'prologue':
            P.barrier()
            P.finish()
            return nc, dbg
        tiles = [("s", i) for i in range(NT)] + [("p", 0)]

        def x_src(l, kind, ti, blk):
            r0 = (ti * T if kind == "s" else S) + blk * 128
            if l == 0:
                return (xs[r0:r0 + 128, :] if kind == "s" else xp[blk * 128:(blk + 1) * 128, :]), []
            return s_x1[r0:r0 + 128, :], [B_sx1]

        def tok0(kind, ti):
            return ti * T if kind == "s" else S

        for l in range(L):
            with ExitStack() as ps:
                wm = [sb(ps, f"wm{i}", [128, KC, 512], BF16) for i in range(2)]
                B_wm = [Buf(f"wm{i}") for i in range(2)]
                cf = sb(ps, "cf", [128, KC, 2], F32)
                sc = sb(ps, "sc", [128, KC, 2], BF16)
                scr = sb(ps, "scr", [128, KC, 2, 128], BF16)
                bmfm = sb(ps, "bmfm", [128, 16], F32)
                bmg = sb(ps, "bmg", [128, 1024], F32)
                ckt = sb(ps, "ckt", [128, 4, 128], BF16)
                cvt = sb(ps, "cvt", [128, 4, 128], BF16)
                B_pp = Buf("prep", True)
                B_ckt, B_cvt = Buf("ckt"), Buf("cvt")
                P.dma("sp", cf[:].rearrange("p a b -> p (a b)"), cfm, [], [B_pp], B_pp)
                P.dma("sp", bmfm[:], bmodfm_d[l], [], [B_pp], B_pp)
                P.dma("sp", bmg[:], bmodg_d[l], [], [B_pp], B_pp)
                P.dma("sp", convw[:].rearrange("p a b -> p (a b)"), convw_d[l], [], [B_lc], B_lc)
                P.dma("sp", esink[:].rearrange("p a b -> p (a b)"), sinkrep_d[l], [], [B_lc], B_lc)
                P.dma("sp", lngb[:].rearrange("p a b -> p (a b)"), lngb_d[l], [], [B_lc], B_lc)
                P.dma("pool", ckt[:], ck[l].rearrange("(b p) f -> p b f", p=128), [], [B_ckt], B_ckt)
                P.dma("pool", cvt[:], cv[l].rearrange("(b p) f -> p b f", p=128), [], [B_cvt], B_cvt)
                if l == 0:
                    l0_casts = cast_list(0)
                    for (o_, i_, b_) in l0_casts[:7]:
                        P.dma("pool", o_, i_, [], [b_], b_)
                    l0_casts = l0_casts[7:]
                P.op("act", lambda e: e.activation(out=esink[:], in_=esink[:], func=AF.Exp), [B_lc], [B_lc])
                P.op("act", lambda e: e.activation(out=sc[:], in_=cf[:], func=AF.Silu), [B_pp], [B_pp])
                P.op("dve", lambda e: e.tensor_scalar(out=nconvw[:], in0=convw[:], scalar1=-1.0, scalar2=None,
                                                      op0=ALU.mult), [B_lc], [B_lc])
                for cnd in range(2):
                    P.op("dve", lambda e, cnd=cnd: e.tensor_copy(
                        out=scr[:, :, cnd, :], in_=sc[:, :, cnd:cnd + 1].to_broadcast([128, KC, 128])),
                        [B_pp], [B_pp])
                bk, bb = next_bank()
                bkv = bk[:].bitcast(BF16)
                for b4 in range(4):
                    P.op("pe", lambda e, b4=b4: e.transpose(bkv[:, b4 * 128:(b4 + 1) * 128], ckt[:, b4, :], ident[:]),
                         [B_ckt, B_const], [bb])
                P.op("dve", lambda e: e.tensor_copy(out=kt_c[:], in_=bkv[:, 0:512]), [bb], [B_ktc])
                for b4 in range(4):
                    P.op("dve", lambda e, b4=b4: e.tensor_copy(
                        out=vaug_dst(va_c, b4), in_=cvt[:, b4, :].rearrange("p (a b) -> p a b", b=64)),
                        [B_cvt], [B_vac])
                sbk, sbb = next_bank()
                for m in range(6):
                    sl = m % 2
                    P.dma("sp", wm[sl][:].rearrange("p a b -> p (a b)"), s_wmod[l, m], [cbuf_of(l, "wmod", m)], [B_wm[sl]], B_wm[sl])
                    if m < 4:
                        for c4 in range(4):
                            mm = m * 4 + c4
                            for kc in range(KC):
                                P.op("pe", lambda e, sl=sl, c4=c4, kc=kc, mm=mm: e.matmul(
                                    sbk[:, mm * 2:mm * 2 + 2], lhsT=wm[sl][:, kc, c4 * 128:(c4 + 1) * 128],
                                    rhs=sc[:, kc, :], start=(kc == 0), stop=(kc == KC - 1)),
                                    [B_wm[sl], B_pp], [sbb])
                    else:
                        hf = m - 4
                        for cnd in range(2):
                            gk, gb = next_bank()
                            for kc in range(KC):
                                P.op("pe", lambda e, sl=sl, kc=kc, cnd=cnd, gk=gk: e.matmul(
                                    gk[:], lhsT=scr[:, kc, cnd, :], rhs=wm[sl][:, kc, :],
                                    start=(kc == 0), stop=(kc == KC - 1)), [B_wm[sl], B_pp], [gb])
                            P.op("dve", lambda e, gk=gk, cnd=cnd, hf=hf: e.tensor_tensor(
                                out=gate[:, cnd, hf * 512:(hf + 1) * 512], in0=gk[:], in1=bmg[:, hf * 512:(hf + 1) * 512],
                                op=ALU.add), [gb, B_pp], [B_lc])
                for cnd in range(2):
                    P.op("dve", lambda e, cnd=cnd: e.tensor_tensor(
                        out=shsc[:, :, cnd], in0=sbk[:, 0:32].rearrange("p (m c) -> p m c", c=2)[:, :, cnd],
                        in1=bmfm[:], op=ALU.add), [sbb, B_pp], [B_lc])
                P.op("dve", lambda e: e.tensor_scalar(out=shsc[:, 8:16, :], in0=shsc[:, 8:16, :], scalar1=1.0,
                                                      scalar2=None, op0=ALU.add), [B_lc], [B_lc])
                P.barrier()
                if stop == 'prep':
                    P.finish()
                    return nc, dbg

            with ExitStack() as ps:
                wp1 = sb(ps, "wp1", [128, KC, 768], BF16)
                B_wp1 = Buf("wp1")
                xn = [sb(ps, f"xn{i}", [128, D], BF16) for i in range(4)]
                B_xn = [Buf(f"xn{i}") for i in range(4)]
                xb1 = [sb(ps, f"xb1_{i}", [128, D], F32) for i in range(4)]
                B_xb1 = [Buf(f"xb1_{i}") for i in range(4)]
                st1 = [sb(ps, f"st1_{i}", [128, 16], F32) for i in range(4)]
                B_st1 = [Buf(f"st1_{i}") for i in range(4)]
                htt = [sb(ps, f"htt{i}", [128, KC, T], BF16) for i in range(2)]
                B_htt = [Buf(f"htt{i}") for i in range(2)]
                ktr = sb(ps, "ktr", [128, T], BF16)
                B_ktr = Buf("ktr")
                kvo = [sb(ps, f"kvo{i}", [128, 256], F32) for i in range(2)]
                B_kvo = [Buf(f"kvo{i}") for i in range(2)]
                kvv = [sb(ps, f"kvv{i}", [128, 128], F32) for i in range(2)]
                B_kvv = [Buf(f"kvv{i}") for i in range(2)]
                rt1 = sb(ps, "rt1", [128, T], F32)
                rt2 = sb(ps, "rt2", [128, T], F32)
                B_rt1, B_rt2 = Buf("rt1"), Buf("rt2")
                P.dma("sp", wp1[:].rearrange("p a b -> p (a b)"), s_wp1[l], [cbuf_of(l, "wp1")], [B_wp1], B_wp1)
                def p1_ln(tix):
                        kind, ti = tiles[tix]
                        cnd = 0 if kind == "s" else 1
                        t0 = tok0(kind, ti)
                        hs = tix % 2
                        for blk in range(4):
                            src, srcb = x_src(l, kind, ti, blk)
                            P.dma("sp", xb1[blk][:], src, srcb, [B_xb1[blk]], B_xb1[blk])
                            st = st1[blk]
                            for h2 in range(2):
                                P.op("dve", lambda e, st=st, blk=blk, h2=h2: e.bn_stats(
                                    st[:, h2 * 6:h2 * 6 + 6], xb1[blk][:, h2 * 512:(h2 + 1) * 512]),
                                    [B_xb1[blk]], [B_st1[blk]])
                            P.op("dve", lambda e, st=st: e.bn_aggr(st[:, 12:14], st[:, 0:12]), [B_st1[blk]], [B_st1[blk]])
                        for blk in range(4):
                            st = st1[blk]
                            P.op("act", lambda e, st=st: e.activation(out=st[:, 14:15], in_=st[:, 13:14], func=AF.Sqrt, bias=EPS, scale=1.0),
                                 [B_st1[blk]], [B_st1[blk]])
                        for blk in range(4):
                            st = st1[blk]
                            P.op("dve", lambda e, st=st: e.reciprocal(out=st[:, 14:15], in_=st[:, 14:15]), [B_st1[blk]], [B_st1[blk]])
                            P.op("dve", lambda e, st=st: e.scalar_tensor_tensor(
                                out=st[:, 15:16], in0=st[:, 12:13], scalar=-1.0, in1=st[:, 14:15],
                                op0=ALU.mult, op1=ALU.mult), [B_st1[blk]], [B_st1[blk]])
                        for blk in range(4):
                            st = st1[blk]
                            P.op("act", lambda e, st=st, blk=blk: e.activation(
                                out=xn[blk][:], in_=xb1[blk][:], func=AF.Identity, bias=st[:, 15:16], scale=st[:, 14:15]),
                                [B_xb1[blk], B_st1[blk]], [B_xn[blk]])

                def p1_tr(tix):
                        kind, ti = tiles[tix]
                        cnd = 0 if kind == "s" else 1
                        t0 = tok0(kind, ti)
                        hs = tix % 2
                        tb_banks = [next_bank() for _ in range(4)]
                        tviews = [bk[:].bitcast(BF16) for bk, _ in tb_banks]
                        for blk in range(4):
                            for kc in range(KC):
                                tv = tviews[kc // 2]
                                c0 = (kc % 2) * 512 + blk * 128
                                P.op("pe", lambda e, tv=tv, c0=c0, blk=blk, kc=kc: e.transpose(
                                    tv[:, c0:c0 + 128], xn[blk][:, kc * 128:(kc + 1) * 128], ident[:]),
                                    [B_xn[blk], B_const], [tb_banks[kc // 2][1]])
                        for kc in range(KC):
                            tv = tviews[kc // 2]
                            c0 = (kc % 2) * 512
                            eng = "act" if (kc // 2) % 2 == 0 else "dve"
                            if eng == "act":
                                P.op("act", lambda e, tv=tv, c0=c0, kc=kc, hs=hs, cnd=cnd: e.activation(
                                    out=htt[hs][:, kc, :], in_=tv[:, c0:c0 + 512], func=AF.Identity,
                                    bias=shsc[:, kc, cnd:cnd + 1], scale=shsc[:, 8 + kc, cnd:cnd + 1]),
                                    [tb_banks[kc // 2][1], B_lc], [B_htt[hs]])
                            else:
                                P.op("dve", lambda e, tv=tv, c0=c0, kc=kc, hs=hs, cnd=cnd: e.tensor_scalar(
                                    out=htt[hs][:, kc, :], in0=tv[:, c0:c0 + 512], scalar1=shsc[:, 8 + kc, cnd:cnd + 1],
                                    scalar2=shsc[:, kc, cnd:cnd + 1], op0=ALU.mult, op1=ALU.add),
                                    [tb_banks[kc // 2][1], B_lc], [B_htt[hs]])

                def p1_mm(tix):
                        kind, ti = tiles[tix]
                        cnd = 0 if kind == "s" else 1
                        t0 = tok0(kind, ti)
                        hs = tix % 2
                        if kind == "s":
                            P.dma("sp", ropet[:].rearrange("p a b -> p (a b)"), rope_d[ti], [], [B_rope], B_rope)
                        P.dma("pool", s_ht[l, :, :, t0:t0 + T], htt[hs][:], [B_htt[hs]], [B_sht[l]], B_htt[hs])
                        if l == 0:
                            for _ in range(3 if tix == 0 else 2):
                                if l0_casts:
                                    o_, i_, b_ = l0_casts.pop(0)
                                    P.dma("pool", o_, i_, [], [b_], b_)
                        if l == 0 and tix == 0:
                            dump("ht0", htt[hs][:], [128, KC, T], BF16, B_htt[hs])
                        for blk in range(4):
                            bglob = (t0 // 128) + blk
                            fk, fb = next_bank()
                            kk, kb = next_bank()
                            for kc in range(KC):
                                P.op("pe", lambda e, fk=fk, kc=kc, hs=hs, blk=blk: e.matmul(
                                    fk[:], lhsT=htt[hs][:, kc, blk * 128:(blk + 1) * 128], rhs=wp1[:, kc, 0:512],
                                    start=(kc == 0), stop=(kc == KC - 1)), [B_htt[hs], B_wp1], [fb])
                            for kc in range(KC):
                                P.op("pe", lambda e, kk=kk, kc=kc, hs=hs, blk=blk: e.matmul(
                                    kk[:, 0:256], lhsT=htt[hs][:, kc, blk * 128:(blk + 1) * 128], rhs=wp1[:, kc, 512:768],
                                    start=(kc == 0), stop=(kc == KC - 1)), [B_htt[hs], B_wp1], [kb])
                            P.op("act", act_copy(fxr[:, bglob, :], fk[:]), [fb], [B_fxr])
                            va, B_va, vblk = (va_s, B_vas, ti * 4 + blk) if kind == "s" else (va_p, B_vap, blk)
                            P.op("dve", lambda e, va=va, vblk=vblk, kk=kk: e.tensor_copy(
                                out=vaug_dst(va, vblk), in_=kk[:, 128:256].rearrange("p (a b) -> p a b", b=64)),
                                [kb], [B_va])
                            if kind == "p" and not os.environ.get("NO_NKV"):
                                ks = blk % 2
                                sq, r0 = blk // 2, (blk % 2) * 128
                                P.op("act", act_copy(kvo[ks][:, 0:128], kk[:, 0:128]), [kb], [B_kvo[ks]])
                                P.dma("sp", nk[sq, l, r0:r0 + 128, :], kvo[ks][:, 0:128], [B_kvo[ks]], [B_out], B_kvo[ks])
                                P.op("act", act_copy(kvv[ks][:], kk[:, 128:256]), [kb], [B_kvv[ks]])
                                P.dma("sp", nv[sq, l, r0:r0 + 128, :], kvv[ks][:], [B_kvv[ks]], [B_out], B_kvv[ks])
                        qk, qb_ = next_bank()
                        for kc in range(KC):
                            P.op("pe", lambda e, qk=qk, kc=kc, hs=hs: e.matmul(
                                qk[:], lhsT=wp1[:, kc, 512:640], rhs=htt[hs][:, kc, :],
                                start=(kc == 0), stop=(kc == KC - 1)), [B_htt[hs], B_wp1], [qb_])
                        if kind == "p":
                            P.op("act", act_copy(kt_p[:], qk[:]), [qb_], [B_ktp])
                        else:
                            P.op("act", act_copy(ktr[:], qk[:]), [qb_], [B_ktr])
                            rk, rb = next_bank()
                            P.op("pe", lambda e, rk=rk: e.matmul(rk[:], lhsT=rmat[:], rhs=ktr[:], start=True, stop=True),
                                 [B_ktr, B_const], [rb])
                            P.op("dve", lambda e, rk=rk: e.tensor_tensor(out=rt1[:], in0=rk[:], in1=ropet[:, 1, :], op=ALU.mult),
                                 [rb, B_rope], [B_rt1])
                            P.op("pool", lambda e: e.tensor_tensor(out=rt2[:], in0=ktr[:], in1=ropet[:, 0, :], op=ALU.mult),
                                 [B_ktr, B_rope], [B_rt2])
                            P.op("dve", lambda e, t0=t0: e.tensor_tensor(out=kt_s[:, t0:t0 + T], in0=rt1[:], in1=rt2[:], op=ALU.add),
                                 [B_rt1, B_rt2], [B_kts])

                p1_ln(0)
                p1_tr(0)
                for tix in range(len(tiles)):
                    kind, ti = tiles[tix]
                    if tix + 1 < len(tiles):
                        p1_ln(tix + 1)
                    p1_mm(tix)
                    if tix + 1 < len(tiles):
                        p1_tr(tix + 1)
                if l == 0:
                    dump("fxr", fxr[:], [128, NBLK, 512], BF16, B_fxr)
                    dump("kts", kt_s[:], [128, S], BF16, B_kts)
                    dump("vas", va_s[:], [128, NBS, 256], BF16, B_vas)
                    dump("ktc", kt_c[:], [128, PAST], BF16, B_ktc)
                    dump("shsc", shsc[:], [128, 16, 2], F32, B_lc)
                    dump("gate", gate[:], [128, 2, 1024], F32, B_lc)
                if l == 0:
                    while l0_casts:
                        o_, i_, b_ = l0_casts.pop(0)
                        P.dma("pool", o_, i_, [], [b_], b_)
                HB = NBS // 2
                P.op("act", act_copy(sprow[:], fxr[0:1, HB, :]), [B_fxr], [B_sprow])
                for b in range(HB - 1, -1, -1):
                    rk, rb = next_bank()
                    P.op("pe", lambda e, rk=rk, b=b: e.matmul(rk[:], lhsT=jmat[:, 0, :], rhs=fxr[:, NBS - 1 - b, :],
                                                             start=True, stop=(b == 0)), [B_fxr, B_const, B_sprow], [rb])
                    if b >= 1:
                        P.op("pe", lambda e, rk=rk, b=b: e.matmul(rk[:], lhsT=jmat[:, 1, :], rhs=fxr[:, NBS - b, :],
                                                                 start=False, stop=True), [B_fxr, B_const], [rb])
                    P.op("dve", lambda e, rk=rk, b=b: e.tensor_tensor(out=fxr[:, NBS - 1 - b, :], in0=fxr[:, b, :], in1=rk[:], op=ALU.subtract),
                         [rb, B_fxr], [B_fxr])
                    P.op("dve", lambda e, rk=rk, b=b: e.tensor_tensor(out=fxr[:, b, :], in0=fxr[:, b, :], in1=rk[:], op=ALU.add),
                         [rb, B_fxr], [B_fxr])
                P.barrier()
                if stop == 'pass1':
                    P.finish()
                    return nc, dbg

            with ExitStack() as ps:
                wsl = [sb(ps, f"wsl{i}", [128, UW], BF16) for i in range(NS_W)]
                B_wsl = [Buf(f"wsl{i}") for i in range(NS_W)]
                tsl = [sb(ps, f"tsl{i}", [128, 4, 512], BF16) for i in range(NS_T)]
                B_tsl = [Buf(f"tsl{i}") for i in range(NS_T)]
                ht = sb(ps, "ht", [128, KC, 516], BF16)
                B_ht = Buf("ht")
                qrqt = sb(ps, "qrqt", [128, 8, T], BF16)
                B_qr, B_qt = Buf("qr"), Buf("qt")
                za = sb(ps, "za", [128, 4, T], BF16)
                zc = sb(ps, "zc", [128, 4, T], BF16)
                B_za, B_zc = Buf("za"), Buf("zc")
                ya = sb(ps, "ya", [128, 4, T], BF16)
                yb = sb(ps, "yb", [128, 4, T], BF16)
                yc = sb(ps, "yc", [128, 4, T], BF16)
                B_ya, B_yb, B_yc = Buf("ya"), Buf("yb"), Buf("yc")
                cbuf = sb(ps, "cbuf", [128, 2064], F32)
                cy = [cbuf[:, 0:512], cbuf[:, 1032:1544]]
                cu = [cbuf[:, 512:1028], cbuf[:, 1544:2060]]
                B_cu = [Buf(f"cu{i}") for i in range(2)]
                B_cy = [Buf(f"cy{i}") for i in range(2)]
                NPT = 6
                pt = [sb(ps, f"pt{i}", [128, T], BF16) for i in range(NPT)]
                B_pt = [Buf(f"pt{i}") for i in range(NPT)]
                sg = [sb(ps, f"sg{i}", [128, T], BF16) for i in range(2)]
                B_sg = [Buf(f"sg{i}") for i in range(2)]
                tmp = [sb(ps, f"tmp{i}", [128, T], F32) for i in range(2)]
                B_tmp = [Buf(f"tmp{i}") for i in range(2)]
                acc = [sb(ps, f"acc{i}", [128, T], F32) for i in range(2)]
                B_acc = [Buf(f"acc{i}") for i in range(2)]
                accv = [a_[:].bitcast(BF16) for a_ in acc]
                cnt = {"tmp": 0, "pt": 0, "sg": 0, "acc": 0, "sb": 0}
                if os.environ.get('KDBG'):
                    print('pass2 sbuf remaining', nc.sbuf_bytes_remaining)

                def nxt(name, arr, barr):
                    i = cnt[name]
                    cnt[name] = (i + 1) % len(arr)
                    return arr[i], barr[i]

                rbuf = [cbuf[:, 0:1024], cbuf[:, 1032:2056],
                        ya[:].rearrange("p a b -> p (a b)").bitcast(F32), yc[:].rearrange("p a b -> p (a b)").bitcast(F32)]
                B_r = [[B_cy[0], B_cu[0]], [B_cy[1], B_cu[1]], [B_ya], [B_yc]]
                st2 = [sb(ps, f"st2_{i}", [128, 16], F32) for i in range(4)]
                B_st2 = [Buf(f"st2_{i}") for i in range(4)]
                pq = qrqt
                mg = qrqt
                B_mgl = [B_qr, B_qt]

                nun = len(tiles) * NUNIT
                wst = {"issued": 0, "cur": 0}

                def issue_w(n):
                    while wst["issued"] < min(n, nun):
                        i = wst["issued"]
                        u = i % NUNIT
                        sl = i % NS_W
                        P.dma("sp", wsl[sl][:], s_wmain[l, u], [cbuf_of(l, "wmain", u)], [B_wsl[sl]], B_wsl[sl])
                        wst["issued"] += 1

                def take_unit(hold=0):
                    i = wst["cur"]
                    wst["cur"] += 1
                    issue_w(i + NS_W - hold)
                    sl = i % NS_W
                    return wsl[sl], B_wsl[sl]

                tunits = [(ti_, u_, tb_) for ti_ in range(NT) for tb_ in range(2) for rep_ in range(2) for u_ in range(NU2)] + [("p", 0, 0)]
                tst = {"issued": 0, "cur": 0}

                def issue_t(n):
                    while tst["issued"] < min(n, len(tunits)):
                        i = tst["issued"]
                        ti_, u_, tb_ = tunits[i]
                        sl = i % NS_T
                        if ti_ == "p":
                            P.dma("sp", tsl[sl][:].rearrange("p a b -> p (a b)")[:, 0:1024], dftp_d, [], [B_tsl[sl]], B_tsl[sl])
                        else:
                            P.dma("sp", tsl[sl][:].rearrange("p a b -> p (a b)"), dfts_d[ti_, u_, tb_], [], [B_tsl[sl]], B_tsl[sl])
                        tst["issued"] += 1

                def take_tunit():
                    i = tst["cur"]
                    tst["cur"] += 1
                    issue_t(i + NS_T)
                    sl = i % NS_T
                    return tsl[sl], B_tsl[sl]

                htp = ht[:, :, :].rearrange("p k (s t) -> p k s t", s=2)

                def load_tile_inputs(tix_):
                    kind_, ti_ = tiles[tix_]
                    t0_ = tok0(kind_, ti_)
                    if kind_ == "s":
                        lo = 1 if ti_ == 0 else 0
                        hi = 513 if ti_ == NT - 1 else 514
                        if ti_ == 0:
                            P.op("pool", lambda e: e.memset(ht[:, :, 0:1], 0.0), [], [B_ht])
                        if ti_ == NT - 1:
                            P.op("pool", lambda e: e.memset(ht[:, :, 513:514], 0.0), [], [B_ht])
                        P.dma("sp", ht[:, :, lo:hi], s_ht[l, :, :, t0_ - 1 + lo:t0_ - 1 + hi], [B_sht[l]], [B_ht], B_ht)
                        P.dma("sp", ropet[:].rearrange("p a b -> p (a b)"), rope_d[ti_], [], [B_rope], B_rope)
                    else:
                        P.op("pool", lambda e: e.memset(htp[:, :, :, 0:258:257], 0.0), [], [B_ht])
                        for sq_ in range(2):
                            P.dma("sp", htp[:, :, sq_, 1:257], s_ht[l, :, :, t0_ + sq_ * 256:t0_ + (sq_ + 1) * 256],
                                  [B_sht[l]], [B_ht], B_ht)

                deferred = []
                issue_w(NS_W - 1)
                issue_t(NS_T - 1)
                load_tile_inputs(0)
                pending_casts = cast_list(l + 1) if l + 1 < L else []
                gblk = 0
                for tix, (kind, ti) in enumerate(tiles):
                    cnd = 0 if kind == "s" else 1
                    t0 = tok0(kind, ti)
                    isS = kind == "s"

                    def v2(ap):
                        return ap if isS else ap.rearrange("p (s t) -> p s t", s=2)

                    if isS:
                        def hmain(kc):
                            return ht[:, kc, 1:513]

                        def hhalo(kc):
                            return ht[:, kc, 0:514:513]
                        nh = 2
                    else:
                        def hmain(kc):
                            return htp[:, kc, :, 1:257]

                        def hhalo(kc):
                            return htp[:, kc, :, 0:258:257]
                        nh = 4

                    def proj(wt, ch, B_w, into, B_into):
                        for kc in range(KC):
                            P.op("pe", lambda e, kc=kc: e.matmul(
                                into, lhsT=wt[:, ch * 1024 + kc * 128: ch * 1024 + (kc + 1) * 128], rhs=hmain(kc),
                                start=(kc == 0), stop=(kc == KC - 1)), [B_w, B_ht], [B_into])

                    wt, B_w = take_unit()
                    for c in range(4):
                        bk, bb = next_bank()
                        proj(wt, c, B_w, bk[:], bb)
                        if isS:
                            P.op("act", act_copy(qrqt[:, c, :], bk[:]), [bb], [B_qr])
                            rk, rb = next_bank()
                            P.op("pe", lambda e, rk=rk, c=c: e.matmul(rk[:], lhsT=rmat[:], rhs=qrqt[:, c, :], start=True, stop=True),
                                 [B_qr, B_const], [rb])
                            t1, B_t1 = nxt("tmp", tmp, B_tmp)
                            t2, B_t2 = nxt("tmp", tmp, B_tmp)
                            P.op("dve", lambda e, rk=rk, t1=t1: e.tensor_tensor(out=t1[:], in0=rk[:], in1=ropet[:, 1, :], op=ALU.mult),
                                 [rb, B_rope], [B_t1])
                            P.op("pool", lambda e, t2=t2, c=c: e.tensor_tensor(out=t2[:], in0=qrqt[:, c, :], in1=ropet[:, 0, :], op=ALU.mult),
                                 [B_qr, B_rope], [B_t2])
                            P.op("dve", lambda e, t1=t1, t2=t2, c=c: e.tensor_tensor(out=qrqt[:, 4 + c, :], in0=t1[:], in1=t2[:], op=ALU.add),
                                 [B_t1, B_t2], [B_qt])
                        else:
                            P.op("act", act_copy(qrqt[:, 4 + c, :], bk[:]), [bb], [B_qt])
                    wt, B_w = take_unit()
                    for c in range(4):
                        bk, bb = next_bank()
                        proj(wt, c, B_w, bk[:], bb)
                        P.op("act", lambda e, bk=bk, c=c: e.activation(out=za[:, c, :], in_=bk[:], func=AF.Silu), [bb], [B_za])
                    while deferred:
                        deferred.pop(0)()
                    for _ in range(4):
                        if pending_casts:
                            o_, i_, b_ = pending_casts.pop(0)
                            P.dma("pool", o_, i_, [], [b_], b_)
                    for c in range(4):
                        wt, B_w = take_unit()
                        bx, bbx = next_bank()
                        bc, bbc = next_bank()
                        proj(wt, 0, B_w, bx[:], bbx)
                        proj(wt, 1, B_w, bc[:], bbc)
                        hk, hb = next_bank()
                        for wi_ in range(2):
                            for kc in range(KC):
                                o_ = hk[:, wi_ * 8: wi_ * 8 + nh]
                                P.op("pe", lambda e, kc=kc, o_=o_, wi_=wi_, wt=wt: e.matmul(
                                    o_, lhsT=wt[:, wi_ * 1024 + kc * 128: wi_ * 1024 + (kc + 1) * 128], rhs=hhalo(kc),
                                    start=(kc == 0), stop=(kc == KC - 1)), [B_w, B_ht], [hb])
                        bcb, bbcb = next_bank()
                        bzb, bbzb = next_bank()
                        proj(wt, 2, B_w, bcb[:], bbcb)
                        proj(wt, 3, B_w, bzb[:], bbzb)
                        cs = c % 2
                        cuu, B_cuu = cu[cs], B_cu[cs]
                        cyy, B_cyy = cy[cs], B_cy[cs]
                        if isS:
                            um = cuu[:, 1:513]
                            uh = cuu[:, 0:514:513]
                            taps = [cuu[:, 0:512], cuu[:, 1:513], cuu[:, 2:514]]
                        else:
                            cup = cuu.rearrange("p (s t) -> p s t", s=2)
                            um = cup[:, :, 1:257]
                            uh = cup[:, :, 0:258:257]
                            taps = [cup[:, :, 0:256], cup[:, :, 1:257], cup[:, :, 2:258]]
                        cyv = v2(cyy)
                        bxm, bcm = v2(bx[:]), v2(bc[:])
                        hx, hc = v2(hk[:, 0:nh]), v2(hk[:, 8:8 + nh])
                        P.op("act", act_copy(um, bxm), [bbx], [B_cuu])
                        P.op("act", act_copy(uh, hx), [hb], [B_cuu])
                        P.op("dve", lambda e, um=um, bcm=bcm: e.tensor_tensor(out=um, in0=bcm, in1=um, op=ALU.mult), [bbc, B_cuu], [B_cuu])
                        P.op("dve", lambda e, uh=uh, hc=hc: e.tensor_tensor(out=uh, in0=hc, in1=uh, op=ALU.mult), [hb, B_cuu], [B_cuu])
                        P.op("act", lambda e, cyv=cyv, taps=taps, c=c: e.activation(
                            out=cyv, in_=taps[1], func=AF.Copy, scale=convw[:, 1, c:c + 1]), [B_cuu, B_lc], [B_cyy])
                        P.op("dve", lambda e, cyv=cyv, taps=taps, c=c: e.scalar_tensor_tensor(
                            out=cyv, in0=taps[0], scalar=convw[:, 0, c:c + 1], in1=cyv, op0=ALU.mult, op1=ALU.add), [B_cuu, B_lc, B_cyy], [B_cyy])
                        P.op("dve", lambda e, cyv=cyv, taps=taps, c=c: e.scalar_tensor_tensor(
                            out=cyv, in0=taps[2], scalar=convw[:, 2, c:c + 1], in1=cyv, op0=ALU.mult, op1=ALU.add), [B_cuu, B_lc, B_cyy], [B_cyy])
                        zs, B_zs = nxt("sg", sg, B_sg)
                        P.op("act", lambda e, zs=zs, bzb=bzb: e.activation(out=zs[:], in_=bzb[:], func=AF.Silu), [bbzb], [B_zs])
                        gt, B_gt = nxt("tmp", tmp, B_tmp)
                        P.op("dve", lambda e, gt=gt, bcb=bcb, zs=zs: e.tensor_tensor(out=gt[:], in0=bcb[:], in1=zs[:], op=ALU.mult),
                             [bbcb, B_zs], [B_gt])
                        P.op("dve", lambda e, gt=gt, cyy=cyy, c=c: e.tensor_tensor(out=yb[:, c, :], in0=cyy, in1=gt[:], op=ALU.mult),
                             [B_cyy, B_gt], [B_yb])
                    wt, B_w = take_unit()
                    for c in range(4):
                        bk, bb = next_bank()
                        proj(wt, c, B_w, bk[:], bb)
                        P.op("act", lambda e, bk=bk, c=c: e.activation(out=zc[:, c, :], in_=bk[:], func=AF.Silu), [bb], [B_zc])

                    pend_norm = []
                    for qb in range(4):
                        if isS:
                            i = ti * 4 + qb
                            chunks = []
                            if i > 0:
                                chunks.append((kt_s, B_kts, (i - 1) * 128, va_s, B_vas, i - 1, 0))
                            chunks.append((kt_s, B_kts, i * 128, va_s, B_vas, i, None))
                            if i < NBS - 1:
                                chunks.append((kt_s, B_kts, (i + 1) * 128, va_s, B_vas, i + 1, 1))
                            for b4 in range(4):
                                chunks.append((kt_c, B_ktc, b4 * 128, va_c, B_vac, b4, None))
                        else:
                            sq = qb // 2
                            chunks = [(kt_p, B_ktp, (2 * sq + j_) * 128, va_p, B_vap, 2 * sq + j_, None) for j_ in range(2)]
                        qs = slice(qb * 128, (qb + 1) * 128)
                        obs = [(banks[(qb % 2) * 2 + g], bbuf[(qb % 2) * 2 + g]) for g in range(2)]

                        def emit_s(g, n):
                            kt, B_kt, k0, va, B_va, vblk, mk = chunks[n]
                            gs = slice(g * 64, (g + 1) * 64)
                            si = 4 + cnt["sb"]
                            cnt["sb"] = (cnt["sb"] + 1) % (2 if isS else 4)
                            sk, sbb = banks[si], bbuf[si]
                            P.op("pe", lambda e, sk=sk, kt=kt, k0=k0, gs=gs: e.matmul(
                                sk[:], lhsT=kt[gs, k0:k0 + 128], rhs=qrqt[gs, 4:8, qs],
                                start=True, stop=True), [B_kt, B_qt], [sbb])
                            pp, B_pp_ = nxt("pt", pt, B_pt)
                            P.op("act", lambda e, pp=pp, sk=sk: e.activation(out=pp[:], in_=sk[:], func=AF.Exp, scale=0.125),
                                 [sbb], [B_pp_])
                            if mk is not None:
                                P.op("pool", lambda e, pp=pp, mk=mk: e.tensor_tensor(out=pp[:], in0=pp[:], in1=masks[:, mk, :], op=ALU.mult),
                                     [B_pp_, B_const], [B_pp_])
                            return pp, B_pp_

                        def emit_pv(g, n, pp, B_pp_):
                            kt, B_kt, k0, va, B_va, vblk, mk = chunks[n]
                            ob, obb = obs[g]
                            P.op("pe", lambda e, va=va, vblk=vblk, pp=pp, n=n, ob=ob, g=g: e.matmul(
                                ob[:], lhsT=va[:, vblk, g * 128:(g + 1) * 128], rhs=pp[:],
                                start=(n == 0), stop=(n == len(chunks) - 1)), [B_va, B_pp_], [obb])

                        tasks = [(g, n) for n in range(len(chunks)) for g in range(2)]
                        LA = 1 if isS else 3
                        fl = []
                        if isS:
                            ftb = qb // 2
                            fgp = [(qb % 2) * 2, (qb % 2) * 2 + 1]
                            fstate = {}
                            for u in range(NU2):
                                for gi, gch in enumerate(fgp):
                                    for sc_ in range(4):
                                        def ffn(u=u, gi=gi, gch=gch, sc_=sc_, ftb=ftb, fstate=fstate):
                                            if u not in fstate:
                                                fstate[u] = take_tunit()
                                            tt, B_tt = fstate[u]
                                            blk = u * 4 + sc_
                                            if ftb == 1:
                                                blk = NBS - 1 - blk
                                            last = (u == NU2 - 1 and sc_ == 3)
                                            P.op("pe", lambda e: e.matmul(
                                                banks[6 + gi][:], lhsT=fxr[:, blk, gch * 128:(gch + 1) * 128], rhs=tt[:, sc_, :],
                                                start=(u == 0 and sc_ == 0), stop=(last and ftb == 1)), [B_fxr, B_tt], [bbuf[6 + gi]])
                                        fl.append(ffn)
                            if ftb == 0:
                                for gi, gch in enumerate(fgp):
                                    def ffs(gi=gi, gch=gch):
                                        P.op("pe", lambda e: e.matmul(
                                            banks[6 + gi][:], lhsT=sprow[0:1, gch * 128:(gch + 1) * 128], rhs=altrow[0:1, :],
                                            start=False, stop=True), [B_sprow, B_const], [bbuf[6 + gi]])
                                    fl.append(ffs)
                        pend = []
                        if isS:
                            prev = None
                            for n in range(len(chunks)):
                                cur = [(g, n) + emit_s(g, n) for g in range(2)]
                                for _ in range(4):
                                    if fl:
                                        fl.pop(0)()
                                if prev is not None:
                                    for t_ in prev:
                                        emit_pv(*t_)
                                    for _ in range(2):
                                        if fl:
                                            fl.pop(0)()
                                prev = cur
                            for t_ in prev:
                                emit_pv(*t_)
                        else:
                            for (g, n) in tasks:
                                pend.append((g, n) + emit_s(g, n))
                                if len(pend) > LA:
                                    emit_pv(*pend.pop(0))
                            while pend:
                                emit_pv(*pend.pop(0))
                        while fl:
                            fl.pop(0)()
                        if isS:
                            for gi, gch in enumerate(fgp):
                                if ftb == 0:
                                    dst, B_dst = pq[:, gch, :], B_qr
                                else:
                                    dst, B_dst = accv[gch // 2][:, (gch % 2) * 512:(gch % 2 + 1) * 512], B_acc[gch // 2]
                                if gi == 0:
                                    P.op("act", act_copy(dst, banks[6 + gi][:]), [bbuf[6 + gi]], [B_dst])
                                else:
                                    P.op("dve", lambda e, dst=dst, gi=gi: e.tensor_copy(out=dst, in_=banks[6 + gi][:]), [bbuf[6 + gi]], [B_dst])
                        def emit_norm(obs=obs, qs=qs):
                            t1, B_t1 = nxt("tmp", tmp, B_tmp)
                            for g in range(2):
                                ob, obb = obs[g]
                                orow = slice(g * 64, (g + 1) * 64)
                                drow = slice(64, 128) if g == 0 else slice(0, 64)
                                P.op("dve", lambda e, t1=t1, ob=ob, orow=orow, drow=drow, g=g: e.tensor_tensor(
                                    out=t1[orow, :], in0=ob[drow, :], in1=esink[orow, g, :], op=ALU.add), [obb, B_lc], [B_t1])
                            P.op("dve", lambda e, t1=t1: e.reciprocal(out=t1[:], in_=t1[:]), [B_t1], [B_t1])
                            for g in range(2):
                                ob, obb = obs[g]
                                orow = slice(g * 64, (g + 1) * 64)
                                P.op("dve", lambda e, t1=t1, ob=ob, orow=orow: e.tensor_tensor(
                                    out=t1[orow, :], in0=ob[orow, :], in1=t1[orow, :], op=ALU.mult), [obb, B_t1], [B_t1])
                            P.op("dve", lambda e, t1=t1, qs=qs: e.tensor_tensor(
                                out=ya[:, :, qs], in0=t1[:].rearrange("p (c q) -> p c q", c=4), in1=za[:, :, qs], op=ALU.mult),
                                [B_t1, B_za], [B_ya])

                        emit_norm()
                    while pend_norm:
                        pend_norm.pop(0)()

                    if not isS:
                        pbs = [next_bank() for _ in range(8)]
                        tt, B_tt = take_tunit()
                        dftp = tt[:].rearrange("p a b -> p (a b)")[:, 0:1024].rearrange("p (a b c) -> p a b c", a=2, b=2)
                        for tb in range(2):
                            for gch in range(4):
                                bk, bb = pbs[tb * 4 + gch]
                                for sq in range(2):
                                    for sc_ in range(2):
                                        blk = NBS + 2 * sq + sc_
                                        P.op("pe", lambda e, bk=bk, blk=blk, gch=gch, tb=tb, sc_=sc_, sq=sq: e.matmul(
                                            bk[:, sq * 256:(sq + 1) * 256], lhsT=fxr[:, blk, gch * 128:(gch + 1) * 128],
                                            rhs=dftp[:, tb, sc_, :], start=(sc_ == 0), stop=(sc_ == 1)), [B_fxr, B_tt], [bb])
                        for gch in range(4):
                            bkp, bbp = pbs[gch]
                            bkq, bbq = pbs[4 + gch]
                            P.op("act", act_copy(pq[:, gch, :], bkp[:]), [bbp], [B_qr])
                            P.op("dve", lambda e, gch=gch, bkq=bkq: e.tensor_copy(out=pq[:, 4 + gch, :], in_=bkq[:]), [bbq], [B_qt])
                    def emit_cdft():
                      for gch in range(4):
                        bk, bb = next_bank()
                        if isS:
                            qsrc, B_qsrc = accv[gch // 2][:, (gch % 2) * 512:(gch % 2 + 1) * 512], B_acc[gch // 2]
                        else:
                            qsrc, B_qsrc = pq[:, 4 + gch, :], B_qt
                        P.op("pe", lambda e, bk=bk, gch=gch: e.matmul(bk[:], lhsT=cdft[:, 0, :], rhs=pq[:, gch, :], start=True, stop=False),
                             [B_const, B_qr], [bb])
                        P.op("pe", lambda e, bk=bk, qsrc=qsrc: e.matmul(bk[:], lhsT=cdft[:, 1, :], rhs=qsrc, start=False, stop=True),
                             [B_const, B_qsrc], [bb])
                        P.op("dve", lambda e, bk=bk, gch=gch: e.tensor_tensor(out=yc[:, gch, :], in0=bk[:], in1=zc[:, gch, :], op=ALU.mult),
                             [bb, B_zc], [B_yc])

                    if l == 0 and tix == 0:
                        dump("ya", ya[:], [128, 4, T], BF16, B_ya)
                        dump("yb", yb[:], [128, 4, T], BF16, B_yb)
                        dump("yc", yc[:], [128, 4, T], BF16, B_yc)

                    Ys = ((ya, B_ya), (yb, B_yb), (yc, B_yc))
                    for j in range(8):
                        wt, B_w = take_unit()
                        ac, B_ac = nxt("acc", acc, B_acc)
                        for bi, b in enumerate((1, 2, 0)):
                            gk, gbb = next_bank()
                            for kc in range(KC):
                                P.op("pe", lambda e, gk=gk, kc=kc, b=b, wt=wt: e.matmul(
                                    gk[:], lhsT=wt[:, b * 1024 + kc * 128: b * 1024 + (kc + 1) * 128], rhs=hmain(kc),
                                    start=(kc == 0), stop=(kc == KC - 1)), [B_w, B_ht], [gbb])
                            s_, B_s = nxt("sg", sg, B_sg)
                            P.op("act", lambda e, s_=s_, gk=gk: e.activation(out=s_[:], in_=gk[:], func=AF.Sigmoid), [gbb], [B_s])
                            bk, bb = next_bank()
                            Y, B_Y = Ys[b]
                            for kc in range(4):
                                P.op("pe", lambda e, bk=bk, kc=kc, b=b, wt=wt, Y=Y: e.matmul(
                                    bk[:], lhsT=wt[:, 3072 + b * 512 + kc * 128: 3072 + b * 512 + (kc + 1) * 128], rhs=Y[:, kc, :],
                                    start=(kc == 0), stop=(kc == 3)), [B_w, B_Y], [bb])
                            if j == 0 and bi == 0:
                                emit_cdft()
                            if bi == 0:
                                P.op("dve", lambda e, ac=ac, bk=bk, s_=s_: e.tensor_tensor(out=ac[:], in0=bk[:], in1=s_[:], op=ALU.mult),
                                     [bb, B_s], [B_ac])
                            else:
                                tq, B_tq = nxt("tmp", tmp, B_tmp)
                                P.op("dve", lambda e, tq=tq, bk=bk, s_=s_: e.tensor_tensor(out=tq[:], in0=bk[:], in1=s_[:], op=ALU.mult),
                                     [bb, B_s], [B_tq])
                                if bi == 1:
                                    P.op("pool", lambda e, ac=ac, tq=tq: e.tensor_tensor(out=ac[:], in0=ac[:], in1=tq[:], op=ALU.add),
                                         [B_ac, B_tq], [B_ac])
                                else:
                                    P.op("pool", lambda e, ac=ac, tq=tq, j=j: e.tensor_tensor(out=mg[:, j, :], in0=ac[:], in1=tq[:], op=ALU.add),
                                         [B_ac, B_tq], [B_mgl[j // 4]])
                    if l == 0 and tix == 0:
                        dump("mg", mg[:], [128, 8, T], BF16, [B_qt, B_qr])

                    if tix + 1 < len(tiles):
                        load_tile_inputs(tix + 1)
                    wo = [take_unit(), take_unit(hold=1)]

                    def load_x(tb_):
                        xsl_ = tb_ % 2
                        src, srcb = x_src(l, kind, ti, tb_)
                        P.dma("sp", xb[xsl_][:], src, srcb, [B_xb[xsl_]], B_xb[xsl_])

                    load_x(0)
                    load_x(1)

                    def fin1(tb):
                        xsl = tb % 2
                        r, B_rr = rbuf[tb], B_r[tb]
                        st, B_st = st2[tb], B_st2[tb]
                        for hf in range(2):
                            bk, bb = next_bank()
                            wot, B_wo = wo[hf]
                            for kc in range(KC):
                                P.op("pe", lambda e, bk=bk, kc=kc, tb=tb, wot=wot: e.matmul(
                                    bk[:], lhsT=mg[:, kc, tb * 128:(tb + 1) * 128], rhs=wot[:, kc * 512:(kc + 1) * 512],
                                    start=(kc == 0), stop=(kc == KC - 1)), [B_qr, B_qt, B_wo], [bb])
                            P.op("dve", lambda e, r=r, bk=bk, hf=hf: e.tensor_tensor(
                                out=r[:, hf * 512:(hf + 1) * 512], in0=bk[:], in1=gate[:, cnd, hf * 512:(hf + 1) * 512], op=ALU.mult),
                                [bb, B_lc], B_rr)
                        P.op("dve", lambda e, r=r, xsl=xsl: e.scalar_tensor_tensor(
                            out=r, in0=xb[xsl][:], scalar=ALPHA, in1=r, op0=ALU.mult, op1=ALU.add), [B_xb[xsl]] + B_rr, B_rr)
                        if tb + 2 < 4:
                            load_x(tb + 2)
                        for h2 in range(2):
                            P.op("dve", lambda e, st=st, r=r, h2=h2: e.bn_stats(st[:, h2 * 6:h2 * 6 + 6], r[:, h2 * 512:(h2 + 1) * 512]),
                                 B_rr, [B_st])
                        P.op("dve", lambda e, st=st: e.bn_aggr(st[:, 12:14], st[:, 0:12]), [B_st], [B_st])

                    def fin2(tb, t0=t0, isS=isS, tix=tix):
                        r, B_rr = rbuf[tb], B_r[tb]
                        st, B_st = st2[tb], B_st2[tb]
                        P.op("act", lambda e, st=st: e.activation(out=st[:, 14:15], in_=st[:, 13:14], func=AF.Sqrt, bias=EPS, scale=1.0),
                             [B_st], [B_st])
                        P.op("dve", lambda e, st=st: e.reciprocal(out=st[:, 14:15], in_=st[:, 14:15]), [B_st], [B_st])
                        P.op("dve", lambda e, st=st: e.scalar_tensor_tensor(
                            out=st[:, 15:16], in0=st[:, 12:13], scalar=-1.0, in1=st[:, 14:15], op0=ALU.mult, op1=ALU.mult), [B_st], [B_st])
                        P.op("act", lambda e, st=st, r=r: e.activation(out=r, in_=r, func=AF.Identity, bias=st[:, 15:16], scale=st[:, 14:15]),
                             B_rr + [B_st], B_rr)
                        P.op("pool", lambda e, r=r: e.tensor_tensor(out=r, in0=r, in1=lngb[:, 0, :], op=ALU.mult), B_rr + [B_lc], B_rr)
                        P.op("pool", lambda e, r=r: e.tensor_tensor(out=r, in0=r, in1=lngb[:, 1, :], op=ALU.add), B_rr + [B_lc], B_rr)
                        r0 = t0 + tb * 128
                        if l == L - 1:
                            dst = ys[r0:r0 + 128, :] if isS else yp[tb * 128:(tb + 1) * 128, :]
                            P.dma("pool", dst, r, B_rr, [B_out], B_rr[0])
                        else:
                            P.dma("pool", s_x1[r0:r0 + 128, :], r, B_rr, [B_sx1], B_rr[0])
                            if debug and tix == 0:
                                if tb == 0:
                                    dbg["x1"] = nc.dram_tensor("dbg_x1", [512, 1024], F32, kind="ExternalOutput").ap()
                                P.dma("sp", dbg["x1"][tb * 128:(tb + 1) * 128, :], r, B_rr, [B_out], B_rr[0])

                    fin1(0)
                    fin1(1)
                    fin2(0)
                    fin1(2)
                    fin2(1)
                    if tix == len(tiles) - 1:
                        while pending_casts:
                            o_, i_, b_ = pending_casts.pop(0)
                            P.dma("pool", o_, i_, [], [b_], b_)
                    fin1(3)
                    issue_w(wst["cur"] + NS_W)
                    if tix + 1 < len(tiles):
                        deferred.extend([lambda f=fin2: f(2), lambda f=fin2: f(3)])
                    else:
                        fin2(2)
                        fin2(3)
                P.barrier()

        P.finish()
    return nc, dbg


_CACHE = {}


def _get_program(S):
    if S not in _CACHE:
        _CACHE[S] = build_program(S)[0]
    return _CACHE[S]


def make_in_maps(x_prompt, x_sample, cache_k, cache_v, c, c_ctx, w_mod, b_mod, w_in, sink, conv_w,
                 w_branch, w_o, ln_g, ln_b, n_cores):
    f = lambda a: np.ascontiguousarray(np.asarray(a, dtype=np.float32))
    x_prompt, x_sample, cache_k, cache_v, c, c_ctx = map(f, (x_prompt, x_sample, cache_k, cache_v, c, c_ctx))
    S = x_sample.shape[1]
    shared = _prep_weights(*map(f, (w_mod, b_mod, w_in, sink, conv_w, w_branch, w_o, ln_g, ln_b)))
    shared.update(_consts(S))
    in_maps = []
    for i in range(n_cores):
        cf = np.zeros((128, 8, 2), np.float32)
        cf[:, :, 0] = c[i].reshape(8, 128).T
        cf[:, :, 1] = c_ctx.reshape(8, 128).T
        m = dict(shared)
        m["xs"] = x_sample[i]
        m["xp"] = np.ascontiguousarray(x_prompt[NPS * i:NPS * (i + 1)].reshape(NPS * SP_LEN, D))
        m["ck"] = np.ascontiguousarray(cache_k[i].reshape(L, PAST, 128))
        m["cv"] = np.ascontiguousarray(cache_v[i].reshape(L, PAST, 128))
        m["cfm"] = cf.reshape(128, 16)
        in_maps.append(m)
    return in_maps, S


def kernel(x_prompt, x_sample, cache_k, cache_v, c, c_ctx, w_mod, b_mod, w_in, sink, conv_w,
           w_branch, w_o, ln_g, ln_b):
    n = 8
    in_maps, S = make_in_maps(x_prompt, x_sample, cache_k, cache_v, c, c_ctx, w_mod, b_mod, w_in, sink, conv_w,
                              w_branch, w_o, ln_g, ln_b, n)
    nc = _get_program(S)
    res = run_bass_kernel_spmd(nc, in_maps, core_ids=list(range(n)))
    rs = res.results
    y_sample = np.stack([np.asarray(r["ys"], dtype=np.float32) for r in rs], axis=0)
    y_prompt = np.concatenate([np.asarray(r["yp"], dtype=np.float32).reshape(NPS, SP_LEN, D) for r in rs], axis=0)
    new_k = np.concatenate([np.asarray(r["nk"], dtype=np.float32).reshape(NPS, L, SP_LEN, 2, 64) for r in rs], axis=0)
    new_v = np.concatenate([np.asarray(r["nv"], dtype=np.float32).reshape(NPS, L, SP_LEN, 2, 64) for r in rs], axis=0)
    return (y_prompt, y_sample, new_k, new_v)
```
